# Optimizing a Trainium2 kernel written in Bass

```python
import math
import jax, jax.numpy as jnp
from jax import lax
import numpy as np

D_MODEL = 2048
BATCH = 2
SEQ = 4096
DEPTH = 2
DEC_BATCH = 128
DEC_SEQ = 4
PAST_LEN = 2048
PAGE_SIZE = 128

N_Q_HEADS = 8
N_KV_HEADS = 4
HEAD_DIM = 128
Q_PER_KV = N_Q_HEADS // N_KV_HEADS
D_ATT = N_Q_HEADS * HEAD_DIM
D_KV = N_KV_HEADS * HEAD_DIM
DILATED_BRANCHES = ((128, 1), (512, 4), (2048, 16))
ATT_WINDOW = 2048
ROPE_THETA = 10000.0
ATT_SCALE = HEAD_DIM ** -0.5
N_SSM_HEADS = 16
SSM_HEAD_DIM = 64
D_SSM = N_SSM_HEADS * SSM_HEAD_DIM
N_SSM_GROUPS = 4
D_STATE = 128
D_BC = N_SSM_GROUPS * D_STATE
CONV_WIDTH = 4
CONV_DIM = D_SSM + 2 * D_BC
SSD_CHUNK = 128
D_MIX = D_ATT + D_SSM
IN_PROJ = D_ATT + 2 * D_KV + D_SSM + CONV_DIM + N_SSM_HEADS
IN_SPLITS = [D_ATT, D_ATT + D_KV, D_ATT + 2 * D_KV, D_ATT + 2 * D_KV + D_SSM,
             D_ATT + 2 * D_KV + D_SSM + CONV_DIM]
D_FF = -(-8 * D_MODEL // (3 * 256)) * 256
PLE_DIM = 256
EPS = 1e-6
NEG_INF = -1e30

kernel_name = 'hymba_dilated_ssd_decode_step'


def rmsnorm(x, g):
    xf = x.astype(jnp.float32)
    y = xf * lax.rsqrt(jnp.mean(xf * xf, axis=-1, keepdims=True) + EPS)
    return (y * g.astype(jnp.float32)).astype(x.dtype)


def rotary(x, pos):
    half = HEAD_DIM // 2
    inv_freq = ROPE_THETA ** (-jnp.arange(half, dtype=jnp.float32) / half)
    ang = pos.astype(jnp.float32)[:, None] * inv_freq[None, :]
    cos = jnp.cos(ang)[None, :, None, :]
    sin = jnp.sin(ang)[None, :, None, :]
    xf = x.astype(jnp.float32)
    x1, x2 = xf[..., :half], xf[..., half:]
    return jnp.concatenate([x1 * cos - x2 * sin, x2 * cos + x1 * sin], axis=-1).astype(x.dtype)


def masked_softmax_parts(scores, mask):
    scores = jnp.where(mask, scores, NEG_INF)
    m = jnp.max(scores, axis=-1, keepdims=True)
    e = jnp.exp(scores - m)
    s = jnp.sum(e, axis=-1, keepdims=True)
    return e / s, (m + jnp.log(s))[..., 0]


def to_residues(t, dil):
    b, s = t.shape[:2]
    t = t.reshape((b, s // dil, dil) + t.shape[2:])
    t = jnp.moveaxis(t, 2, 1)
    return t.reshape((b * dil, s // dil) + t.shape[3:])


def from_residues(t, b, dil):
    n = t.shape[1]
    t = t.reshape((b, dil, n) + t.shape[2:])
    t = jnp.moveaxis(t, 1, 2)
    return t.reshape((b, n * dil) + t.shape[3:])


def pad_seq(t, front, back):
    cfg = [(0, 0)] * t.ndim
    cfg[1] = (front, back)
    return jnp.pad(t, cfg)


def dilated_branch_prompt(q, k, v, dil, steps):
    b = q.shape[0]
    qr, kr, vr = to_residues(q, dil), to_residues(k, dil), to_residues(v, dil)
    bd, n = qr.shape[:2]
    blk = steps
    nb = -(-n // blk)
    tail = nb * blk - n
    qb = pad_seq(qr, 0, tail).reshape((bd, nb, blk) + qr.shape[2:])
    kb = pad_seq(kr, blk, tail).reshape((bd, nb + 1, blk) + kr.shape[2:])
    vb = pad_seq(vr, blk, tail).reshape((bd, nb + 1, blk) + vr.shape[2:])
    k_band = jnp.concatenate([kb[:, :-1], kb[:, 1:]], axis=2)
    v_band = jnp.concatenate([vb[:, :-1], vb[:, 1:]], axis=2)
    qi = jnp.arange(blk)[:, None]
    ki = jnp.arange(2 * blk)[None, :] - blk
    dist = qi - ki
    in_window = (dist >= 0) & (dist <= steps)
    in_seq = (jnp.arange(nb)[:, None, None] * blk + ki[None]) >= 0
    mask = in_window[None] & in_seq
    scores = jnp.einsum('znqgrd,znkgd->zngrqk', qb, k_band,
                        preferred_element_type=jnp.float32) * ATT_SCALE
    probs, lse = masked_softmax_parts(scores, mask[None, :, None, None])
    o = jnp.einsum('zngrqk,znkgd->znqgrd', probs.astype(v.dtype), v_band,
                   preferred_element_type=jnp.float32)
    o = o.reshape((bd, nb * blk) + o.shape[3:])[:, :n]
    lse = jnp.moveaxis(lse, -1, 2).reshape((bd, nb * blk) + lse.shape[2:4])[:, :n]
    return from_residues(o, b, dil), from_residues(lse, b, dil)


def dilated_branch_sample(q, k_all, v_all, n_past, dil, steps):
    t_new = q.shape[1]
    idx = n_past + jnp.arange(t_new)[:, None] - dil * jnp.arange(steps + 1)[None, :]
    valid = idx >= 0
    idx = jnp.maximum(idx, 0)
    kg = k_all[:, idx]
    vg = v_all[:, idx]
    scores = jnp.einsum('btgrd,btkgd->btgrk', q, kg,
                        preferred_element_type=jnp.float32) * ATT_SCALE
    probs, lse = masked_softmax_parts(scores, valid[None, :, None, None, :])
    o = jnp.einsum('btgrk,btkgd->btgrd', probs.astype(v_all.dtype), vg,
                   preferred_element_type=jnp.float32)
    return o, lse


def merge_branches(outs, lses):
    w = jax.nn.softmax(jnp.stack(lses, axis=0), axis=0)
    return jnp.einsum('nbtgr,nbtgrd->btgrd', w, jnp.stack(outs, axis=0))


def dilated_attention(q, k, v, kv_past):
    b, l = q.shape[:2]
    qg = q.reshape(b, l, N_KV_HEADS, Q_PER_KV, HEAD_DIM)
    if kv_past is None:
        parts = [dilated_branch_prompt(qg, k, v, d, w // d) for w, d in DILATED_BRANCHES]
        keep = min(ATT_WINDOW, l)
        new_k, new_v = k[:, l - keep:], v[:, l - keep:]
    else:
        k_past, v_past = kv_past
        n_past = k_past.shape[1]
        k_all = jnp.concatenate([k_past.astype(k.dtype), k], axis=1)
        v_all = jnp.concatenate([v_past.astype(v.dtype), v], axis=1)
        parts = [dilated_branch_sample(qg, k_all, v_all, n_past, d, w // d) for w, d in DILATED_BRANCHES]
        keep = min(ATT_WINDOW, n_past + l)
        new_k, new_v = k_all[:, n_past + l - keep:], v_all[:, n_past + l - keep:]
    o = merge_branches([pr[0] for pr in parts], [pr[1] for pr in parts])
    return o.reshape(b, l, D_ATT).astype(q.dtype), new_k, new_v


def ssd_scan(x, dt, a, b_mat, c_mat, h0):
    bsz, l = x.shape[:2]
    cs = SSD_CHUNK if l % SSD_CHUNK == 0 else l
    nc = l // cs
    rep = N_SSM_HEADS // N_SSM_GROUPS

    def chunked(t):
        return t.reshape((bsz, nc, cs) + t.shape[2:])

    xc = chunked(x.astype(jnp.float32) * dt[..., None])
    bc = chunked(jnp.repeat(b_mat.astype(jnp.float32), rep, axis=2))
    cc = chunked(jnp.repeat(c_mat.astype(jnp.float32), rep, axis=2))
    acum = jnp.cumsum(chunked(dt * a), axis=2)
    causal = jnp.tril(jnp.ones((cs, cs), dtype=bool))[None, None, :, :, None]
    seg = acum[:, :, :, None, :] - acum[:, :, None, :, :]
    decay = jnp.exp(jnp.where(causal, seg, NEG_INF))
    cb = jnp.einsum('bcthn,bcshn->bctsh', cc, bc) * decay
    y_diag = jnp.einsum('bctsh,bcshp->bcthp', cb, xc)
    to_end = jnp.exp(acum[:, :, -1:, :] - acum)
    chunk_states = jnp.einsum('bcshn,bcsh,bcshp->bchpn', bc, to_end, xc)
    chunk_decay = jnp.exp(acum[:, :, -1, :])

    def carry_step(h, inp):
        st, dec = inp
        return h * dec[:, :, None, None] + st, h

    h_last, h_in = lax.scan(carry_step, h0,
                            (jnp.moveaxis(chunk_states, 1, 0), jnp.moveaxis(chunk_decay, 1, 0)))
    h_in = jnp.moveaxis(h_in, 0, 1)
    y_off = jnp.einsum('bcthn,bchpn,bcth->bcthp', cc, h_in, jnp.exp(acum))
    return (y_diag + y_off).reshape(x.shape), h_last


def mixer_block(hn, pos, kv_past, conv_buf, ssm_h0, w_in, conv_w, conv_b, dt_bias, a_log,
                d_skip, ssm_norm_g, w_out):
    b, l, _ = hn.shape
    q, k, v, z, xbc, dt_raw = jnp.split(hn @ w_in, IN_SPLITS, axis=-1)
    q = rotary(q.reshape(b, l, N_Q_HEADS, HEAD_DIM), pos)
    k = rotary(k.reshape(b, l, N_KV_HEADS, HEAD_DIM), pos)
    v = v.reshape(b, l, N_KV_HEADS, HEAD_DIM)
    att, new_k, new_v = dilated_attention(q, k, v, kv_past)
    xbc_pad = jnp.concatenate([conv_buf.astype(xbc.dtype), xbc], axis=1)
    xbc = jax.nn.silu(lax.conv_general_dilated(
        xbc_pad, conv_w[:, None, :].astype(xbc.dtype), window_strides=(1,), padding='VALID',
        dimension_numbers=('NWC', 'WIO', 'NWC'), feature_group_count=CONV_DIM) + conv_b)
    new_conv = xbc_pad[:, -(CONV_WIDTH - 1):]
    xs = xbc[..., :D_SSM].reshape(b, l, N_SSM_HEADS, SSM_HEAD_DIM)
    b_mat = xbc[..., D_SSM:D_SSM + D_BC].reshape(b, l, N_SSM_GROUPS, D_STATE)
    c_mat = xbc[..., D_SSM + D_BC:].reshape(b, l, N_SSM_GROUPS, D_STATE)
    dt = jax.nn.softplus(dt_raw.astype(jnp.float32) + dt_bias.astype(jnp.float32))
    a = -jnp.exp(a_log.astype(jnp.float32))
    y, h_last = ssd_scan(xs, dt, a, b_mat, c_mat, ssm_h0.astype(jnp.float32))
    y = y + d_skip.astype(jnp.float32)[:, None] * xs.astype(jnp.float32)
    y = y.reshape(b, l, D_SSM) * jax.nn.silu(z.astype(jnp.float32))
    ssm = rmsnorm(y, ssm_norm_g).astype(hn.dtype)
    out = jnp.concatenate([att, ssm], axis=-1) @ w_out
    return out, new_k, new_v, h_last.astype(ssm_h0.dtype), new_conv


def trunk(x, p, pos, cache_k, cache_v, state_ssm, state_conv, params):
    (norm_mix_g, w_in, conv_w, conv_b, dt_bias, a_log, d_skip, ssm_norm_g, w_out,
     norm_ffn_g, w_ffn_gate, w_ffn_up, w_ffn_down, norm_ple_g, w_ple_gate, w_ple_proj,
     final_norm_g) = params
    h = x
    ks, vs, ss, cs = [], [], [], []
    for i in range(DEPTH):
        kv_past = None if cache_k is None else (cache_k[i], cache_v[i])
        mix, k_i, v_i, s_i, c_i = mixer_block(
            rmsnorm(h, norm_mix_g[i]), pos, kv_past, state_conv[i], state_ssm[i], w_in[i],
            conv_w[i], conv_b[i], dt_bias[i], a_log[i], d_skip[i], ssm_norm_g[i], w_out[i])
        h = h + mix
        hf = rmsnorm(h, norm_ffn_g[i])
        h = h + (jax.nn.silu(hf @ w_ffn_gate[i]) * (hf @ w_ffn_up[i])) @ w_ffn_down[i]
        gate = jax.nn.sigmoid(rmsnorm(h, norm_ple_g[i]) @ w_ple_gate[i])
        h = h + gate * (p[i] @ w_ple_proj[i])
        ks.append(k_i)
        vs.append(v_i)
        ss.append(s_i)
        cs.append(c_i)
    return rmsnorm(h, final_norm_g), jnp.stack(ks), jnp.stack(vs), jnp.stack(ss), jnp.stack(cs)


def _nrm(k, shape, scale):
    return jax.random.normal(k, shape, jnp.float32) * scale


def setup_inputs(seed: int = 0) -> dict:
    key = jax.random.key(seed)
    ks = jax.random.split(key, 25)
    l_win = min(ATT_WINDOW, PAST_LEN)
    dt0 = jnp.exp(jax.random.uniform(ks[12], (DEPTH, N_SSM_HEADS), jnp.float32,
                                     minval=math.log(1e-3), maxval=math.log(0.1)))
    return {
        'x_prompt': _nrm(ks[0], (BATCH, SEQ, D_MODEL), 1.0),
        'x_sample': _nrm(ks[1], (DEC_BATCH, DEC_SEQ, D_MODEL), 1.0),
        'cache_k': _nrm(ks[2], (DEPTH, DEC_BATCH, l_win, N_KV_HEADS, HEAD_DIM), 1.0),
        'cache_v': _nrm(ks[3], (DEPTH, DEC_BATCH, l_win, N_KV_HEADS, HEAD_DIM), 1.0),
        'state_ssm': _nrm(ks[4], (DEPTH, DEC_BATCH, N_SSM_HEADS, SSM_HEAD_DIM, D_STATE), 0.1),
        'state_conv': _nrm(ks[5], (DEPTH, DEC_BATCH, CONV_WIDTH - 1, CONV_DIM), 1.0),
        'p_prompt': _nrm(ks[6], (DEPTH, BATCH, SEQ, PLE_DIM), 1.0),
        'p_sample': _nrm(ks[7], (DEPTH, DEC_BATCH, DEC_SEQ, PLE_DIM), 1.0),
        'norm_mix_g': 1.0 + _nrm(ks[8], (DEPTH, D_MODEL), 0.01),
        'w_in': _nrm(ks[9], (DEPTH, D_MODEL, IN_PROJ), D_MODEL ** -0.5),
        'conv_w': _nrm(ks[10], (DEPTH, CONV_WIDTH, CONV_DIM), CONV_WIDTH ** -0.5),
        'conv_b': _nrm(ks[11], (DEPTH, CONV_DIM), 0.01),
        'dt_bias': dt0 + jnp.log(-jnp.expm1(-dt0)),
        'a_log': jnp.log(jax.random.uniform(ks[13], (DEPTH, N_SSM_HEADS), jnp.float32,
                                            minval=1.0, maxval=16.0)),
        'd_skip': 1.0 + _nrm(ks[14], (DEPTH, N_SSM_HEADS), 0.01),
        'ssm_norm_g': 1.0 + _nrm(ks[15], (DEPTH, D_SSM), 0.01),
        'w_out': _nrm(ks[16], (DEPTH, D_MIX, D_MODEL), D_MIX ** -0.5),
        'norm_ffn_g': 1.0 + _nrm(ks[17], (DEPTH, D_MODEL), 0.01),
        'w_ffn_gate': _nrm(ks[18], (DEPTH, D_MODEL, D_FF), D_MODEL ** -0.5),
        'w_ffn_up': _nrm(ks[19], (DEPTH, D_MODEL, D_FF), D_MODEL ** -0.5),
        'w_ffn_down': _nrm(ks[20], (DEPTH, D_FF, D_MODEL), D_FF ** -0.5),
        'norm_ple_g': 1.0 + _nrm(ks[21], (DEPTH, D_MODEL), 0.01),
        'w_ple_gate': _nrm(ks[22], (DEPTH, D_MODEL, D_MODEL), D_MODEL ** -0.5),
        'w_ple_proj': _nrm(ks[23], (DEPTH, PLE_DIM, D_MODEL), PLE_DIM ** -0.5),
        'final_norm_g': 1.0 + _nrm(ks[24], (D_MODEL,), 0.01),
    }


def reference(x_prompt, x_sample, cache_k, cache_v, state_ssm, state_conv, p_prompt, p_sample,
              norm_mix_g, w_in, conv_w, conv_b, dt_bias, a_log, d_skip, ssm_norm_g, w_out,
              norm_ffn_g, w_ffn_gate, w_ffn_up, w_ffn_down, norm_ple_g, w_ple_gate, w_ple_proj,
              final_norm_g):
    params = (norm_mix_g, w_in, conv_w, conv_b, dt_bias, a_log, d_skip, ssm_norm_g, w_out,
              norm_ffn_g, w_ffn_gate, w_ffn_up, w_ffn_down, norm_ple_g, w_ple_gate, w_ple_proj,
              final_norm_g)
    b, s = x_prompt.shape[:2]
    zero_ssm = jnp.zeros((DEPTH, b, N_SSM_HEADS, SSM_HEAD_DIM, D_STATE), state_ssm.dtype)
    zero_conv = jnp.zeros((DEPTH, b, CONV_WIDTH - 1, CONV_DIM), x_prompt.dtype)
    pos_prompt = jnp.arange(s, dtype=jnp.int32)
    pos_sample = PAST_LEN + jnp.arange(x_sample.shape[1], dtype=jnp.int32)
    y_prompt, new_k_prompt, new_v_prompt, new_ssm_prompt, new_conv_prompt = trunk(
        x_prompt, p_prompt, pos_prompt, None, None, zero_ssm, zero_conv, params)
    y_sample, new_k_sample, new_v_sample, new_ssm_sample, new_conv_sample = trunk(
        x_sample, p_sample, pos_sample, cache_k, cache_v, state_ssm, state_conv, params)
    return (y_prompt, y_sample, new_k_prompt, new_v_prompt, new_ssm_prompt, new_conv_prompt,
            new_k_sample, new_v_sample, new_ssm_sample, new_conv_sample)
```

```python
import math
from contextlib import ExitStack

import numpy as np
import concourse.bass as bass
import concourse.mybir as mybir
from concourse.bass_utils import run_bass_kernel_spmd

F32 = mybir.dt.float32
BF16 = mybir.dt.bfloat16
I32 = mybir.dt.int32
AF = mybir.ActivationFunctionType
ALU = mybir.AluOpType
AX = mybir.AxisListType

NCORES = 8
D = 2048
SEQ = 4096
NSB = 16
TS = 64
T = SEQ + TS
NBLK = 33
DEPTH = 2
L = 2048
DFF = 5632
PLE = 256
INP = 5136
EPS = 1e-6
SCALE = 128 ** -0.5
NEG = -30000.0


def nr_of(b):
    return 128 if b < 32 else 64


class Buf:
    __slots__ = ("w", "r", "name", "dram", "dg")

    def __init__(self, name="", dram=False):
        self.w = None
        self.r = {}
        self.name = name
        self.dram = dram
        self.dg = None


class DG:
    def __init__(self, sem, bulk=False):
        self.sem = sem
        self.cnt = 0
        self.bulk = bulk


class Op:
    __slots__ = ("eng", "fn", "deps", "dg", "has_dep", "sig", "epoch")


class Sched:
    ENG = ("pe", "act", "dve", "pool", "sp")

    def __init__(self, nc, es):
        self.nc = nc
        self.es = es
        self.ops = {e: [] for e in self.ENG}
        self.epoch = 0
        self.dgs = []
        self.nsem = 0
        self.dgpool = []
        self.pool_idx = 0

    def phase_reset(self):
        self.pool_idx = 0

    def buf_dg(self, buf):
        if buf.dg is None:
            if self.pool_idx >= len(self.dgpool):
                self.dgpool.append(self.dgroup("p%d" % len(self.dgpool)))
            buf.dg = self.dgpool[self.pool_idx]
            self.pool_idx += 1
        return buf.dg

    def newsem(self, name):
        self.nsem += 1
        return self.es.enter_context(self.nc.semaphore(name))

    def dgroup(self, name, bulk=False):
        g = DG(self.newsem("dg_" + name), bulk)
        self.dgs.append(g)
        return g

    def add(self, eng, fn, reads=(), writes=(), dg=None):
        op = Op()
        op.eng = eng
        op.fn = fn
        op.dg = dg
        op.has_dep = False
        op.sig = None
        op.epoch = self.epoch
        deps = []
        for b in reads:
            if b.w is not None:
                deps.append(b.w)
        for b in writes:
            if b.w is not None:
                deps.append(b.w)
            deps.extend(b.r.values())
        if dg is not None:
            dg.cnt += 16
            me = (dg, dg.cnt)
            key = dg
        else:
            me = op
            key = eng
        clean = []
        for d in deps:
            if isinstance(d, Op):
                if eng == "pe" and d.eng == "pe":
                    continue
                if d is op:
                    continue
                d.has_dep = True
            clean.append(d)
        op.deps = clean
        for b in reads:
            b.r[key] = me
        for b in writes:
            b.w = me
            b.r = {}
        self.ops[eng].append(op)
        return op

    def barrier(self):
        lasts = []
        for e in self.ENG:
            for o in reversed(self.ops[e]):
                if o.dg is None and o.fn is not None:
                    lasts.append(o)
                    break
        dgl = [(g, g.cnt) for g in self.dgs if g.cnt > 0 and not g.bulk]
        for e in self.ENG:
            op = Op()
            op.eng = e
            op.fn = None
            op.dg = None
            op.has_dep = False
            op.sig = None
            op.epoch = self.epoch
            op.deps = []
            for o in lasts:
                if o.eng != e:
                    o.has_dep = True
                    op.deps.append(o)
            op.deps.extend(dgl)
            self.ops[e].append(op)

    def final_wait(self):
        dgl = [(g, g.cnt) for g in self.dgs if g.cnt > 0]
        op = Op()
        op.eng = "sp"
        op.fn = None
        op.dg = None
        op.has_dep = False
        op.sig = None
        op.epoch = self.epoch
        op.deps = list(dgl)
        for e in self.ENG:
            for o in reversed(self.ops[e]):
                if o.dg is None and o.fn is not None:
                    if e != "sp":
                        o.has_dep = True
                        op.deps.append(o)
                    break
        self.ops["sp"].append(op)

    def emit(self):
        nc = self.nc
        sems = {}
        for e in self.ENG:
            cnt = {}
            for o in self.ops[e]:
                if o.dg is None and o.has_dep and o.fn is not None:
                    k = (e, o.epoch)
                    cnt[k] = cnt.get(k, 0) + 1
                    o.sig = cnt[k]
                    if k not in sems:
                        sems[k] = self.newsem("e_%s_%d" % (e, o.epoch))
        block = self.es.enter_context(nc.Block())

        def run(eng_name):
            def body(e):
                waited = {}
                for o in self.ops[eng_name]:
                    for d in o.deps:
                        if isinstance(d, Op):
                            key = (d.eng, d.epoch)
                            sem = sems[key]
                            v = d.sig
                        else:
                            key = d[0]
                            sem = d[0].sem
                            v = d[1]
                        if waited.get(key, 0) < v:
                            e.wait_ge(sem, v)
                            waited[key] = v
                    if o.fn is None:
                        continue
                    ins = o.fn(e)
                    if o.dg is not None:
                        ins.then_inc(o.dg.sem, 16)
                    elif o.has_dep:
                        ins.then_inc(sems[(eng_name, o.epoch)], 1)
            return body

        block.tensor(run("pe"))
        block.scalar(run("act"))
        block.vector(run("dve"))
        block.gpsimd(run("pool"))
        block.sync(run("sp"))


def make_consts():
    c = {}
    ident = np.eye(128, dtype=np.float32)
    c["ident"] = ident
    k = np.arange(128)[:, None]
    q = np.arange(128)[None, :]
    mcur = (q >= k).astype(np.float32)
    mprev = (k >= q).astype(np.float32)
    c["mcur"] = np.concatenate([mcur, mcur], axis=1)
    c["mprev"] = np.concatenate([mprev, mprev], axis=1)
    ltri = (k <= q).astype(np.float32)
    c["ltri"] = ltri
    c["negm"] = np.where(k <= q, 0.0, NEG).astype(np.float32)
    c["ones"] = np.ones((128, 128), np.float32)
    bs = np.arange(64)[:, None] // 4
    bt = np.arange(64)[None, :] // 4
    same = bs == bt
    s_ = np.arange(64)[:, None]
    t_ = np.arange(64)[None, :]
    lbd = np.zeros((128, 128), np.float32)
    lbd[:64, :64] = (same & (s_ <= t_)).astype(np.float32)
    c["ltri_s"] = lbd
    nbd = np.full((128, 128), NEG, np.float32)
    nbd[:64, :64] = np.where(same & (s_ <= t_), 0.0, NEG)
    c["negm_s"] = nbd
    obd = np.zeros((128, 128), np.float32)
    obd[:64, :64] = same.astype(np.float32)
    c["ones_s"] = obd
    sel = np.zeros((128, 16, 128), np.float32)
    for h in range(16):
        sel[h, h, :] = 1.0
    c["sel"] = sel.reshape(128, 2048)
    bm = np.zeros((128, 16), np.float32)
    bm[np.arange(64), np.arange(64) // 4] = 1.0
    c["bmask"] = bm
    msk = np.zeros((128, 7, 4), np.float32)
    for j in range(7):
        for p in range(128):
            if j < 4:
                row = 1536 + 128 * j + p
            else:
                r = p // 32
                m = 32 * (j - 4) + (p % 32)
                row = 16 * m + r
            for t in range(4):
                diff = 2048 + t - row
                mult = 0
                if 0 <= diff <= 128:
                    mult += 1
                if diff % 4 == 0 and 0 <= diff <= 512:
                    mult += 1
                if diff % 16 == 0 and 0 <= diff <= 2048:
                    mult += 1
                msk[p, j, t] = mult
    mexp = np.broadcast_to(msk[:, :, None, None, :], (128, 7, 4, 2, 4)).reshape(128, 7 * 32)
    c["smask"] = np.ascontiguousarray(mexp)
    mn = np.zeros((128, 128), np.float32)
    for kk in range(64):
        b1, t1 = kk // 4, kk % 4
        for b in range(16):
            for r in range(2):
                for t in range(4):
                    if b == b1 and t1 <= t:
                        mn[kk, b * 8 + r * 4 + t] = 3.0 if t1 == t else 1.0
    c["nmask"] = mn
    pos = np.zeros((128, NBLK), np.float32)
    for b in range(32):
        pos[:, b] = 128 * b + np.arange(128)
    pos[:64, 32] = 2048 + (np.arange(64) % 4)
    c["pos"] = pos
    c["jidx"] = np.broadcast_to(np.arange(64, dtype=np.float32)[None, :], (128, 64)).copy()
    return c


CONST_ORDER = ["ident", "mcur", "mprev", "ltri", "negm", "ones", "ltri_s", "negm_s", "ones_s",
               "sel", "bmask", "smask", "nmask", "pos", "jidx"]


def pack_consts():
    c = make_consts()
    offs = {}
    cols = 0
    for n in CONST_ORDER:
        offs[n] = (cols, c[n].shape[1])
        cols += c[n].shape[1]
    arr = np.zeros((128, cols), np.float32)
    for n in CONST_ORDER:
        o, w = offs[n]
        arr[:, o:o + w] = c[n]
    return arr, offs


def build_program():
    carr, coffs = pack_consts()
    NCC = carr.shape[1]
    nc = bass.Bass("TRN2", target_bir_lowering=False)
    es = ExitStack()
    K = Sched(nc, es)

    def din(name, shape, dt=F32):
        return nc.dram_tensor(name, list(shape), dt, kind="ExternalInput").ap()

    def dout(name, shape, dt=F32):
        return nc.dram_tensor(name, list(shape), dt, kind="ExternalOutput").ap()

    def dscr(name, shape, dt):
        return nc.dram_tensor(name, list(shape), dt).ap()

    x_p = din("x_p", [SEQ, D])
    x_s = din("x_s", [TS, D])
    ck = din("ck", [DEPTH, NSB, L, 512])
    cv = din("cv", [DEPTH, NSB, L, 512])
    st_ssm = din("st_ssm", [DEPTH, NSB, 1024, 128])
    st_conv = din("st_conv", [DEPTH, NSB * 3, 2048])
    p_p = din("p_p", [DEPTH, SEQ, PLE])
    p_s = din("p_s", [DEPTH, TS, PLE])
    norm_mix_g = din("norm_mix_g", [DEPTH, D])
    w_in = din("w_in", [DEPTH, D, INP])
    conv_w = din("conv_w", [DEPTH, 4, 2048])
    conv_b = din("conv_b", [DEPTH, 2048])
    dt_bias = din("dt_bias", [DEPTH, 16])
    a_log = din("a_log", [DEPTH, 16])
    d_skip = din("d_skip", [DEPTH, 16])
    ssm_norm_g = din("ssm_norm_g", [DEPTH, 1024])
    w_out = din("w_out", [DEPTH, D, D])
    norm_ffn_g = din("norm_ffn_g", [DEPTH, D])
    w_g = din("w_ffn_gate", [DEPTH, D, DFF])
    w_u = din("w_ffn_up", [DEPTH, D, DFF])
    w_d = din("w_ffn_down", [DEPTH, DFF, D])
    norm_ple_g = din("norm_ple_g", [DEPTH, D])
    w_pg = din("w_ple_gate", [DEPTH, D, D])
    w_pp = din("w_ple_proj", [DEPTH, PLE, D])
    final_g = din("final_norm_g", [1, D])
    cst = din("consts", [128, NCC])
    y_p = dout("y_p", [SEQ, D])
    y_s = dout("y_s", [TS, D])
    nk_p = dout("nk_p", [DEPTH, 2048, 512])
    nv_p = dout("nv_p", [DEPTH, 2048, 512])
    nssm_p = dout("nssm_p", [DEPTH, 1024, 128])
    nconv_p = dout("nconv_p", [DEPTH, 3, 2048])
    nk_s = dout("nk_s", [DEPTH, NSB, L, 512])
    nv_s = dout("nv_s", [DEPTH, NSB, L, 512])
    nssm_s = dout("nssm_s", [DEPTH, NSB, 1024, 128])
    nconv_s = dout("nconv_s", [DEPTH, NSB * 3, 2048])
    h_d = dscr("h_d", [T, D], F32)
    hnT_d = dscr("hnT_d", [16, 128, T], BF16)
    qT_d = dscr("qT_d", [8, 128, T], BF16)
    kT_d = dscr("kT_d", [4, 128, T], BF16)
    v_d = dscr("v_d", [T, 512], BF16)
    zs_d = dscr("zs_d", [T, 1024], F32)
    dt_d = dscr("dt_d", [T, 16], F32)
    xs_d = dscr("xs_d", [T, 1024], F32)
    bt_d = dscr("bt_d", [T, 512], BF16)
    BT_d = dscr("BT_d", [4, 128, T], BF16)
    CT_d = dscr("CT_d", [4, 128, T], BF16)
    mixT_d = dscr("mixT_d", [16, 128, T], BF16)
    cs_d = dscr("cs_d", [T, 128], F32)

    B_h = [Buf("h%d" % b, True) for b in range(NBLK)]
    B_hnT = Buf("hnT", True)
    B_qT, B_kT, B_v, B_zs, B_dt, B_xs, B_bt, B_BT, B_CT = (Buf("d", True) for _ in range(9))
    B_mixA, B_mixS, B_cs = Buf("d", True), Buf("d", True), Buf("d", True)

    ARENA = 48600
    cs_t = es.enter_context(nc.sbuf_tensor("cst", [128, NCC], F32))
    arena = es.enter_context(nc.sbuf_tensor("arena", [128, ARENA], F32))
    identb_t = es.enter_context(nc.sbuf_tensor("identb", [128, 128], BF16))
    onesb_t = es.enter_context(nc.sbuf_tensor("onesb", [128, 128], BF16))
    negmb_t = es.enter_context(nc.sbuf_tensor("negmb", [128, 128], BF16))
    negmsb_t = es.enter_context(nc.sbuf_tensor("negmsb", [128, 128], BF16))
    mcurb_t = es.enter_context(nc.sbuf_tensor("mcurb", [128, 256], BF16))
    mprevb_t = es.enter_context(nc.sbuf_tensor("mprevb", [128, 256], BF16))
    B_const = Buf("const")

    def C(name, rows=128):
        o, w = coffs[name]
        return cs_t[0:rows, o:o + w]

    PS = [es.enter_context(nc.psum_tensor("ps%d" % i, [128, 512], F32)) for i in range(6)]
    PSB = [es.enter_context(nc.psum_tensor("psb%d" % i, [128, 1024], BF16)) for i in range(2)]
    B_PS = [Buf("ps%d" % i) for i in range(6)]
    B_PSB = [Buf("psb%d" % i) for i in range(2)]
    ps_rr = [0]
    psb_rr = [0]

    def next_ps():
        i = ps_rr[0] % 6
        ps_rr[0] += 1
        return PS[i], B_PS[i]

    def next_psb():
        i = psb_rr[0] % 2
        psb_rr[0] += 1
        return PSB[i], B_PSB[i]

    apos = [0]

    phase0 = [True]

    def areset():
        apos[0] = 0
        if not phase0[0]:
            K.phase_reset()

    def alloc(shape, dt=F32):
        n = 1
        for s in shape[1:]:
            n *= s
        words = n if dt == F32 or dt == I32 else (n + 1) // 2
        o = apos[0]
        apos[0] += words
        assert apos[0] <= ARENA, ("arena overflow", apos[0])
        v = arena[0:shape[0], o:o + words]
        if dt != F32:
            v = v.bitcast(dt)
            v = v[:, 0:n]
        if len(shape) == 3:
            v = v.rearrange("p (a b) -> p a b", b=shape[2])
        elif len(shape) == 4:
            v = v.rearrange("p (a b c) -> p a b c", b=shape[2], c=shape[3])
        return v, Buf()

    def mm(out, lhsT, rhs, start, stop, R, W):
        return K.add("pe", lambda e: e.matmul(out, lhsT, rhs, start=start, stop=stop), R, W)

    def tr(out, in_, ident, R, W):
        return K.add("pe", lambda e: e.transpose(out, in_, ident), R, W)

    def act(out, in_, func, R, W, bias=None, scale=None):
        def f(e):
            kw = {}
            if bias is not None:
                kw["bias"] = bias
            if scale is not None:
                kw["scale"] = scale
            return e.activation(out, in_, func, **kw)
        return K.add("act", f, R, W)

    def cp(eng, out, in_, R, W):
        if eng == "act":
            return act(out, in_, AF.Copy, R, W)
        return K.add(eng, lambda e: e.tensor_copy(out, in_), R, W)

    def tt(eng, out, a, b, op, R, W):
        return K.add(eng, lambda e: e.tensor_tensor(out, a, b, op), R, W)

    def ts(eng, out, a, s1, s2, op0, op1, R, W):
        if s2 is None:
            return K.add(eng, lambda e: e.tensor_scalar(out, a, s1, None, op0), R, W)
        return K.add(eng, lambda e: e.tensor_scalar(out, a, s1, s2, op0, op1), R, W)

    def stt(eng, out, a, s, b, op0, op1, R, W):
        return K.add(eng, lambda e: e.scalar_tensor_tensor(out, a, s, b, op0, op1), R, W)

    def rsqrt(col, b_col):
        act(col, col, AF.Sqrt, [b_col], [b_col])
        K.add("dve", lambda e: e.reciprocal(col, col), [b_col], [b_col])

    def red(eng, out, in_, R, W):
        return K.add(eng, lambda e: e.reduce_sum(out, in_, axis=AX.X), R, W)

    def mset(eng, out, val, W):
        return K.add(eng, lambda e: e.memset(out, val), (), W)

    def dma(q, out, in_, dg, R, W, slow=False):
        if dg in g_st:
            q = "pool"
        sb = None
        for b_ in W:
            if not b_.dram:
                sb = b_
                break
        if sb is None:
            for b_ in R:
                if not b_.dram:
                    sb = b_
                    break
        if sb is not None:
            dg = K.buf_dg(sb)
        if slow:
            return K.add(q, lambda e: e.dma_start(out=out, in_=in_, allow_slow_non_contiguous=True), R, W, dg=dg)
        return K.add(q, lambda e: e.dma_start(out=out, in_=in_), R, W, dg=dg)

    def bc_mid(ap2d, n_mid, n_in):
        raise NotImplementedError

    def bcast_last(ap, n):
        a = [list(x) for x in ap.ap]
        return bass.AP(ap.tensor, ap.offset, a + [[0, n]])

    def bcast_mid(ap, n):
        a = [list(x) for x in ap.ap]
        return bass.AP(ap.tensor, ap.offset, [a[0], [0, n]] + a[1:])

    def rowbc(dram_row_ap, ncols):
        return bass.AP(dram_row_ap.tensor, dram_row_ap.offset, [[0, 128], [1, ncols]])

    g_in = [K.dgroup("in%d" % i) for i in range(4)]
    g_w = [K.dgroup("w%d" % i) for i in range(4)]
    g_st = [K.dgroup("st%d" % i) for i in range(4)]
    g_bulk = K.dgroup("bulk", bulk=True)
    g_misc = K.dgroup("misc")
    rr = {"in": 0, "w": 0, "st": 0}

    def G(kind):
        lst = {"in": g_in, "w": g_w, "st": g_st}[kind]
        i = rr[kind] % len(lst)
        rr[kind] += 1
        return lst[i]

    dma("sp", cs_t[:, :], cst, g_misc, (), [B_const])
    cp("dve", identb_t[:, :], C("ident"), [B_const], [B_const])
    cp("dve", onesb_t[:, :], C("ones"), [B_const], [B_const])
    cp("dve", negmb_t[:, :], C("negm"), [B_const], [B_const])
    cp("dve", negmsb_t[:, :], C("negm_s"), [B_const], [B_const])
    cp("dve", mcurb_t[:, :], C("mcur"), [B_const], [B_const])
    cp("dve", mprevb_t[:, :], C("mprev"), [B_const], [B_const])
    identb = identb_t[:, :]
    identf = C("ident")

    for l in range(DEPTH):
        for b in range(NSB):
            for (src, dst) in ((ck, nk_s), (cv, nv_s)):
                for part in range(4):
                    r0 = 4 + part * 511
                    dma("sp", dst[l, b, r0 - 4:r0 - 4 + 511, :], src[l, b, r0:r0 + 511, :], g_bulk, (), ())
    for i in range(8):
        dma("sp", h_d[i * 512:(i + 1) * 512, :], x_p[i * 512:(i + 1) * 512, :], g_misc, (), B_h[4 * i:4 * i + 4])
    dma("sp", h_d[SEQ:T, :], x_s[:, :], g_misc, (), [B_h[32]])

    areset()
    invf, b_invf = alloc([128, 64])
    act(invf, C("jidx"), AF.Exp, [B_const], [b_invf], scale=-math.log(10000.0) / 64.0)
    TWO_PI = 2.0 * math.pi
    for b in range(NBLK):
        nr = nr_of(b)
        ang, b_ang = alloc([128, 128])
        o, _ = coffs["pos"]
        posc = cs_t[:, o + b:o + b + 1]
        kf, b_kf = alloc([128, 128])
        ki, b_ki = alloc([128, 128], I32)
        ts("dve", ang[:, 64:128], invf, posc, None, ALU.mult, None, [b_invf, B_const], [b_ang])
        ts("dve", ang[:, 0:64], invf, posc, 0.5 * math.pi, ALU.mult, ALU.add, [b_invf, B_const], [b_ang])
        ts("dve", kf, ang, 1.0 / TWO_PI, None, ALU.mult, None, [b_ang], [b_kf])
        cp("dve", ki, kf, [b_kf], [b_ki])
        cp("dve", kf, ki, [b_ki], [b_kf])
        stt("dve", ang, kf, -TWO_PI, ang, ALU.mult, ALU.add, [b_kf, b_ang], [b_ang])
        ts("dve", ang, ang, math.pi, -math.pi, ALU.min, ALU.max, [b_ang], [b_ang])
        act(ang, ang, AF.Sin, [b_ang], [b_ang])
        r0 = b * 128
        dma("sp", cs_d[r0:r0 + nr, :], ang[0:nr, :], g_misc, [b_ang], [B_cs])
    K.barrier()
    phase0[0] = False

    def norm_block(hblk, b_hblk, gam, b_gam, dstT, b_dst, blk, tmp):
        nr = nr_of(blk)
        sq, b_sq = tmp["sq"]
        ss, b_ss = tmp["ss"]
        hn, b_hn = tmp["hn"]
        hT, b_hT = tmp["hT"]
        act(sq[0:nr, :], hblk[0:nr, :], AF.Square, [b_hblk], [b_sq])
        red("dve", ss[0:nr, 0:1], sq[0:nr, :], [b_sq], [b_ss])
        ts("dve", ss[0:nr, 0:1], ss[0:nr, 0:1], 1.0 / D, EPS, ALU.mult, ALU.add, [b_ss], [b_ss])
        rsqrt(ss[0:nr, 0:1], b_ss)
        stt("dve", hn[0:nr, :], hblk[0:nr, :], ss[0:nr, 0:1], gam[0:nr, :], ALU.mult, ALU.mult,
            [b_hblk, b_ss, b_gam], [b_hn])
        for half in range(2):
            pb, b_pb = next_psb()
            for j in range(8):
                kc = half * 8 + j
                tr(pb[:, j * 128:j * 128 + nr], hn[0:nr, kc * 128:(kc + 1) * 128], identb[0:nr, 0:nr],
                   [b_hn, B_const], [b_pb])
            cp("act" if half == 0 else "dve", hT[:, half * 8:(half + 1) * 8, 0:nr],
               pb[:, :].rearrange("p (a b) -> p a b", b=128)[:, :, 0:nr], [b_pb], [b_hT])
        r0 = blk * 128
        dma("sp", dstT[:, :, r0:r0 + nr].rearrange("k p t -> p k t"), hT[:, :, 0:nr], G("st"), [b_hT], [b_dst])

    def norm_tmp():
        return {"sq": alloc([128, 2048]), "ss": alloc([128, 4]), "hn": alloc([128, 2048], BF16),
                "hT": alloc([128, 16, 128], BF16)}

    def load_gamma(row_ap):
        g, b_g = alloc([128, 2048])
        dma("sp", g, rowbc(row_ap, 2048), g_misc, (), [b_g])
        return g, b_g

    def phase_norm_only(gamma_row, dstT=None, b_dst=None):
        if dstT is None:
            dstT, b_dst = hnT_d, B_hnT
        areset()
        gam, b_gam = load_gamma(gamma_row)
        tmps = [norm_tmp() for _ in range(2)]
        hb = [alloc([128, 2048]) for _ in range(2)]
        for blk in range(NBLK):
            nr = nr_of(blk)
            h, b_h = hb[blk % 2]
            dma("sp", h[0:nr, :], h_d[blk * 128:blk * 128 + nr, :], G("in"), [B_h[blk]], [b_h])
            norm_block(h, b_h, gam, b_gam, dstT, b_dst, blk, tmps[blk % 2])
        K.barrier()

    def load_w(dst, b_dst, wsrc, k0, nk, c0, ncols):
        for kk in range(0, nk, 4):
            n = min(4, nk - kk)
            src = wsrc[(k0 + kk) * 128:(k0 + kk + n) * 128, c0:c0 + ncols].rearrange("(k p) c -> p k c", p=128)
            dma("pool", dst[:, kk:kk + n, 0:ncols], src, G("w"), (), [b_dst])

    def phase_inproj(l):
        areset()
        wb = [alloc([128, 16, 512], BF16) for _ in range(2)]
        hx = [alloc([128, 16, 512], BF16) for _ in range(2)]
        ev = [alloc([128, 512]) for _ in range(2)]
        evb = [alloc([128, 512], BF16) for _ in range(2)]
        cst_t = [alloc([128, 128]) for _ in range(2)]
        rot = [alloc([128, 512]) for _ in range(2)]
        rtmp = [alloc([128, 4, 64]) for _ in range(4)]
        hT4 = [alloc([128, 4, 128], BF16) for _ in range(2)]
        dtb, b_dtb = alloc([128, 16])
        dtw = [alloc([128, 16]) for _ in range(2)]
        pad = [alloc([128, 3 + 512]) for _ in range(2)]
        carry = alloc([128, 16, 3])
        acc = [alloc([128, 512]) for _ in range(2)]
        cw, b_cw = alloc([128, 16, 4])
        cb, b_cb = alloc([128, 16])
        sct, b_sct = alloc([128, 16, 48])
        scin, b_scin = alloc([48, 2048])
        pads = [alloc([128, 16, 7]) for _ in range(2)]
        accs = [alloc([128, 16, 4]) for _ in range(2)]
        xsf = [alloc([128, 512]) for _ in range(2)]
        xsb = [alloc([128, 512], BF16) for _ in range(2)]
        xtk = [alloc([128, 4, 128]) for _ in range(2)]
        btk = [alloc([128, 4, 128], BF16) for _ in range(2)]
        ncs, b_ncs = alloc([48, 2048])
        ncp, b_ncp = alloc([128, 16, 3])

        dma("sp", dtb, rowbc(dt_bias[l:l + 1, :], 16), g_misc, (), [b_dtb])
        cwin, b_cwin = alloc([8, 2048])
        dma("sp", cwin[0:4, :], conv_w[l], g_misc, (), [b_cwin])
        dma("sp", cwin[4:5, :], conv_b[l:l + 1, :], g_misc, (), [b_cwin])
        for kc in range(16):
            ps, b_ps = next_ps()
            tr(ps[:, 0:5], cwin[0:5, kc * 128:(kc + 1) * 128], identf[0:5, 0:5], [b_cwin, B_const], [b_ps])
            cp("act", cw[:, kc, :], ps[:, 0:4], [b_ps], [b_cw])
            cp("act", cb[:, kc:kc + 1], ps[:, 4:5], [b_ps], [b_cb])
        dma("sp", scin, st_conv[l], g_misc, (), [b_scin])
        for kc in range(16):
            ps, b_ps = next_ps()
            tr(ps[:, 0:48], scin[:, kc * 128:(kc + 1) * 128], identf[0:48, 0:48], [b_scin, B_const], [b_ps])
            cp("act", sct[:, kc, :], ps[:, 0:48], [b_ps], [b_sct])
        mset("dve", carry[0], 0.0, [carry[1]])

        groups = [("q", 0, 512), ("q", 512, 512), ("k", 1024, 512), ("v", 1536, 512),
                  ("z", 2048, 512), ("z", 2560, 512),
                  ("x", 3072, 512), ("x", 3584, 512), ("x", 4096, 512), ("x", 4608, 512),
                  ("dt", 5120, 16)]
        wi = w_in[l]
        for gi, (kind, c0, ncw) in enumerate(groups):
            w, b_w = wb[gi % 2]
            load_w(w, b_w, wi, 0, 16, c0, ncw)
            for tti in range(9):
                tw = 512 if tti < 8 else 64
                t0 = tti * 512
                hxt, b_hx = hx[tti % 2]
                dma("sp", hxt[:, :, 0:tw], hnT_d[:, :, t0:t0 + tw].rearrange("k p t -> p k t"), G("in"),
                    [B_hnT], [b_hx])
                if kind == "x":
                    for sc in range(4):
                        kcx = (c0 - 3072) // 128 + sc
                        ps, b_ps = next_ps()
                        for kc in range(16):
                            mm(ps[:, 0:tw], w[:, kc, sc * 128:(sc + 1) * 128], hxt[:, kc, 0:tw],
                               kc == 0, kc == 15, [b_w, b_hx], [b_ps])
                        xf, b_xf = xsf[sc % 2]
                        if tti < 8:
                            pd, b_pd = pad[sc % 2]
                            ac, b_ac = acc[sc % 2]
                            cp("dve", pd[:, 0:3], carry[0][:, kcx, :], [carry[1]], [b_pd])
                            cp("act", pd[:, 3:3 + 512], ps[:, :], [b_ps], [b_pd])
                            cp("dve", carry[0][:, kcx, :], pd[:, 512:515], [b_pd], [carry[1]])
                            if tti == 7:
                                cp("dve", ncp[:, kcx, :], pd[:, 512:515], [b_pd], [b_ncp])
                            ts("dve", ac, pd[:, 0:512], cw[:, kcx, 0:1], None, ALU.mult, None, [b_pd, b_cw], [b_ac])
                            for wv in range(1, 4):
                                stt("dve", ac, pd[:, wv:wv + 512], cw[:, kcx, wv:wv + 1], ac, ALU.mult, ALU.add,
                                    [b_pd, b_cw, b_ac], [b_ac])
                            act(xf, ac, AF.Silu, [b_ac, b_cb], [b_xf], bias=cb[:, kcx:kcx + 1])
                        else:
                            pd, b_pd = pads[sc % 2]
                            ac, b_ac = accs[sc % 2]
                            cp("dve", pd[:, :, 0:3], sct[:, kcx, :].rearrange("p (b t) -> p b t", t=3),
                               [b_sct], [b_pd])
                            cp("act", pd[:, :, 3:7], ps[:, 0:64].rearrange("p (b t) -> p b t", t=4), [b_ps], [b_pd])
                            ts("dve", ac, pd[:, :, 0:4], cw[:, kcx, 0:1], None, ALU.mult, None, [b_pd, b_cw], [b_ac])
                            for wv in range(1, 4):
                                stt("dve", ac, pd[:, :, wv:wv + 4], cw[:, kcx, wv:wv + 1], ac, ALU.mult, ALU.add,
                                    [b_pd, b_cw, b_ac], [b_ac])
                            act(xf[:, 0:64], ac.rearrange("p b t -> p (b t)"), AF.Silu, [b_ac, b_cb], [b_xf],
                                bias=cb[:, kcx:kcx + 1])
                            nct, b_nct = xtk[sc % 2]
                            cp("dve", nct[:, 0, 0:48].rearrange("p (b t) -> p b t", t=3), pd[:, :, 4:7], [b_pd], [b_nct])
                            ps2, b_ps2 = next_ps()
                            tr(ps2[0:48, 0:128], nct[:, 0, 0:48], identf, [b_nct, B_const], [b_ps2])
                            cp("act", ncs[:, kcx * 128:(kcx + 1) * 128], ps2[0:48, 0:128], [b_ps2], [b_ncs])
                        if kcx < 8:
                            xt, b_xt = xtk[sc % 2]
                            nsb = 4 if tti < 8 else 1
                            ps3, b_ps3 = next_ps()
                            for sb in range(nsb):
                                nrr = 128 if tti < 8 else 64
                                tr(ps3[0:nrr, sb * 128:(sb + 1) * 128], xf[:, sb * 128:sb * 128 + nrr], identf,
                                   [b_xf, B_const], [b_ps3])
                            nrr = 128 if tti < 8 else 64
                            cp("act", xt[0:nrr, 0:nsb, :], ps3[0:nrr, 0:nsb * 128].rearrange("p (a b) -> p a b", b=128),
                               [b_ps3], [b_xt])
                            if tti < 8:
                                dma("sp", xs_d[t0:t0 + 512, kcx * 128:(kcx + 1) * 128].rearrange("(a p) c -> p a c", p=128),
                                    xt[:, :, :], G("st"), [b_xt], [B_xs])
                            else:
                                dma("sp", xs_d[t0:t0 + 64, kcx * 128:(kcx + 1) * 128], xt[0:64, 0, :], G("st"),
                                    [b_xt], [B_xs])
                        else:
                            xb_, b_xb = xsb[sc % 2]
                            cp("dve", xb_[:, 0:tw], xf[:, 0:tw], [b_xf], [b_xb])
                            isB = kcx < 12
                            gq = kcx - 8 if isB else kcx - 12
                            dst = BT_d if isB else CT_d
                            dma("sp", dst[gq, :, t0:t0 + tw], xb_[:, 0:tw], G("st"), [b_xb], [B_BT if isB else B_CT])
                            if isB:
                                bk, b_bk = btk[sc % 2]
                                nsb = 4 if tti < 8 else 1
                                nrr = 128 if tti < 8 else 64
                                pb, b_pb = next_psb()
                                for sb in range(nsb):
                                    tr(pb[0:nrr, sb * 128:(sb + 1) * 128], xb_[:, sb * 128:sb * 128 + nrr], identb,
                                       [b_xb, B_const], [b_pb])
                                cp("act", bk[0:nrr, 0:nsb, :], pb[0:nrr, 0:nsb * 128].rearrange("p (a b) -> p a b", b=128),
                                   [b_pb], [b_bk])
                                if tti < 8:
                                    dma("sp", bt_d[t0:t0 + 512, gq * 128:(gq + 1) * 128].rearrange("(a p) c -> p a c", p=128),
                                        bk[:, :, :], G("st"), [b_bk], [B_bt])
                                else:
                                    dma("sp", bt_d[t0:t0 + 64, gq * 128:(gq + 1) * 128], bk[0:64, 0, :], G("st"),
                                        [b_bk], [B_bt])
                    continue
                nsb = 4 if tti < 8 else 1
                for sb in range(nsb):
                    blk = tti * 4 + sb if tti < 8 else 32
                    nr = nr_of(blk)
                    r0 = blk * 128
                    ps, b_ps = next_ps()
                    for kc in range(16):
                        mm(ps[0:nr, 0:ncw], hxt[:, kc, sb * 128:sb * 128 + nr], w[:, kc, 0:ncw],
                           kc == 0, kc == 15, [b_w, b_hx], [b_ps])
                    if kind in ("q", "k"):
                        cs_, b_cs_ = cst_t[blk % 2]
                        dma("sp", cs_[0:nr, :], cs_d[r0:r0 + nr, :], G("in"), [B_cs], [b_cs_])
                        ro, b_ro = rot[blk % 2]
                        pv = ps[0:nr, :].rearrange("p (h d) -> p h d", d=128)
                        rv = ro[0:nr, :].rearrange("p (h d) -> p h d", d=128)
                        cosb = bcast_mid(cs_[0:nr, 0:64], 4)
                        sinb = bcast_mid(cs_[0:nr, 64:128], 4)
                        t1, b_t1 = rtmp[0]
                        t2, b_t2 = rtmp[1]
                        t3, b_t3 = rtmp[2]
                        t4, b_t4 = rtmp[3]
                        tt("dve", t1[0:nr], pv[:, :, 0:64], cosb, ALU.mult, [b_ps, b_cs_], [b_t1])
                        tt("dve", t2[0:nr], pv[:, :, 64:128], sinb, ALU.mult, [b_ps, b_cs_], [b_t2])
                        tt("dve", t3[0:nr], pv[:, :, 64:128], cosb, ALU.mult, [b_ps, b_cs_], [b_t3])
                        tt("dve", t4[0:nr], pv[:, :, 0:64], sinb, ALU.mult, [b_ps, b_cs_], [b_t4])
                        tt("dve", rv[:, :, 0:64], t1[0:nr], t2[0:nr], ALU.subtract, [b_t1, b_t2], [b_ro])
                        tt("dve", rv[:, :, 64:128], t3[0:nr], t4[0:nr], ALU.add, [b_t3, b_t4], [b_ro])
                        if kind == "k":
                            if 16 <= blk < 32:
                                dma("sp", nk_p[l, r0 - 2048:r0 - 2048 + 128, :], ro[:, :], G("st"), [b_ro], ())
                            if blk == 32:
                                for bb in range(NSB):
                                    dma("sp", nk_s[l, bb, 2044:2048, :], ro[bb * 4:bb * 4 + 4, :], G("st"), [b_ro], ())
                        rb_, b_rb = evb[blk % 2]
                        cp("act", rb_[0:nr, :], ro[0:nr, :], [b_ro], [b_rb])
                        pb, b_pb = next_psb()
                        for hh in range(4):
                            tr(pb[:, hh * 128:hh * 128 + nr], rb_[0:nr, hh * 128:(hh + 1) * 128], identb[0:nr, 0:nr],
                               [b_rb, B_const], [b_pb])
                        h4, b_h4 = hT4[blk % 2]
                        cp("act", h4[:, :, 0:nr], pb[:, 0:512].rearrange("p (a b) -> p a b", b=128)[:, :, 0:nr],
                           [b_pb], [b_h4])
                        if kind == "q":
                            h0 = c0 // 128
                            dma("sp", qT_d[h0:h0 + 4, :, r0:r0 + nr].rearrange("h p t -> p h t"), h4[:, :, 0:nr],
                                G("st"), [b_h4], [B_qT])
                        else:
                            dma("sp", kT_d[:, :, r0:r0 + nr].rearrange("h p t -> p h t"), h4[:, :, 0:nr],
                                G("st"), [b_h4], [B_kT])
                    elif kind == "v":
                        e_, b_e = ev[blk % 2]
                        cp("act", e_[0:nr, :], ps[0:nr, :], [b_ps], [b_e])
                        if 16 <= blk < 32:
                            dma("sp", nv_p[l, r0 - 2048:r0 - 2048 + 128, :], e_[:, :], G("st"), [b_e], ())
                        if blk == 32:
                            for bb in range(NSB):
                                dma("sp", nv_s[l, bb, 2044:2048, :], e_[bb * 4:bb * 4 + 4, :], G("st"), [b_e], ())
                        eb_, b_eb = evb[blk % 2]
                        cp("dve", eb_[0:nr, :], e_[0:nr, :], [b_e], [b_eb])
                        dma("sp", v_d[r0:r0 + nr, :], eb_[0:nr, :], G("st"), [b_eb], [B_v])
                    elif kind == "z":
                        e_, b_e = ev[blk % 2]
                        act(e_[0:nr, :], ps[0:nr, :], AF.Silu, [b_ps], [b_e])
                        zc = c0 - 2048
                        dma("sp", zs_d[r0:r0 + nr, zc:zc + 512], e_[0:nr, :], G("st"), [b_e], [B_zs])
                    else:
                        dw, b_dw = dtw[blk % 2]
                        tt("dve", dw[0:nr, :], ps[0:nr, 0:16], dtb[0:nr, :], ALU.add, [b_ps, b_dtb], [b_dw])
                        act(dw[0:nr, :], dw[0:nr, :], AF.Exp, [b_dw], [b_dw])
                        act(dw[0:nr, :], dw[0:nr, :], AF.Ln, [b_dw], [b_dw], bias=1.0)
                        dma("sp", dt_d[r0:r0 + nr, :], dw[0:nr, :], G("st"), [b_dw], [B_dt])
        dma("sp", nconv_s[l], ncs[:, :], G("st"), [b_ncs], ())
        ncpo, b_ncpo = alloc([4, 2048])
        for kc in range(16):
            ps, b_ps = next_ps()
            tr(ps[0:3, 0:128], ncp[:, kc, :], identf, [b_ncp, B_const], [b_ps])
            cp("act", ncpo[0:3, kc * 128:(kc + 1) * 128], ps[0:3, 0:128], [b_ps], [b_ncpo])
        dma("sp", nconv_p[l], ncpo[0:3, :], G("st"), [b_ncpo], ())
        K.barrier()

    def phase_attention(l):
        areset()
        qT, b_q = alloc([128, 2, SEQ], BF16)
        kT, b_k = alloc([128, SEQ], BF16)
        accN, b_aN = alloc([128, 2, SEQ])
        accD, b_aD = alloc([128, 2, SEQ])
        vb = [alloc([128, 128], BF16) for _ in range(4)]
        eb = [alloc([128, 256], BF16) for _ in range(4)]
        rc, b_rc = alloc([128, 2, 512])
        ob = [alloc([128, 2, 512], BF16) for _ in range(2)]
        for g in range(4):
            dma("sp", qT, qT_d[2 * g:2 * g + 2, :, 0:SEQ].rearrange("h p t -> p h t"), G("in"), [B_qT], [b_q])
            dma("sp", kT, kT_d[g, :, 0:SEQ], G("in"), [B_kT], [b_k])
            vi = 0
            for bi, dil in enumerate((1, 4, 16)):
                nblk_r = 32 // dil
                for r in range(dil):
                    vprev = None
                    for m in range(nblk_r):
                        s0 = r + dil * 128 * m
                        vt, b_vt = vb[vi % 4]
                        vi += 1
                        vsrc = bass.AP(v_d.tensor, v_d.offset + s0 * 512 + g * 128, [[dil * 512, 128], [1, 128]])
                        dma("sp", vt, vsrc, G("in"), [B_v], [b_vt])
                        qsl = qT[:, :, s0:s0 + dil * 127 + 1:dil]
                        psN, b_pN = next_ps()
                        psD, b_pD = next_ps()
                        kbl = [(s0, vt, b_vt, mcurb_t)]
                        if m > 0:
                            kbl.append((s0 - dil * 128, vprev[0], vprev[1], mprevb_t))
                        for ki, (ks0, vtile, b_vtile, mtile) in enumerate(kbl):
                            psS, b_pS = next_ps()
                            mm(psS[:, 0:256].rearrange("p (h q) -> p h q", q=128), kT[:, ks0:ks0 + dil * 127 + 1:dil], qsl,
                               True, True, [b_k, b_q], [b_pS])
                            et, b_et = eb[(vi + ki) % 4]
                            act(et, psS[:, 0:256], AF.Exp, [b_pS], [b_et], scale=SCALE)
                            tt("dve", et, et, mtile[:, :], ALU.mult, [b_et, B_const], [b_et])
                            mm(psN[:, 0:256], vtile, et, ki == 0, ki == len(kbl) - 1, [b_vtile, b_et], [b_pN])
                            mm(psD[:, 0:256], onesb_t[:, :], et, ki == 0, ki == len(kbl) - 1, [B_const, b_et], [b_pD])
                        vprev = (vt, b_vt)
                        aN = accN[:, :, s0:s0 + dil * 127 + 1:dil]
                        aD = accD[:, :, s0:s0 + dil * 127 + 1:dil]
                        pN = psN[:, 0:256].rearrange("p (h q) -> p h q", q=128)
                        pD = psD[:, 0:256].rearrange("p (h q) -> p h q", q=128)
                        if bi == 0:
                            cp("act", aN, pN, [b_pN], [b_aN])
                            cp("dve", aD, pD, [b_pD], [b_aD])
                        else:
                            tt("dve", aN, aN, pN, ALU.add, [b_aN, b_pN], [b_aN])
                            tt("pool", aD, aD, pD, ALU.add, [b_aD, b_pD], [b_aD]) if False else \
                                tt("dve", aD, aD, pD, ALU.add, [b_aD, b_pD], [b_aD])
            for tti in range(8):
                t0 = tti * 512
                K.add("dve", lambda e, t0=t0: e.reciprocal(rc, accD[:, :, t0:t0 + 512]), [b_aD], [b_rc])
                o_, b_o = ob[tti % 2]
                tt("dve", o_, accN[:, :, t0:t0 + 512], rc, ALU.mult, [b_aN, b_rc], [b_o])
                dma("sp", mixT_d[2 * g:2 * g + 2, :, t0:t0 + 512].rearrange("h p t -> p h t"), o_, G("st"),
                    [b_o], [B_mixA])
        K.barrier()

        areset()
        qs_t, b_qs = alloc([128, 4 * 128], BF16)
        qraw, b_qraw = alloc([128, 8, 64], BF16)
        kn, b_kn = alloc([128, 4, 64], BF16)
        vn, b_vn = alloc([64, 512], BF16)
        dma("sp", qraw, qT_d[:, :, SEQ:T].rearrange("h p t -> p h t"), G("in"), [B_qT], [b_qraw])
        dma("sp", kn, kT_d[:, :, SEQ:T].rearrange("h p t -> p h t"), G("in"), [B_kT], [b_kn])
        dma("sp", vn, v_d[SEQ:T, :], G("in"), [B_v], [b_vn])
        qs5 = qs_t.rearrange("p (g b r t) -> p g b r t", g=4, b=NSB, r=2)
        for g in range(4):
            for r in range(2):
                cp("dve", qs5[:, g, :, r, :], qraw[:, 2 * g + r, :].rearrange("p (b t) -> p b t", t=4),
                   [b_qraw], [b_qs])
        kc_t = [alloc([128, 7, 512], BF16) for _ in range(2)]
        vc_t = [alloc([128, 7, 512], BF16) for _ in range(2)]
        kTt = [alloc([128, 4, 7 * 128], BF16) for _ in range(2)]
        et_t = [alloc([128, 7 * 32], BF16) for _ in range(2)]
        smb, b_smb = alloc([128, 7 * 32], BF16)
        cp("dve", smb, C("smask"), [B_const], [b_smb])
        nmb, b_nmb = alloc([64, 128], BF16)
        cp("dve", nmb, C("nmask", 64), [B_const], [b_nmb])
        numC, b_numC = PS[0], B_PS[0]
        denC, b_denC = PS[1], B_PS[1]
        for b in range(NSB):
            kc, b_kc = kc_t[b % 2]
            vc, b_vc = vc_t[b % 2]
            for (src, dstt, b_dst) in ((ck, kc, b_kc), (cv, vc, b_vc)):
                dma("pool", dstt[:, 0:4, :], src[l, b, 1536:2048, :].rearrange("(j p) c -> p j c", p=128),
                    G("in"), (), [b_dst])
                for r in range(4):
                    base = src[l, b].offset + r * 512
                    sap = bass.AP(src.tensor, base, [[16 * 512, 32], [32 * 16 * 512, 3], [1, 512]])
                    dma("pool", dstt[32 * r:32 * r + 32, 4:7, :], sap, G("in"), (), [b_dst])
            kt_, b_kt = kTt[b % 2]
            for g in range(4):
                for half in range(2):
                    js = range(0, 4) if half == 0 else range(4, 7)
                    pb, b_pb = next_psb()
                    for jj, j in enumerate(js):
                        tr(pb[:, jj * 128:(jj + 1) * 128], kc[:, j, g * 128:(g + 1) * 128], identb, [b_kc, B_const], [b_pb])
                    n = len(js)
                    j0 = js[0]
                    cp("act" if half == 0 else "dve", kt_[:, g, j0 * 128:(j0 + n) * 128], pb[:, 0:n * 128], [b_pb], [b_kt])
            psS, b_pS = PS[2 + b % 2], B_PS[2 + b % 2]
            for j in range(7):
                for g in range(4):
                    mm(psS[:, j * 32 + g * 8:j * 32 + g * 8 + 8], kt_[:, g, j * 128:(j + 1) * 128],
                       qs5[:, g, b, :, :], True, True, [b_kt, b_qs], [b_pS])
            et, b_et = et_t[b % 2]
            act(et, psS[:, 0:224], AF.Exp, [b_pS], [b_et], scale=SCALE)
            tt("dve", et, et, smb, ALU.mult, [b_et, b_smb], [b_et])
            for g in range(4):
                c0 = g * 128 + b * 8
                for j in range(7):
                    mm(numC[:, c0:c0 + 8], vc[:, j, g * 128:(g + 1) * 128], et[:, j * 32 + g * 8:j * 32 + g * 8 + 8],
                       j == 0, j == 6, [b_vc, b_et], [b_numC])
                for j in range(7):
                    mm(denC[:, c0:c0 + 8], onesb_t[:, :], et[:, j * 32 + g * 8:j * 32 + g * 8 + 8],
                       j == 0, j == 6, [B_const, b_et], [b_denC])
        numS, b_numS = alloc([128, 512])
        denS, b_denS = alloc([128, 512])
        cp("act", numS, numC[:, :], [b_numC], [b_numS])
        cp("dve", denS, denC[:, :], [b_denC], [b_denS])
        en_t = [alloc([64, 128], BF16) for _ in range(2)]
        for g in range(4):
            psS, b_pS = PS[2 + g % 2], B_PS[2 + g % 2]
            mm(psS[0:64, 0:128], kn[:, g, :], qs_t[:, g * 128:(g + 1) * 128], True, True, [b_kn, b_qs], [b_pS])
            en, b_en = en_t[g % 2]
            act(en, psS[0:64, 0:128], AF.Exp, [b_pS], [b_en], scale=SCALE)
            tt("dve", en, en, nmb, ALU.mult, [b_en, b_nmb], [b_en])
            psA, b_pA = PS[4], B_PS[4]
            psB, b_pB = PS[5], B_PS[5]
            mm(psA[:, 0:128], vn[:, g * 128:(g + 1) * 128], en, True, True, [b_vn, b_en], [b_pA])
            mm(psB[:, 0:128], onesb_t[0:64, :], en, True, True, [B_const, b_en], [b_pB])
            tt("dve", numS[:, g * 128:(g + 1) * 128], numS[:, g * 128:(g + 1) * 128], psA[:, 0:128], ALU.add,
               [b_numS, b_pA], [b_numS])
            tt("dve", denS[:, g * 128:(g + 1) * 128], denS[:, g * 128:(g + 1) * 128], psB[:, 0:128], ALU.add,
               [b_denS, b_pB], [b_denS])
        K.add("dve", lambda e: e.reciprocal(denS, denS), [b_denS], [b_denS])
        tt("dve", numS, numS, denS, ALU.mult, [b_numS, b_denS], [b_numS])
        so, b_so = alloc([128, 8, 64], BF16)
        n5 = numS.rearrange("p (g b r t) -> p g b r t", g=4, b=NSB, r=2)
        for g in range(4):
            for r in range(2):
                cp("dve", so[:, 2 * g + r, :].rearrange("p (b t) -> p b t", t=4), n5[:, g, :, r, :], [b_numS], [b_so])
        dma("sp", mixT_d[0:8, :, SEQ:T].rearrange("h p t -> p h t"), so, G("st"), [b_so], [B_mixA])
        K.barrier()

    def phase_ssd(l):
        areset()
        alc, b_alc = alloc([128, 16])
        dsk, b_dsk = alloc([128, 16])
        Dbc, b_Dbc = alloc([128, 1024])
        gs, b_gs = alloc([128, 1024])
        dma("sp", alc, rowbc(a_log[l:l + 1, :], 16), g_misc, (), [b_alc])
        act(alc, alc, AF.Exp, [b_alc], [b_alc])
        ts("dve", alc, alc, -1.0, None, ALU.mult, None, [b_alc], [b_alc])
        dma("sp", dsk, rowbc(d_skip[l:l + 1, :], 16), g_misc, (), [b_dsk])
        cp("dve", Dbc.rearrange("p (h q) -> p h q", q=64), bcast_last(dsk, 64), [b_dsk], [b_Dbc])
        dma("sp", gs, rowbc(ssm_norm_g[l:l + 1, :], 1024), g_misc, (), [b_gs])
        hst, b_hst = alloc([128, 1024])
        hsb, b_hsb = alloc([128, 1024], BF16)
        mset("dve", hst, 0.0, [b_hst])
        xs2 = [alloc([128, 1024]) for _ in range(2)]
        zs2 = [alloc([128, 1024]) for _ in range(2)]
        dt2 = [alloc([128, 16]) for _ in range(2)]
        bt2 = [alloc([128, 512], BF16) for _ in range(2)]
        BT2 = [alloc([128, 4, 128], BF16) for _ in range(2)]
        CT2 = [alloc([128, 4, 128], BF16) for _ in range(2)]
        dtA, b_dtA = alloc([128, 16])
        acu, b_acu = alloc([128, 16])
        nacu, b_nacu = alloc([128, 16])
        tot, b_tot = alloc([128, 16])
        exA, b_exA = alloc([128, 16])
        toe, b_toe = alloc([128, 16])
        cde, b_cde = alloc([128, 16])
        acT, b_acT = alloc([16, 128])
        nacT, b_nacT = alloc([16, 128])
        cbT, b_cbT = alloc([128, 128])
        dec = [alloc([128, 4, 128]) for _ in range(2)]
        MT = [alloc([128, 4, 128], BF16) for _ in range(2)]
        xc, b_xc = alloc([128, 1024])
        xcb, b_xcb = alloc([128, 1024], BF16)
        xte, b_xte = alloc([128, 1024], BF16)
        y1, b_y1 = alloc([128, 1024])
        yb, b_yb = alloc([128, 1024], BF16)
        sq, b_sq = alloc([128, 1024])
        ss, b_ss = alloc([128, 4])
        yT, b_yT = alloc([128, 8, 128], BF16)
        h0T, b_h0T = alloc([128, NSB, 1024], BF16)
        h0n = [alloc([128, 8, 128]) for _ in range(2)]
        CTz, b_CTz = alloc([128, NSB * 64], BF16)
        Bz, b_Bz = alloc([64, NSB * 128], BF16)
        dtx, b_dtx = alloc([64, 1024])
        cdT, b_cdT = alloc([128, 8 * 16])
        nst = [alloc([128, 128]) for _ in range(2)]
        bmb, b_bmb = alloc([64, 16], BF16)
        cp("dve", bmb, C("bmask", 64), [B_const], [b_bmb])

        P_small, B_small = PS[0], B_PS[0]

        def chunk(cidx, nr, Ltri, Ones, negmb, sample):
            r0 = cidx * 128
            i2 = cidx % 2
            xs, b_xs = xs2[i2]
            zs, b_zs = zs2[i2]
            dtt, b_dtt = dt2[i2]
            btk, b_btk = bt2[i2]
            BTt, b_BTt = BT2[i2]
            CTt, b_CTt = CT2[i2]
            dma("sp", xs[0:nr, :], xs_d[r0:r0 + nr, :], G("in"), [B_xs], [b_xs])
            dma("sp", zs[0:nr, :], zs_d[r0:r0 + nr, :], G("in"), [B_zs], [b_zs])
            dma("sp", dtt[0:nr, :], dt_d[r0:r0 + nr, :], G("in"), [B_dt], [b_dtt])
            dma("sp", btk[0:nr, :], bt_d[r0:r0 + nr, :], G("in"), [B_bt], [b_btk])
            dma("sp", BTt[:, :, 0:nr], BT_d[:, :, r0:r0 + nr].rearrange("g p t -> p g t"), G("in"), [B_BT], [b_BTt])
            dma("sp", CTt[:, :, 0:nr], CT_d[:, :, r0:r0 + nr].rearrange("g p t -> p g t"), G("in"), [B_CT], [b_CTt])
            tt("dve", dtA[0:nr, :], dtt[0:nr, :], alc[0:nr, :], ALU.mult, [b_dtt, b_alc], [b_dtA])
            mm(P_small[0:nr, 0:16], Ltri[0:nr, 0:nr], dtA[0:nr, :], True, True, [B_const, b_dtA], [B_small])
            mm(P_small[0:nr, 16:32], Ones[0:nr, 0:nr], dtA[0:nr, :], True, True, [B_const, b_dtA], [B_small])
            mm(P_small[0:16, 32:32 + nr], dtA[0:nr, :], Ltri[0:nr, 0:nr], True, True, [B_const, b_dtA], [B_small])
            cp("act", acu[0:nr, :], P_small[0:nr, 0:16], [B_small], [b_acu])
            cp("act", tot[0:nr, :], P_small[0:nr, 16:32], [B_small], [b_tot])
            cp("act", acT[:, 0:nr], P_small[0:16, 32:32 + nr], [B_small], [b_acT])
            ts("dve", nacT[:, 0:nr], acT[:, 0:nr], -1.0, None, ALU.mult, None, [b_acT], [b_nacT])
            act(exA[0:nr, :], acu[0:nr, :], AF.Exp, [b_acu], [b_exA])
            tt("dve", toe[0:nr, :], tot[0:nr, :], acu[0:nr, :], ALU.subtract, [b_tot, b_acu], [b_toe])
            act(toe[0:nr, :], toe[0:nr, :], AF.Exp, [b_toe], [b_toe])
            act(cde[0:nr, :], tot[0:nr, :], AF.Exp, [b_tot], [b_cde])
            xs3 = xs[0:nr, :].rearrange("p (h q) -> p h q", q=64)
            tt("dve", xc[0:nr, :].rearrange("p (h q) -> p h q", q=64), xs3, bcast_last(dtt[0:nr, :], 64), ALU.mult,
               [b_xs, b_dtt], [b_xc])
            cp("act", xcb[0:nr, :], xc[0:nr, :], [b_xc], [b_xcb])
            tt("dve", xte[0:nr, :].rearrange("p (h q) -> p h q", q=64), xc[0:nr, :].rearrange("p (h q) -> p h q", q=64),
               bcast_last(toe[0:nr, :], 64), ALU.mult, [b_xc, b_toe], [b_xte])
            psY = [(PS[3], B_PS[3]), (PS[4], B_PS[4])]
            psO = [(PS[1], B_PS[1]), (PS[2], B_PS[2])]
            if not sample:
                cp("dve", hsb, hst, [b_hst], [b_hsb])
                for g in range(4):
                    po, b_po = psO[g // 2]
                    mm(po[0:nr, (g % 2) * 256:(g % 2) * 256 + 256], CTt[:, g, 0:nr], hsb[:, g * 256:(g + 1) * 256],
                       True, True, [b_CTt, b_hsb], [b_po])
            else:
                mset("dve", CTz, 0.0, [b_CTz])
                for g in range(4):
                    if g > 0:
                        mset("dve", CTz, 0.0, [b_CTz])
                    dstz = bass.AP(CTz.tensor, CTz.offset, [list(CTz.ap[0]), [68, NSB], [1, 4]])
                    cp("dve", dstz, CTt[:, g, 0:64].rearrange("p (b t) -> p b t", t=4), [b_CTt], [b_CTz])
                    po, b_po = psO[g // 2]
                    for b in range(NSB):
                        mm(po[0:64, (g % 2) * 256:(g % 2) * 256 + 256], CTz[:, b * 64:(b + 1) * 64],
                           h0T[:, b, g * 256:(g + 1) * 256], b == 0, b == NSB - 1, [b_CTz, b_h0T], [b_po])
            for g in range(4):
                mm(P_small[0:nr, 256:256 + nr], BTt[:, g, 0:nr], CTt[:, g, 0:nr], True, True, [b_BTt, b_CTt], [B_small])
                cp("act", cbT[0:nr, 0:nr], P_small[0:nr, 256:256 + nr], [B_small], [b_cbT])
                pd, b_pd = (PS[5], B_PS[5])
                selc = C("sel").rearrange("p (h m) -> p h m", m=128)
                for hh in range(4):
                    h = g * 4 + hh
                    o_ = pd[0:nr, hh * 128:hh * 128 + nr]
                    mm(o_, selc[0:16, h, 0:nr], acT[:, 0:nr], True, False, [B_const, b_acT], [b_pd])
                    mm(o_, nacT[:, 0:nr], selc[0:16, h, 0:nr], False, False, [B_const, b_nacT], [b_pd])
                    mm(o_, identb[0:nr, 0:nr], negmb[0:nr, 0:nr], False, True, [B_const], [b_pd])
                dc, b_dc = dec[g % 2]
                act(dc[0:nr, :, 0:nr], pd[0:nr, :].rearrange("p (a b) -> p a b", b=128)[:, :, 0:nr], AF.Exp, [b_pd], [b_dc])
                mt, b_mt = MT[g % 2]
                tt("dve", mt[0:nr, :, 0:nr], dc[0:nr, :, 0:nr], bcast_mid(cbT[0:nr, 0:nr], 4), ALU.mult,
                   [b_dc, b_cbT], [b_mt])
                py, b_py = psY[g // 2]
                for hh in range(4):
                    h = g * 4 + hh
                    cc = (h % 8) * 64
                    mm(py[0:nr, cc:cc + 64], mt[0:nr, hh, 0:nr], xcb[0:nr, h * 64:(h + 1) * 64], True, True,
                       [b_mt, b_xcb], [b_py])
            for half in range(2):
                po, b_po = psO[half]
                py, b_py = psY[half]
                cs_ = slice(half * 512, half * 512 + 512)
                tt("dve", y1[0:nr, cs_].rearrange("p (h q) -> p h q", q=64),
                   po[0:nr, :].rearrange("p (h q) -> p h q", q=64),
                   bcast_last(exA[0:nr, half * 8:half * 8 + 8], 64), ALU.mult, [b_po, b_exA], [b_y1])
                tt("dve", y1[0:nr, cs_], y1[0:nr, cs_], py[0:nr, :], ALU.add, [b_y1, b_py], [b_y1])
            tt("dve", sq[0:nr, :], xs[0:nr, :], Dbc[0:nr, :], ALU.mult, [b_xs, b_Dbc], [b_sq])
            tt("dve", y1[0:nr, :], y1[0:nr, :], sq[0:nr, :], ALU.add, [b_y1, b_sq], [b_y1])
            tt("dve", y1[0:nr, :], y1[0:nr, :], zs[0:nr, :], ALU.mult, [b_y1, b_zs], [b_y1])
            act(sq[0:nr, :], y1[0:nr, :], AF.Square, [b_y1], [b_sq])
            red("dve", ss[0:nr, 0:1], sq[0:nr, :], [b_sq], [b_ss])
            ts("dve", ss[0:nr, 0:1], ss[0:nr, 0:1], 1.0 / 1024, EPS, ALU.mult, ALU.add, [b_ss], [b_ss])
            rsqrt(ss[0:nr, 0:1], b_ss)
            stt("dve", yb[0:nr, :], y1[0:nr, :], ss[0:nr, 0:1], gs[0:nr, :], ALU.mult, ALU.mult,
                [b_y1, b_ss, b_gs], [b_yb])
            pb, b_pb = next_psb()
            for j in range(8):
                tr(pb[:, j * 128:j * 128 + nr], yb[0:nr, j * 128:(j + 1) * 128], identb[0:nr, 0:nr], [b_yb, B_const], [b_pb])
            cp("act", yT[:, :, 0:nr], pb[:, :].rearrange("p (a b) -> p a b", b=128)[:, :, 0:nr], [b_pb], [b_yT])
            dma("sp", mixT_d[8:16, :, r0:r0 + nr].rearrange("k p t -> p k t"), yT[:, :, 0:nr], G("st"), [b_yT], [B_mixS])
            if not sample:
                for g in range(4):
                    po, b_po = psO[g // 2]
                    mm(po[:, (g % 2) * 256:(g % 2) * 256 + 256], btk[0:nr, g * 128:(g + 1) * 128],
                       xte[0:nr, g * 256:(g + 1) * 256], True, True, [b_btk, b_xte], [b_po])
                tt("dve", hst.rearrange("p (h q) -> p h q", q=64), hst.rearrange("p (h q) -> p h q", q=64),
                   bcast_last(cde[:, :], 64), ALU.mult, [b_hst, b_cde], [b_hst])
                for half in range(2):
                    po, b_po = psO[half]
                    tt("dve", hst[:, half * 512:half * 512 + 512], hst[:, half * 512:half * 512 + 512], po[:, :],
                       ALU.add, [b_hst, b_po], [b_hst])
            return xte, b_xte, btk, b_btk, dtA, b_dtA

        for b in range(NSB):
            hn_, b_hn = h0n[b % 2]
            dma("sp", hn_, st_ssm[l, b].rearrange("(j p) n -> p j n", p=128), G("in"), (), [b_hn])
            for half in range(2):
                ps, b_ps = next_ps()
                for jj in range(4):
                    j = half * 4 + jj
                    tr(ps[:, jj * 128:(jj + 1) * 128], hn_[:, j, :], identf, [b_hn, B_const], [b_ps])
                cp("act" if half == 0 else "dve", h0T[:, b, half * 512:half * 512 + 512], ps[:, :], [b_ps], [b_h0T])
        xte_s, b_xte_s, btk_s, b_btk_s, dtA_s, b_dtA_s = chunk(32, 64, C("ltri_s"), C("ones_s"), negmsb_t, True)
        cp("dve", dtx.rearrange("p (h q) -> p h q", q=64), bcast_last(dtA_s[0:64, :], 64), [b_dtA_s], [b_dtx])
        pcd, b_pcd = next_ps()
        for j in range(8):
            mm(pcd[:, j * 16:(j + 1) * 16], dtx[:, j * 128:(j + 1) * 128], C("bmask", 64), True, True,
               [b_dtx, B_const], [b_pcd])
        act(cdT, pcd[:, 0:128], AF.Exp, [b_pcd], [b_cdT])
        for g in range(4):
            tt("dve", Bz.rearrange("p (b n) -> p b n", n=128), bcast_mid(btk_s[0:64, g * 128:(g + 1) * 128], NSB),
               bcast_last(bmb, 128), ALU.mult, [b_btk_s, b_bmb], [b_Bz])
            for b in range(NSB):
                hn_, b_hn = h0n[b % 2]
                if g == 0 or True:
                    dma("sp", hn_[:, 2 * g:2 * g + 2, :],
                        st_ssm[l, b, g * 256:(g + 1) * 256, :].rearrange("(j p) n -> p j n", p=128), G("in"), (), [b_hn])
                for jj in range(2):
                    j = 2 * g + jj
                    ps, b_ps = next_ps()
                    mm(ps[:, 0:128], xte_s[0:64, j * 128:(j + 1) * 128], Bz[:, b * 128:(b + 1) * 128], True, True,
                       [b_xte_s, b_Bz], [b_ps])
                    ns, b_ns = nst[(b * 2 + jj) % 2]
                    stt("dve", ns, hn_[:, j, :], cdT[:, j * 16 + b:j * 16 + b + 1], ps[:, 0:128], ALU.mult, ALU.add,
                        [b_hn, b_cdT, b_ps], [b_ns])
                    dma("sp", nssm_s[l, b, j * 128:(j + 1) * 128, :], ns, G("st"), [b_ns], ())
        for c in range(32):
            chunk(c, 128, C("ltri"), C("ones"), negmb_t, False)
        for half in range(2):
            ps, b_ps = next_ps()
            for jj in range(4):
                j = half * 4 + jj
                tr(ps[:, jj * 128:(jj + 1) * 128], hst[:, j * 128:(j + 1) * 128], identf, [b_hst, B_const], [b_ps])
            fo, b_fo = nst[half]
            fo4, b_fo4 = xs2[half]
            cp("act", fo4[:, 0:512], ps[:, :], [b_ps], [b_fo4])
            dma("sp", nssm_p[l, half * 512:half * 512 + 512, :].rearrange("(j p) n -> p j n", p=128),
                fo4[:, 0:512].rearrange("p (j n) -> p j n", n=128), G("st"), [b_fo4], ())
        K.barrier()

    def dense_tok(l, wsrc, nk, srcT, b_srcT, gamma_next_row, dstT_next, post):
        pass

    def phase_outproj(l):
        areset()
        w, b_w = alloc([128, 16, 2048], BF16)
        load_w(w, b_w, w_out[l], 0, 16, 0, 2048)
        gam, b_gam = load_gamma(norm_ffn_g[l:l + 1, :])
        tmps = [norm_tmp() for _ in range(2)]
        hb = [alloc([128, 2048]) for _ in range(2)]
        mx = [alloc([128, 16, 128], BF16) for _ in range(2)]
        for blk in range(NBLK):
            nr = nr_of(blk)
            r0 = blk * 128
            h, b_h = hb[blk % 2]
            m, b_m = mx[blk % 2]
            dma("sp", h[0:nr, :], h_d[r0:r0 + nr, :], G("in"), [B_h[blk]], [b_h])
            dma("sp", m[:, :, 0:nr], mixT_d[:, :, r0:r0 + nr].rearrange("k p t -> p k t"), G("in"),
                [B_mixA, B_mixS], [b_m])
            for cg in range(4):
                ps, b_ps = next_ps()
                for kc in range(16):
                    mm(ps[0:nr, :], m[:, kc, 0:nr], w[:, kc, cg * 512:(cg + 1) * 512], kc == 0, kc == 15,
                       [b_m, b_w], [b_ps])
                tt("dve", h[0:nr, cg * 512:(cg + 1) * 512], h[0:nr, cg * 512:(cg + 1) * 512], ps[0:nr, :], ALU.add,
                   [b_h, b_ps], [b_h])
            dma("sp", h_d[r0:r0 + nr, :], h[0:nr, :], G("st"), [b_h], [B_h[blk]])
            norm_block(h, b_h, gam, b_gam, hnT_d, B_hnT, blk, tmps[blk % 2])
        K.barrier()

    def phase_ffn(l):
        NG = 4
        GC = 11
        for fg in range(NG):
            areset()
            wg_, b_wg = alloc([128, 16, GC * 128], BF16)
            wu_, b_wu = alloc([128, 16, GC * 128], BF16)
            wd_, b_wd = alloc([128, GC, 2048], BF16)
            c0 = fg * GC * 128
            load_w(wg_, b_wg, w_g[l], 0, 16, c0, GC * 128)
            load_w(wu_, b_wu, w_u[l], 0, 16, c0, GC * 128)
            load_w(wd_, b_wd, w_d[l], fg * GC, GC, 0, 2048)
            last = False
            hx = [alloc([128, 16, 512], BF16) for _ in range(1)]
            at = [alloc([128, GC, 512], BF16) for _ in range(1)]
            sg = [alloc([128, 512]) for _ in range(2)]
            hb = [alloc([128, 2048]) for _ in range(1)]
            for tti in range(9):
                tw = 512 if tti < 8 else 64
                t0 = tti * 512
                hxt, b_hx = hx[0]
                dma("sp", hxt[:, :, 0:tw], hnT_d[:, :, t0:t0 + tw].rearrange("k p t -> p k t"), G("in"),
                    [B_hnT], [b_hx])
                a_, b_a = at[0]
                for j in range(GC):
                    pg, b_pg = next_ps()
                    pu, b_pu = next_ps()
                    for kc in range(16):
                        mm(pg[:, 0:tw], wg_[:, kc, j * 128:(j + 1) * 128], hxt[:, kc, 0:tw], kc == 0, kc == 15,
                           [b_wg, b_hx], [b_pg])
                    for kc in range(16):
                        mm(pu[:, 0:tw], wu_[:, kc, j * 128:(j + 1) * 128], hxt[:, kc, 0:tw], kc == 0, kc == 15,
                           [b_wu, b_hx], [b_pu])
                    s_, b_s = sg[j % 2]
                    act(s_[:, 0:tw], pg[:, 0:tw], AF.Silu, [b_pg], [b_s])
                    tt("dve", a_[:, j, 0:tw], s_[:, 0:tw], pu[:, 0:tw], ALU.mult, [b_s, b_pu], [b_a])
                nsb = 4 if tti < 8 else 1
                for sb in range(nsb):
                    blk = tti * 4 + sb if tti < 8 else 32
                    nr = nr_of(blk)
                    r0 = blk * 128
                    h, b_h = hb[0]
                    dma("sp", h[0:nr, :], h_d[r0:r0 + nr, :], G("in"), [B_h[blk]], [b_h])
                    for cg in range(4):
                        ps, b_ps = next_ps()
                        for j in range(GC):
                            mm(ps[0:nr, :], a_[:, j, sb * 128:sb * 128 + nr], wd_[:, j, cg * 512:(cg + 1) * 512],
                               j == 0, j == GC - 1, [b_a, b_wd], [b_ps])
                        tt("dve", h[0:nr, cg * 512:(cg + 1) * 512], h[0:nr, cg * 512:(cg + 1) * 512], ps[0:nr, :],
                           ALU.add, [b_h, b_ps], [b_h])
                    dma("sp", h_d[r0:r0 + nr, :], h[0:nr, :], G("st"), [b_h], [B_h[blk]])
                    if last:
                        norm_block(h, b_h, gam, b_gam, mixT_d, B_mixA, blk, tmps[0])
            K.barrier()

    def phase_ple(l):
        areset()
        w, b_w = alloc([128, 16, 2048], BF16)
        wp, b_wp = alloc([128, 2, 2048], BF16)
        load_w(w, b_w, w_pg[l], 0, 16, 0, 2048)
        load_w(wp, b_wp, w_pp[l], 0, 2, 0, 2048)
        lastl = l == DEPTH - 1
        gam, b_gam = load_gamma(final_g[0:1, :] if lastl else norm_mix_g[l + 1:l + 2, :])
        tmps = [norm_tmp() for _ in range(1)]
        hb = [alloc([128, 2048]) for _ in range(2)]
        mx = [alloc([128, 16, 128], BF16) for _ in range(2)]
        pin = [alloc([128, 256]) for _ in range(2)]
        pbf = [alloc([128, 256], BF16) for _ in range(2)]
        pT = [alloc([128, 2, 128], BF16) for _ in range(2)]
        sgm = [alloc([128, 512]) for _ in range(2)]
        yo = [alloc([128, 2048]) for _ in range(2)]
        for blk in range(NBLK):
            nr = nr_of(blk)
            r0 = blk * 128
            h, b_h = hb[blk % 2]
            m, b_m = mx[blk % 2]
            dma("sp", h[0:nr, :], h_d[r0:r0 + nr, :], G("in"), [B_h[blk]], [b_h])
            dma("sp", m[:, :, 0:nr], mixT_d[:, :, r0:r0 + nr].rearrange("k p t -> p k t"), G("in"), [B_mixA], [b_m])
            pi, b_pi = pin[blk % 2]
            psrc = p_p[l, r0:r0 + nr, :] if blk < 32 else p_s[l, :, :]
            dma("sp", pi[0:nr, :], psrc, G("in"), (), [b_pi])
            pb_, b_pb_ = pbf[blk % 2]
            cp("dve", pb_[0:nr, :], pi[0:nr, :], [b_pi], [b_pb_])
            pbk, b_pbk = next_psb()
            for j in range(2):
                tr(pbk[:, j * 128:j * 128 + nr], pb_[0:nr, j * 128:(j + 1) * 128], identb[0:nr, 0:nr],
                   [b_pb_, B_const], [b_pbk])
            pt_, b_pt = pT[blk % 2]
            cp("act", pt_[:, :, 0:nr], pbk[:, 0:256].rearrange("p (a b) -> p a b", b=128)[:, :, 0:nr], [b_pbk], [b_pt])
            for cg in range(4):
                pg, b_pg = next_ps()
                pp_, b_pp = next_ps()
                for kc in range(16):
                    mm(pg[0:nr, :], m[:, kc, 0:nr], w[:, kc, cg * 512:(cg + 1) * 512], kc == 0, kc == 15,
                       [b_m, b_w], [b_pg])
                for kc in range(2):
                    mm(pp_[0:nr, :], pt_[:, kc, 0:nr], wp[:, kc, cg * 512:(cg + 1) * 512], kc == 0, kc == 1,
                       [b_pt, b_wp], [b_pp])
                s_, b_s = sgm[cg % 2]
                act(s_[0:nr, :], pg[0:nr, :], AF.Sigmoid, [b_pg], [b_s])
                tt("dve", s_[0:nr, :], s_[0:nr, :], pp_[0:nr, :], ALU.mult, [b_s, b_pp], [b_s])
                tt("dve", h[0:nr, cg * 512:(cg + 1) * 512], h[0:nr, cg * 512:(cg + 1) * 512], s_[0:nr, :], ALU.add,
                   [b_h, b_s], [b_h])
            if not lastl:
                dma("sp", h_d[r0:r0 + nr, :], h[0:nr, :], G("st"), [b_h], [B_h[blk]])
                norm_block(h, b_h, gam, b_gam, hnT_d, B_hnT, blk, tmps[0])
            else:
                sq, b_sq = tmps[0]["sq"]
                ss, b_ss = tmps[0]["ss"]
                y_, b_y = yo[blk % 2]
                act(sq[0:nr, :], h[0:nr, :], AF.Square, [b_h], [b_sq])
                red("dve", ss[0:nr, 0:1], sq[0:nr, :], [b_sq], [b_ss])
                ts("dve", ss[0:nr, 0:1], ss[0:nr, 0:1], 1.0 / D, EPS, ALU.mult, ALU.add, [b_ss], [b_ss])
                rsqrt(ss[0:nr, 0:1], b_ss)
                stt("dve", y_[0:nr, :], h[0:nr, :], ss[0:nr, 0:1], gam[0:nr, :], ALU.mult, ALU.mult,
                    [b_h, b_ss, b_gam], [b_y])
                if blk < 32:
                    dma("sp", y_p[r0:r0 + 128, :], y_[:, :], G("st"), [b_y], ())
                else:
                    dma("sp", y_s[:, :], y_[0:64, :], G("st"), [b_y], ())
        K.barrier()

    import os as _os
    KSTOP = int(_os.environ.get("KSTOP", "99"))
    phases = [lambda l: phase_inproj(l), lambda l: phase_attention(l), lambda l: phase_ssd(l),
              lambda l: phase_outproj(l), lambda l: phase_ffn(l),
              lambda l: phase_norm_only(norm_ple_g[l:l + 1, :], mixT_d, B_mixA), lambda l: phase_ple(l)]
    if KSTOP >= 1:
        phase_norm_only(norm_mix_g[0:1, :])
    cnt = 1
    for l in range(DEPTH):
        K.epoch = l + 1
        for ph in phases:
            cnt += 1
            if KSTOP >= cnt:
                ph(l)
    K.final_wait()
    K.emit()
    es.close()
    return nc, carr


_CACHE = {}


def kernel(**inp):
    if "prog" not in _CACHE:
        _CACHE["prog"] = build_program()
    nc, carr = _CACHE["prog"]
    f = lambda a: np.ascontiguousarray(np.asarray(a, dtype=np.float32))
    x_prompt = f(inp["x_prompt"])
    x_sample = f(inp["x_sample"])
    ck = np.asarray(inp["cache_k"], dtype=np.float32).reshape(DEPTH, 128, L, 512)
    cv = np.asarray(inp["cache_v"], dtype=np.float32).reshape(DEPTH, 128, L, 512)
    ssm = np.asarray(inp["state_ssm"], dtype=np.float32).reshape(DEPTH, 128, 1024, 128)
    conv = np.asarray(inp["state_conv"], dtype=np.float32)
    p_prompt = f(inp["p_prompt"])
    p_sample = f(inp["p_sample"])
    shared = {}
    for k_ in ("norm_mix_g", "w_in", "conv_w", "conv_b", "dt_bias", "a_log", "d_skip", "ssm_norm_g", "w_out",
               "norm_ffn_g", "w_ffn_gate", "w_ffn_up", "w_ffn_down", "norm_ple_g", "w_ple_gate", "w_ple_proj"):
        shared[k_] = f(inp[k_])
    shared["final_norm_g"] = f(inp["final_norm_g"]).reshape(1, D)
    shared["consts"] = carr
    in_maps = []
    for c in range(NCORES):
        s = c % 2
        b0 = c * NSB
        m = dict(shared)
        m["x_p"] = x_prompt[s]
        m["x_s"] = np.ascontiguousarray(x_sample[b0:b0 + NSB].reshape(TS, D))
        m["ck"] = np.ascontiguousarray(ck[:, b0:b0 + NSB])
        m["cv"] = np.ascontiguousarray(cv[:, b0:b0 + NSB])
        m["st_ssm"] = np.ascontiguousarray(ssm[:, b0:b0 + NSB])
        m["st_conv"] = np.ascontiguousarray(conv[:, b0:b0 + NSB].reshape(DEPTH, NSB * 3, 2048))
        m["p_p"] = np.ascontiguousarray(p_prompt[:, s])
        m["p_s"] = np.ascontiguousarray(p_sample[:, b0:b0 + NSB].reshape(DEPTH, TS, PLE))
        in_maps.append(m)
    res = run_bass_kernel_spmd(nc, in_maps, core_ids=list(range(NCORES)))
    R = res.results
    y_prompt = np.stack([R[0]["y_p"], R[1]["y_p"]]).reshape(2, SEQ, D)
    y_sample = np.concatenate([R[c]["y_s"].reshape(NSB, 4, D) for c in range(NCORES)], axis=0)
    nk_p = np.stack([R[0]["nk_p"], R[1]["nk_p"]], axis=1).reshape(DEPTH, 2, 2048, 4, 128)
    nv_p = np.stack([R[0]["nv_p"], R[1]["nv_p"]], axis=1).reshape(DEPTH, 2, 2048, 4, 128)
    nssm_p = np.stack([R[0]["nssm_p"], R[1]["nssm_p"]], axis=1).reshape(DEPTH, 2, 16, 64, 128)
    nconv_p = np.stack([R[0]["nconv_p"], R[1]["nconv_p"]], axis=1).reshape(DEPTH, 2, 3, 2048)
    nk_s = np.concatenate([R[c]["nk_s"] for c in range(NCORES)], axis=1).reshape(DEPTH, 128, L, 4, 128)
    nv_s = np.concatenate([R[c]["nv_s"] for c in range(NCORES)], axis=1).reshape(DEPTH, 128, L, 4, 128)
    nssm_s = np.concatenate([R[c]["nssm_s"] for c in range(NCORES)], axis=1).reshape(DEPTH, 128, 16, 64, 128)
    nconv_s = np.concatenate([R[c]["nconv_s"].reshape(DEPTH, NSB, 3, 2048) for c in range(NCORES)], axis=1)
    outs = (y_prompt, y_sample, nk_p, nv_p, nssm_p, nconv_p, nk_s, nv_s, nssm_s, nconv_s)
    return tuple(np.ascontiguousarray(o, dtype=np.float32) for o in outs)
```

```python
import math
from contextlib import ExitStack

import numpy as np
import concourse.bass as bass
import concourse.mybir as mybir
from concourse.bass_utils import run_bass_kernel_spmd

F32 = mybir.dt.float32
BF16 = mybir.dt.bfloat16
I32 = mybir.dt.int32
AF = mybir.ActivationFunctionType
ALU = mybir.AluOpType
AX = mybir.AxisListType

NCORES = 8
D = 2048
SEQ = 4096
NSB = 16
TS = 64
T = SEQ + TS
NBLK = 33
DEPTH = 2
L = 2048
DFF = 5632
PLE = 256
INP = 5136
EPS = 1e-6
SCALE = 128 ** -0.5
NEG = -30000.0


def nr_of(b):
    return 128 if b < 32 else 64


class Buf:
    __slots__ = ("w", "r", "name", "dram", "dg")

    def __init__(self, name="", dram=False):
        self.w = None
        self.r = {}
        self.name = name
        self.dram = dram
        self.dg = None


class DG:
    def __init__(self, sem, bulk=False):
        self.sem = sem
        self.cnt = 0
        self.bulk = bulk


class Op:
    __slots__ = ("eng", "fn", "deps", "dg", "has_dep", "sig", "epoch")


class Sched:
    ENG = ("pe", "act", "dve", "pool", "sp")

    def __init__(self, nc, es):
        self.nc = nc
        self.es = es
        self.ops = {e: [] for e in self.ENG}
        self.epoch = 0
        self.dgs = []
        self.nsem = 0
        self.dgpool = []
        self.pool_idx = 0

    def phase_reset(self):
        self.pool_idx = 0

    def buf_dg(self, buf):
        if buf.dg is None:
            if self.pool_idx >= len(self.dgpool):
                self.dgpool.append(self.dgroup("p%d" % len(self.dgpool)))
            buf.dg = self.dgpool[self.pool_idx]
            self.pool_idx += 1
        return buf.dg

    def newsem(self, name):
        self.nsem += 1
        return self.es.enter_context(self.nc.semaphore(name))

    def dgroup(self, name, bulk=False):
        g = DG(self.newsem("dg_" + name), bulk)
        self.dgs.append(g)
        return g

    def add(self, eng, fn, reads=(), writes=(), dg=None):
        op = Op()
        op.eng = eng
        op.fn = fn
        op.dg = dg
        op.has_dep = False
        op.sig = None
        op.epoch = self.epoch
        deps = []
        for b in reads:
            if b.w is not None:
                deps.append(b.w)
        for b in writes:
            if b.w is not None:
                deps.append(b.w)
            deps.extend(b.r.values())
        if dg is not None:
            dg.cnt += 16
            me = (dg, dg.cnt)
            key = dg
        else:
            me = op
            key = eng
        clean = []
        for d in deps:
            if isinstance(d, Op):
                if eng == "pe" and d.eng == "pe":
                    continue
                if d is op:
                    continue
                d.has_dep = True
            clean.append(d)
        op.deps = clean
        for b in reads:
            b.r[key] = me
        for b in writes:
            b.w = me
            b.r = {}
        self.ops[eng].append(op)
        return op

    def barrier(self):
        lasts = []
        for e in self.ENG:
            for o in reversed(self.ops[e]):
                if o.dg is None and o.fn is not None:
                    lasts.append(o)
                    break
        dgl = [(g, g.cnt) for g in self.dgs if g.cnt > 0 and not g.bulk]
        for e in self.ENG:
            op = Op()
            op.eng = e
            op.fn = None
            op.dg = None
            op.has_dep = False
            op.sig = None
            op.epoch = self.epoch
            op.deps = []
            for o in lasts:
                if o.eng != e:
                    o.has_dep = True
                    op.deps.append(o)
            op.deps.extend(dgl)
            self.ops[e].append(op)

    def final_wait(self):
        dgl = [(g, g.cnt) for g in self.dgs if g.cnt > 0]
        op = Op()
        op.eng = "sp"
        op.fn = None
        op.dg = None
        op.has_dep = False
        op.sig = None
        op.epoch = self.epoch
        op.deps = list(dgl)
        for e in self.ENG:
            for o in reversed(self.ops[e]):
                if o.dg is None and o.fn is not None:
                    if e != "sp":
                        o.has_dep = True
                        op.deps.append(o)
                    break
        self.ops["sp"].append(op)

    def emit(self):
        nc = self.nc
        sems = {}
        for e in self.ENG:
            cnt = {}
            for o in self.ops[e]:
                if o.dg is None and o.has_dep and o.fn is not None:
                    k = (e, o.epoch)
                    cnt[k] = cnt.get(k, 0) + 1
                    o.sig = cnt[k]
                    if k not in sems:
                        sems[k] = self.newsem("e_%s_%d" % (e, o.epoch))
        block = self.es.enter_context(nc.Block())

        def run(eng_name):
            def body(e):
                waited = {}
                for o in self.ops[eng_name]:
                    for d in o.deps:
                        if isinstance(d, Op):
                            key = (d.eng, d.epoch)
                            sem = sems[key]
                            v = d.sig
                        else:
                            key = d[0]
                            sem = d[0].sem
                            v = d[1]
                        if waited.get(key, 0) < v:
                            e.wait_ge(sem, v)
                            waited[key] = v
                    if o.fn is None:
                        continue
                    ins = o.fn(e)
                    if o.dg is not None:
                        ins.then_inc(o.dg.sem, 16)
                    elif o.has_dep:
                        ins.then_inc(sems[(eng_name, o.epoch)], 1)
            return body

        block.tensor(run("pe"))
        block.scalar(run("act"))
        block.vector(run("dve"))
        block.gpsimd(run("pool"))
        block.sync(run("sp"))


def make_consts():
    c = {}
    ident = np.eye(128, dtype=np.float32)
    c["ident"] = ident
    k = np.arange(128)[:, None]
    q = np.arange(128)[None, :]
    mcur = (q >= k).astype(np.float32)
    mprev = (k >= q).astype(np.float32)
    c["mcur"] = np.concatenate([mcur, mcur], axis=1)
    c["mprev"] = np.concatenate([mprev, mprev], axis=1)
    ltri = (k <= q).astype(np.float32)
    c["ltri"] = ltri
    c["negm"] = np.where(k <= q, 0.0, NEG).astype(np.float32)
    c["ones"] = np.ones((128, 128), np.float32)
    bs = np.arange(64)[:, None] // 4
    bt = np.arange(64)[None, :] // 4
    same = bs == bt
    s_ = np.arange(64)[:, None]
    t_ = np.arange(64)[None, :]
    lbd = np.zeros((128, 128), np.float32)
    lbd[:64, :64] = (same & (s_ <= t_)).astype(np.float32)
    c["ltri_s"] = lbd
    nbd = np.full((128, 128), NEG, np.float32)
    nbd[:64, :64] = np.where(same & (s_ <= t_), 0.0, NEG)
    c["negm_s"] = nbd
    obd = np.zeros((128, 128), np.float32)
    obd[:64, :64] = same.astype(np.float32)
    c["ones_s"] = obd
    sel = np.zeros((128, 16, 128), np.float32)
    for h in range(16):
        sel[h, h, :] = 1.0
    c["sel"] = sel.reshape(128, 2048)
    bm = np.zeros((128, 16), np.float32)
    bm[np.arange(64), np.arange(64) // 4] = 1.0
    c["bmask"] = bm
    msk = np.zeros((128, 7, 4), np.float32)
    for j in range(7):
        for p in range(128):
            if j < 4:
                row = 1536 + 128 * j + p
            else:
                r = p // 32
                m = 32 * (j - 4) + (p % 32)
                row = 16 * m + r
            for t in range(4):
                diff = 2048 + t - row
                mult = 0
                if 0 <= diff <= 128:
                    mult += 1
                if diff % 4 == 0 and 0 <= diff <= 512:
                    mult += 1
                if diff % 16 == 0 and 0 <= diff <= 2048:
                    mult += 1
                msk[p, j, t] = mult
    mexp = np.broadcast_to(msk[:, :, None, None, :], (128, 7, 4, 2, 4)).reshape(128, 7 * 32)
    c["smask"] = np.ascontiguousarray(mexp)
    mn = np.zeros((128, 128), np.float32)
    for kk in range(64):
        b1, t1 = kk // 4, kk % 4
        for b in range(16):
            for r in range(2):
                for t in range(4):
                    if b == b1 and t1 <= t:
                        mn[kk, b * 8 + r * 4 + t] = 3.0 if t1 == t else 1.0
    c["nmask"] = mn
    pos = np.zeros((128, NBLK), np.float32)
    for b in range(32):
        pos[:, b] = 128 * b + np.arange(128)
    pos[:64, 32] = 2048 + (np.arange(64) % 4)
    c["pos"] = pos
    c["jidx"] = np.broadcast_to(np.arange(64, dtype=np.float32)[None, :], (128, 64)).copy()
    return c


CONST_ORDER = ["ident", "mcur", "mprev", "ltri", "negm", "ones", "ltri_s", "negm_s", "ones_s",
               "sel", "bmask", "smask", "nmask", "pos", "jidx"]


def pack_consts():
    c = make_consts()
    offs = {}
    cols = 0
    for n in CONST_ORDER:
        offs[n] = (cols, c[n].shape[1])
        cols += c[n].shape[1]
    arr = np.zeros((128, cols), np.float32)
    for n in CONST_ORDER:
        o, w = offs[n]
        arr[:, o:o + w] = c[n]
    return arr, offs


def build_program():
    carr, coffs = pack_consts()
    NCC = carr.shape[1]
    nc = bass.Bass("TRN2", target_bir_lowering=False)
    es = ExitStack()
    K = Sched(nc, es)

    def din(name, shape, dt=F32):
        return nc.dram_tensor(name, list(shape), dt, kind="ExternalInput").ap()

    def dout(name, shape, dt=F32):
        return nc.dram_tensor(name, list(shape), dt, kind="ExternalOutput").ap()

    def dscr(name, shape, dt):
        return nc.dram_tensor(name, list(shape), dt).ap()

    x_p = din("x_p", [SEQ, D])
    x_s = din("x_s", [TS, D])
    ck = din("ck", [DEPTH, NSB, L, 512])
    cv = din("cv", [DEPTH, NSB, L, 512])
    st_ssm = din("st_ssm", [DEPTH, NSB, 1024, 128])
    st_conv = din("st_conv", [DEPTH, NSB * 3, 2048])
    p_p = din("p_p", [DEPTH, SEQ, PLE])
    p_s = din("p_s", [DEPTH, TS, PLE])
    norm_mix_g = din("norm_mix_g", [DEPTH, D])
    w_in = din("w_in", [DEPTH, D, INP])
    conv_w = din("conv_w", [DEPTH, 4, 2048])
    conv_b = din("conv_b", [DEPTH, 2048])
    dt_bias = din("dt_bias", [DEPTH, 16])
    a_log = din("a_log", [DEPTH, 16])
    d_skip = din("d_skip", [DEPTH, 16])
    ssm_norm_g = din("ssm_norm_g", [DEPTH, 1024])
    w_out = din("w_out", [DEPTH, D, D])
    norm_ffn_g = din("norm_ffn_g", [DEPTH, D])
    w_g = din("w_ffn_gate", [DEPTH, D, DFF])
    w_u = din("w_ffn_up", [DEPTH, D, DFF])
    w_d = din("w_ffn_down", [DEPTH, DFF, D])
    norm_ple_g = din("norm_ple_g", [DEPTH, D])
    w_pg = din("w_ple_gate", [DEPTH, D, D])
    w_pp = din("w_ple_proj", [DEPTH, PLE, D])
    final_g = din("final_norm_g", [1, D])
    cst = din("consts", [128, NCC])
    y_p = dout("y_p", [SEQ, D])
    y_s = dout("y_s", [TS, D])
    nk_p = dout("nk_p", [DEPTH, 2048, 512])
    nv_p = dout("nv_p", [DEPTH, 2048, 512])
    nssm_p = dout("nssm_p", [DEPTH, 1024, 128])
    nconv_p = dout("nconv_p", [DEPTH, 3, 2048])
    nk_s = dout("nk_s", [DEPTH, NSB, L, 512])
    nv_s = dout("nv_s", [DEPTH, NSB, L, 512])
    nssm_s = dout("nssm_s", [DEPTH, NSB, 1024, 128])
    nconv_s = dout("nconv_s", [DEPTH, NSB * 3, 2048])
    h_d = dscr("h_d", [T, D], F32)
    hnT_d = dscr("hnT_d", [16, 128, T], BF16)
    qT_d = dscr("qT_d", [8, 128, T], BF16)
    kT_d = dscr("kT_d", [4, 128, T], BF16)
    v_d = dscr("v_d", [T, 512], BF16)
    zs_d = dscr("zs_d", [T, 1024], F32)
    dt_d = dscr("dt_d", [T, 16], F32)
    xs_d = dscr("xs_d", [T, 1024], F32)
    bt_d = dscr("bt_d", [T, 512], BF16)
    BT_d = dscr("BT_d", [4, 128, T], BF16)
    CT_d = dscr("CT_d", [4, 128, T], BF16)
    mixT_d = dscr("mixT_d", [16, 128, T], BF16)
    cs_d = dscr("cs_d", [T, 128], F32)

    B_h = [Buf("h%d" % b, True) for b in range(NBLK)]
    B_hnT = Buf("hnT", True)
    B_qT, B_kT, B_v, B_zs, B_dt, B_xs, B_bt, B_BT, B_CT = (Buf("d", True) for _ in range(9))
    B_mixA, B_mixS, B_cs = Buf("d", True), Buf("d", True), Buf("d", True)

    ARENA = 48600
    cs_t = es.enter_context(nc.sbuf_tensor("cst", [128, NCC], F32))
    arena = es.enter_context(nc.sbuf_tensor("arena", [128, ARENA], F32))
    identb_t = es.enter_context(nc.sbuf_tensor("identb", [128, 128], BF16))
    onesb_t = es.enter_context(nc.sbuf_tensor("onesb", [128, 128], BF16))
    negmb_t = es.enter_context(nc.sbuf_tensor("negmb", [128, 128], BF16))
    negmsb_t = es.enter_context(nc.sbuf_tensor("negmsb", [128, 128], BF16))
    mcurb_t = es.enter_context(nc.sbuf_tensor("mcurb", [128, 256], BF16))
    mprevb_t = es.enter_context(nc.sbuf_tensor("mprevb", [128, 256], BF16))
    B_const = Buf("const")

    def C(name, rows=128):
        o, w = coffs[name]
        return cs_t[0:rows, o:o + w]

    PS = [es.enter_context(nc.psum_tensor("ps%d" % i, [128, 512], F32)) for i in range(6)]
    PSB = [es.enter_context(nc.psum_tensor("psb%d" % i, [128, 1024], BF16)) for i in range(2)]
    B_PS = [Buf("ps%d" % i) for i in range(6)]
    B_PSB = [Buf("psb%d" % i) for i in range(2)]
    ps_rr = [0]
    psb_rr = [0]

    def next_ps():
        i = ps_rr[0] % 6
        ps_rr[0] += 1
        return PS[i], B_PS[i]

    def next_psb():
        i = psb_rr[0] % 2
        psb_rr[0] += 1
        return PSB[i], B_PSB[i]

    apos = [0]

    phase0 = [True]

    def areset():
        apos[0] = 0
        if not phase0[0]:
            K.phase_reset()

    def alloc(shape, dt=F32):
        n = 1
        for s in shape[1:]:
            n *= s
        words = n if dt == F32 or dt == I32 else (n + 1) // 2
        o = apos[0]
        apos[0] += words
        assert apos[0] <= ARENA, ("arena overflow", apos[0])
        v = arena[0:shape[0], o:o + words]
        if dt != F32:
            v = v.bitcast(dt)
            v = v[:, 0:n]
        if len(shape) == 3:
            v = v.rearrange("p (a b) -> p a b", b=shape[2])
        elif len(shape) == 4:
            v = v.rearrange("p (a b c) -> p a b c", b=shape[2], c=shape[3])
        return v, Buf()

    def mm(out, lhsT, rhs, start, stop, R, W):
        return K.add("pe", lambda e: e.matmul(out, lhsT, rhs, start=start, stop=stop), R, W)

    def tr(out, in_, ident, R, W):
        return K.add("pe", lambda e: e.transpose(out, in_, ident), R, W)

    def act(out, in_, func, R, W, bias=None, scale=None):
        def f(e):
            kw = {}
            if bias is not None:
                kw["bias"] = bias
            if scale is not None:
                kw["scale"] = scale
            return e.activation(out, in_, func, **kw)
        return K.add("act", f, R, W)

    def cp(eng, out, in_, R, W):
        if eng == "act":
            return act(out, in_, AF.Copy, R, W)
        return K.add(eng, lambda e: e.tensor_copy(out, in_), R, W)

    def tt(eng, out, a, b, op, R, W):
        return K.add(eng, lambda e: e.tensor_tensor(out, a, b, op), R, W)

    def ts(eng, out, a, s1, s2, op0, op1, R, W):
        if s2 is None:
            return K.add(eng, lambda e: e.tensor_scalar(out, a, s1, None, op0), R, W)
        return K.add(eng, lambda e: e.tensor_scalar(out, a, s1, s2, op0, op1), R, W)

    def stt(eng, out, a, s, b, op0, op1, R, W):
        return K.add(eng, lambda e: e.scalar_tensor_tensor(out, a, s, b, op0, op1), R, W)

    def rsqrt(col, b_col):
        act(col, col, AF.Sqrt, [b_col], [b_col])
        K.add("dve", lambda e: e.reciprocal(col, col), [b_col], [b_col])

    def red(eng, out, in_, R, W):
        return K.add(eng, lambda e: e.reduce_sum(out, in_, axis=AX.X), R, W)

    def mset(eng, out, val, W):
        return K.add(eng, lambda e: e.memset(out, val), (), W)

    def dma(q, out, in_, dg, R, W, slow=False):
        if dg in g_st:
            q = "pool"
        sb = None
        for b_ in W:
            if not b_.dram:
                sb = b_
                break
        if sb is None:
            for b_ in R:
                if not b_.dram:
                    sb = b_
                    break
        if sb is not None:
            dg = K.buf_dg(sb)
        if slow:
            return K.add(q, lambda e: e.dma_start(out=out, in_=in_, allow_slow_non_contiguous=True), R, W, dg=dg)
        return K.add(q, lambda e: e.dma_start(out=out, in_=in_), R, W, dg=dg)

    def bc_mid(ap2d, n_mid, n_in):
        raise NotImplementedError

    def bcast_last(ap, n):
        a = [list(x) for x in ap.ap]
        return bass.AP(ap.tensor, ap.offset, a + [[0, n]])

    def bcast_mid(ap, n):
        a = [list(x) for x in ap.ap]
        return bass.AP(ap.tensor, ap.offset, [a[0], [0, n]] + a[1:])

    def rowbc(dram_row_ap, ncols):
        return bass.AP(dram_row_ap.tensor, dram_row_ap.offset, [[0, 128], [1, ncols]])

    g_in = [K.dgroup("in%d" % i) for i in range(4)]
    g_w = [K.dgroup("w%d" % i) for i in range(4)]
    g_st = [K.dgroup("st%d" % i) for i in range(4)]
    g_bulk = K.dgroup("bulk", bulk=True)
    g_misc = K.dgroup("misc")
    rr = {"in": 0, "w": 0, "st": 0}

    def G(kind):
        lst = {"in": g_in, "w": g_w, "st": g_st}[kind]
        i = rr[kind] % len(lst)
        rr[kind] += 1
        return lst[i]

    dma("sp", cs_t[:, :], cst, g_misc, (), [B_const])
    cp("dve", identb_t[:, :], C("ident"), [B_const], [B_const])
    cp("dve", onesb_t[:, :], C("ones"), [B_const], [B_const])
    cp("dve", negmb_t[:, :], C("negm"), [B_const], [B_const])
    cp("dve", negmsb_t[:, :], C("negm_s"), [B_const], [B_const])
    cp("dve", mcurb_t[:, :], C("mcur"), [B_const], [B_const])
    cp("dve", mprevb_t[:, :], C("mprev"), [B_const], [B_const])
    identb = identb_t[:, :]
    identf = C("ident")

    for l in range(DEPTH):
        for b in range(NSB):
            for (src, dst) in ((ck, nk_s), (cv, nv_s)):
                for part in range(4):
                    r0 = 4 + part * 511
                    dma("sp", dst[l, b, r0 - 4:r0 - 4 + 511, :], src[l, b, r0:r0 + 511, :], g_bulk, (), ())
    for i in range(8):
        dma("sp", h_d[i * 512:(i + 1) * 512, :], x_p[i * 512:(i + 1) * 512, :], g_misc, (), B_h[4 * i:4 * i + 4])
    dma("sp", h_d[SEQ:T, :], x_s[:, :], g_misc, (), [B_h[32]])

    areset()
    invf, b_invf = alloc([128, 64])
    act(invf, C("jidx"), AF.Exp, [B_const], [b_invf], scale=-math.log(10000.0) / 64.0)
    TWO_PI = 2.0 * math.pi
    for b in range(NBLK):
        nr = nr_of(b)
        ang, b_ang = alloc([128, 128])
        o, _ = coffs["pos"]
        posc = cs_t[:, o + b:o + b + 1]
        kf, b_kf = alloc([128, 128])
        ki, b_ki = alloc([128, 128], I32)
        ts("dve", ang[:, 64:128], invf, posc, None, ALU.mult, None, [b_invf, B_const], [b_ang])
        ts("dve", ang[:, 0:64], invf, posc, 0.5 * math.pi, ALU.mult, ALU.add, [b_invf, B_const], [b_ang])
        ts("dve", kf, ang, 1.0 / TWO_PI, None, ALU.mult, None, [b_ang], [b_kf])
        cp("dve", ki, kf, [b_kf], [b_ki])
        cp("dve", kf, ki, [b_ki], [b_kf])
        stt("dve", ang, kf, -TWO_PI, ang, ALU.mult, ALU.add, [b_kf, b_ang], [b_ang])
        ts("dve", ang, ang, math.pi, -math.pi, ALU.min, ALU.max, [b_ang], [b_ang])
        act(ang, ang, AF.Sin, [b_ang], [b_ang])
        r0 = b * 128
        dma("sp", cs_d[r0:r0 + nr, :], ang[0:nr, :], g_misc, [b_ang], [B_cs])
    K.barrier()
    phase0[0] = False

    def norm_block(hblk, b_hblk, gam, b_gam, dstT, b_dst, blk, tmp):
        nr = nr_of(blk)
        sq, b_sq = tmp["sq"]
        ss, b_ss = tmp["ss"]
        hn, b_hn = tmp["hn"]
        hT, b_hT = tmp["hT"]
        act(sq[0:nr, :], hblk[0:nr, :], AF.Square, [b_hblk], [b_sq])
        red("dve", ss[0:nr, 0:1], sq[0:nr, :], [b_sq], [b_ss])
        ts("dve", ss[0:nr, 0:1], ss[0:nr, 0:1], 1.0 / D, EPS, ALU.mult, ALU.add, [b_ss], [b_ss])
        rsqrt(ss[0:nr, 0:1], b_ss)
        stt("dve", hn[0:nr, :], hblk[0:nr, :], ss[0:nr, 0:1], gam[0:nr, :], ALU.mult, ALU.mult,
            [b_hblk, b_ss, b_gam], [b_hn])
        for half in range(2):
            pb, b_pb = next_psb()
            for j in range(8):
                kc = half * 8 + j
                tr(pb[:, j * 128:j * 128 + nr], hn[0:nr, kc * 128:(kc + 1) * 128], identb[0:nr, 0:nr],
                   [b_hn, B_const], [b_pb])
            cp("act" if half == 0 else "dve", hT[:, half * 8:(half + 1) * 8, 0:nr],
               pb[:, :].rearrange("p (a b) -> p a b", b=128)[:, :, 0:nr], [b_pb], [b_hT])
        r0 = blk * 128
        dma("sp", dstT[:, :, r0:r0 + nr].rearrange("k p t -> p k t"), hT[:, :, 0:nr], G("st"), [b_hT], [b_dst])

    def norm_tmp():
        return {"sq": alloc([128, 2048]), "ss": alloc([128, 4]), "hn": alloc([128, 2048], BF16),
                "hT": alloc([128, 16, 128], BF16)}

    def load_gamma(row_ap):
        g, b_g = alloc([128, 2048])
        dma("sp", g, rowbc(row_ap, 2048), g_misc, (), [b_g])
        return g, b_g

    def phase_norm_only(gamma_row, dstT=None, b_dst=None):
        if dstT is None:
            dstT, b_dst = hnT_d, B_hnT
        areset()
        gam, b_gam = load_gamma(gamma_row)
        tmps = [norm_tmp() for _ in range(2)]
        hb = [alloc([128, 2048]) for _ in range(2)]
        for blk in range(NBLK):
            nr = nr_of(blk)
            h, b_h = hb[blk % 2]
            dma("sp", h[0:nr, :], h_d[blk * 128:blk * 128 + nr, :], G("in"), [B_h[blk]], [b_h])
            norm_block(h, b_h, gam, b_gam, dstT, b_dst, blk, tmps[blk % 2])
        K.barrier()

    def load_w(dst, b_dst, wsrc, k0, nk, c0, ncols):
        for kk in range(0, nk, 4):
            n = min(4, nk - kk)
            src = wsrc[(k0 + kk) * 128:(k0 + kk + n) * 128, c0:c0 + ncols].rearrange("(k p) c -> p k c", p=128)
            dma("pool", dst[:, kk:kk + n, 0:ncols], src, G("w"), (), [b_dst])

    def phase_inproj(l):
        areset()
        wb = [alloc([128, 16, 512], BF16) for _ in range(2)]
        hx = [alloc([128, 16, 512], BF16) for _ in range(2)]
        ev = [alloc([128, 512]) for _ in range(2)]
        evb = [alloc([128, 512], BF16) for _ in range(2)]
        cst_t = [alloc([128, 128]) for _ in range(2)]
        rot = [alloc([128, 512]) for _ in range(2)]
        rtmp = [alloc([128, 4, 64]) for _ in range(4)]
        hT4 = [alloc([128, 4, 128], BF16) for _ in range(2)]
        dtb, b_dtb = alloc([128, 16])
        dtw = [alloc([128, 16]) for _ in range(2)]
        pad = [alloc([128, 3 + 512]) for _ in range(2)]
        carry = alloc([128, 16, 3])
        acc = [alloc([128, 512]) for _ in range(2)]
        cw, b_cw = alloc([128, 16, 4])
        cb, b_cb = alloc([128, 16])
        sct, b_sct = alloc([128, 16, 48])
        scin, b_scin = alloc([48, 2048])
        pads = [alloc([128, 16, 7]) for _ in range(2)]
        accs = [alloc([128, 16, 4]) for _ in range(2)]
        xsf = [alloc([128, 512]) for _ in range(2)]
        xsb = [alloc([128, 512], BF16) for _ in range(2)]
        xtk = [alloc([128, 4, 128]) for _ in range(2)]
        btk = [alloc([128, 4, 128], BF16) for _ in range(2)]
        ncs, b_ncs = alloc([48, 2048])
        ncp, b_ncp = alloc([128, 16, 3])

        dma("sp", dtb, rowbc(dt_bias[l:l + 1, :], 16), g_misc, (), [b_dtb])
        cwin, b_cwin = alloc([8, 2048])
        dma("sp", cwin[0:4, :], conv_w[l], g_misc, (), [b_cwin])
        dma("sp", cwin[4:5, :], conv_b[l:l + 1, :], g_misc, (), [b_cwin])
        for kc in range(16):
            ps, b_ps = next_ps()
            tr(ps[:, 0:5], cwin[0:5, kc * 128:(kc + 1) * 128], identf[0:5, 0:5], [b_cwin, B_const], [b_ps])
            cp("act", cw[:, kc, :], ps[:, 0:4], [b_ps], [b_cw])
            cp("act", cb[:, kc:kc + 1], ps[:, 4:5], [b_ps], [b_cb])
        dma("sp", scin, st_conv[l], g_misc, (), [b_scin])
        for kc in range(16):
            ps, b_ps = next_ps()
            tr(ps[:, 0:48], scin[:, kc * 128:(kc + 1) * 128], identf[0:48, 0:48], [b_scin, B_const], [b_ps])
            cp("act", sct[:, kc, :], ps[:, 0:48], [b_ps], [b_sct])
        mset("dve", carry[0], 0.0, [carry[1]])

        groups = [("q", 0, 512), ("q", 512, 512), ("k", 1024, 512), ("v", 1536, 512),
                  ("z", 2048, 512), ("z", 2560, 512),
                  ("x", 3072, 512), ("x", 3584, 512), ("x", 4096, 512), ("x", 4608, 512),
                  ("dt", 5120, 16)]
        wi = w_in[l]
        for gi, (kind, c0, ncw) in enumerate(groups):
            w, b_w = wb[gi % 2]
            load_w(w, b_w, wi, 0, 16, c0, ncw)
            for tti in range(9):
                tw = 512 if tti < 8 else 64
                t0 = tti * 512
                hxt, b_hx = hx[tti % 2]
                dma("sp", hxt[:, :, 0:tw], hnT_d[:, :, t0:t0 + tw].rearrange("k p t -> p k t"), G("in"),
                    [B_hnT], [b_hx])
                if kind == "x":
                    for sc in range(4):
                        kcx = (c0 - 3072) // 128 + sc
                        ps, b_ps = next_ps()
                        for kc in range(16):
                            mm(ps[:, 0:tw], w[:, kc, sc * 128:(sc + 1) * 128], hxt[:, kc, 0:tw],
                               kc == 0, kc == 15, [b_w, b_hx], [b_ps])
                        xf, b_xf = xsf[sc % 2]
                        if tti < 8:
                            pd, b_pd = pad[sc % 2]
                            ac, b_ac = acc[sc % 2]
                            cp("dve", pd[:, 0:3], carry[0][:, kcx, :], [carry[1]], [b_pd])
                            cp("act", pd[:, 3:3 + 512], ps[:, :], [b_ps], [b_pd])
                            cp("dve", carry[0][:, kcx, :], pd[:, 512:515], [b_pd], [carry[1]])
                            if tti == 7:
                                cp("dve", ncp[:, kcx, :], pd[:, 512:515], [b_pd], [b_ncp])
                            ts("dve", ac, pd[:, 0:512], cw[:, kcx, 0:1], None, ALU.mult, None, [b_pd, b_cw], [b_ac])
                            for wv in range(1, 4):
                                stt("dve", ac, pd[:, wv:wv + 512], cw[:, kcx, wv:wv + 1], ac, ALU.mult, ALU.add,
                                    [b_pd, b_cw, b_ac], [b_ac])
                            act(xf, ac, AF.Silu, [b_ac, b_cb], [b_xf], bias=cb[:, kcx:kcx + 1])
                        else:
                            pd, b_pd = pads[sc % 2]
                            ac, b_ac = accs[sc % 2]
                            cp("dve", pd[:, :, 0:3], sct[:, kcx, :].rearrange("p (b t) -> p b t", t=3),
                               [b_sct], [b_pd])
                            cp("act", pd[:, :, 3:7], ps[:, 0:64].rearrange("p (b t) -> p b t", t=4), [b_ps], [b_pd])
                            ts("dve", ac, pd[:, :, 0:4], cw[:, kcx, 0:1], None, ALU.mult, None, [b_pd, b_cw], [b_ac])
                            for wv in range(1, 4):
                                stt("dve", ac, pd[:, :, wv:wv + 4], cw[:, kcx, wv:wv + 1], ac, ALU.mult, ALU.add,
                                    [b_pd, b_cw, b_ac], [b_ac])
                            act(xf[:, 0:64], ac.rearrange("p b t -> p (b t)"), AF.Silu, [b_ac, b_cb], [b_xf],
                                bias=cb[:, kcx:kcx + 1])
                            nct, b_nct = xtk[sc % 2]
                            cp("dve", nct[:, 0, 0:48].rearrange("p (b t) -> p b t", t=3), pd[:, :, 4:7], [b_pd], [b_nct])
                            ps2, b_ps2 = next_ps()
                            tr(ps2[0:48, 0:128], nct[:, 0, 0:48], identf, [b_nct, B_const], [b_ps2])
                            cp("act", ncs[:, kcx * 128:(kcx + 1) * 128], ps2[0:48, 0:128], [b_ps2], [b_ncs])
                        if kcx < 8:
                            xt, b_xt = xtk[sc % 2]
                            nsb = 4 if tti < 8 else 1
                            ps3, b_ps3 = next_ps()
                            for sb in range(nsb):
                                nrr = 128 if tti < 8 else 64
                                tr(ps3[0:nrr, sb * 128:(sb + 1) * 128], xf[:, sb * 128:sb * 128 + nrr], identf,
                                   [b_xf, B_const], [b_ps3])
                            nrr = 128 if tti < 8 else 64
                            cp("act", xt[0:nrr, 0:nsb, :], ps3[0:nrr, 0:nsb * 128].rearrange("p (a b) -> p a b", b=128),
                               [b_ps3], [b_xt])
                            if tti < 8:
                                dma("sp", xs_d[t0:t0 + 512, kcx * 128:(kcx + 1) * 128].rearrange("(a p) c -> p a c", p=128),
                                    xt[:, :, :], G("st"), [b_xt], [B_xs])
                            else:
                                dma("sp", xs_d[t0:t0 + 64, kcx * 128:(kcx + 1) * 128], xt[0:64, 0, :], G("st"),
                                    [b_xt], [B_xs])
                        else:
                            xb_, b_xb = xsb[sc % 2]
                            cp("dve", xb_[:, 0:tw], xf[:, 0:tw], [b_xf], [b_xb])
                            isB = kcx < 12
                            gq = kcx - 8 if isB else kcx - 12
                            dst = BT_d if isB else CT_d
                            dma("sp", dst[gq, :, t0:t0 + tw], xb_[:, 0:tw], G("st"), [b_xb], [B_BT if isB else B_CT])
                            if isB:
                                bk, b_bk = btk[sc % 2]
                                nsb = 4 if tti < 8 else 1
                                nrr = 128 if tti < 8 else 64
                                pb, b_pb = next_psb()
                                for sb in range(nsb):
                                    tr(pb[0:nrr, sb * 128:(sb + 1) * 128], xb_[:, sb * 128:sb * 128 + nrr], identb,
                                       [b_xb, B_const], [b_pb])
                                cp("act", bk[0:nrr, 0:nsb, :], pb[0:nrr, 0:nsb * 128].rearrange("p (a b) -> p a b", b=128),
                                   [b_pb], [b_bk])
                                if tti < 8:
                                    dma("sp", bt_d[t0:t0 + 512, gq * 128:(gq + 1) * 128].rearrange("(a p) c -> p a c", p=128),
                                        bk[:, :, :], G("st"), [b_bk], [B_bt])
                                else:
                                    dma("sp", bt_d[t0:t0 + 64, gq * 128:(gq + 1) * 128], bk[0:64, 0, :], G("st"),
                                        [b_bk], [B_bt])
                    continue
                nsb = 4 if tti < 8 else 1
                for sb in range(nsb):
                    blk = tti * 4 + sb if tti < 8 else 32
                    nr = nr_of(blk)
                    r0 = blk * 128
                    ps, b_ps = next_ps()
                    for kc in range(16):
                        mm(ps[0:nr, 0:ncw], hxt[:, kc, sb * 128:sb * 128 + nr], w[:, kc, 0:ncw],
                           kc == 0, kc == 15, [b_w, b_hx], [b_ps])
                    if kind in ("q", "k"):
                        cs_, b_cs_ = cst_t[blk % 2]
                        dma("sp", cs_[0:nr, :], cs_d[r0:r0 + nr, :], G("in"), [B_cs], [b_cs_])
                        ro, b_ro = rot[blk % 2]
                        pv = ps[0:nr, :].rearrange("p (h d) -> p h d", d=128)
                        rv = ro[0:nr, :].rearrange("p (h d) -> p h d", d=128)
                        cosb = bcast_mid(cs_[0:nr, 0:64], 4)
                        sinb = bcast_mid(cs_[0:nr, 64:128], 4)
                        t1, b_t1 = rtmp[0]
                        t2, b_t2 = rtmp[1]
                        t3, b_t3 = rtmp[2]
                        t4, b_t4 = rtmp[3]
                        tt("dve", t1[0:nr], pv[:, :, 0:64], cosb, ALU.mult, [b_ps, b_cs_], [b_t1])
                        tt("dve", t2[0:nr], pv[:, :, 64:128], sinb, ALU.mult, [b_ps, b_cs_], [b_t2])
                        tt("dve", t3[0:nr], pv[:, :, 64:128], cosb, ALU.mult, [b_ps, b_cs_], [b_t3])
                        tt("dve", t4[0:nr], pv[:, :, 0:64], sinb, ALU.mult, [b_ps, b_cs_], [b_t4])
                        tt("dve", rv[:, :, 0:64], t1[0:nr], t2[0:nr], ALU.subtract, [b_t1, b_t2], [b_ro])
                        tt("dve", rv[:, :, 64:128], t3[0:nr], t4[0:nr], ALU.add, [b_t3, b_t4], [b_ro])
                        if kind == "k":
                            if 16 <= blk < 32:
                                dma("sp", nk_p[l, r0 - 2048:r0 - 2048 + 128, :], ro[:, :], G("st"), [b_ro], ())
                            if blk == 32:
                                for bb in range(NSB):
                                    dma("sp", nk_s[l, bb, 2044:2048, :], ro[bb * 4:bb * 4 + 4, :], G("st"), [b_ro], ())
                        rb_, b_rb = evb[blk % 2]
                        cp("act", rb_[0:nr, :], ro[0:nr, :], [b_ro], [b_rb])
                        pb, b_pb = next_psb()
                        for hh in range(4):
                            tr(pb[:, hh * 128:hh * 128 + nr], rb_[0:nr, hh * 128:(hh + 1) * 128], identb[0:nr, 0:nr],
                               [b_rb, B_const], [b_pb])
                        h4, b_h4 = hT4[blk % 2]
                        cp("act", h4[:, :, 0:nr], pb[:, 0:512].rearrange("p (a b) -> p a b", b=128)[:, :, 0:nr],
                           [b_pb], [b_h4])
                        if kind == "q":
                            h0 = c0 // 128
                            dma("sp", qT_d[h0:h0 + 4, :, r0:r0 + nr].rearrange("h p t -> p h t"), h4[:, :, 0:nr],
                                G("st"), [b_h4], [B_qT])
                        else:
                            dma("sp", kT_d[:, :, r0:r0 + nr].rearrange("h p t -> p h t"), h4[:, :, 0:nr],
                                G("st"), [b_h4], [B_kT])
                    elif kind == "v":
                        e_, b_e = ev[blk % 2]
                        cp("act", e_[0:nr, :], ps[0:nr, :], [b_ps], [b_e])
                        if 16 <= blk < 32:
                            dma("sp", nv_p[l, r0 - 2048:r0 - 2048 + 128, :], e_[:, :], G("st"), [b_e], ())
                        if blk == 32:
                            for bb in range(NSB):
                                dma("sp", nv_s[l, bb, 2044:2048, :], e_[bb * 4:bb * 4 + 4, :], G("st"), [b_e], ())
                        eb_, b_eb = evb[blk % 2]
                        cp("dve", eb_[0:nr, :], e_[0:nr, :], [b_e], [b_eb])
                        dma("sp", v_d[r0:r0 + nr, :], eb_[0:nr, :], G("st"), [b_eb], [B_v])
                    elif kind == "z":
                        e_, b_e = ev[blk % 2]
                        act(e_[0:nr, :], ps[0:nr, :], AF.Silu, [b_ps], [b_e])
                        zc = c0 - 2048
                        dma("sp", zs_d[r0:r0 + nr, zc:zc + 512], e_[0:nr, :], G("st"), [b_e], [B_zs])
                    else:
                        dw, b_dw = dtw[blk % 2]
                        tt("dve", dw[0:nr, :], ps[0:nr, 0:16], dtb[0:nr, :], ALU.add, [b_ps, b_dtb], [b_dw])
                        act(dw[0:nr, :], dw[0:nr, :], AF.Exp, [b_dw], [b_dw])
                        act(dw[0:nr, :], dw[0:nr, :], AF.Ln, [b_dw], [b_dw], bias=1.0)
                        dma("sp", dt_d[r0:r0 + nr, :], dw[0:nr, :], G("st"), [b_dw], [B_dt])
        dma("sp", nconv_s[l], ncs[:, :], G("st"), [b_ncs], ())
        ncpo, b_ncpo = alloc([4, 2048])
        for kc in range(16):
            ps, b_ps = next_ps()
            tr(ps[0:3, 0:128], ncp[:, kc, :], identf, [b_ncp, B_const], [b_ps])
            cp("act", ncpo[0:3, kc * 128:(kc + 1) * 128], ps[0:3, 0:128], [b_ps], [b_ncpo])
        dma("sp", nconv_p[l], ncpo[0:3, :], G("st"), [b_ncpo], ())
        K.barrier()

    def phase_attention(l):
        areset()
        qT, b_q = alloc([128, 2, SEQ], BF16)
        kT, b_k = alloc([128, SEQ], BF16)
        accN, b_aN = alloc([128, 2, SEQ])
        accD, b_aD = alloc([128, 2, SEQ])
        vb = [alloc([128, 128], BF16) for _ in range(4)]
        eb = [alloc([128, 256], BF16) for _ in range(4)]
        rc, b_rc = alloc([128, 2, 512])
        ob = [alloc([128, 2, 512], BF16) for _ in range(2)]
        for g in range(4):
            dma("sp", qT, qT_d[2 * g:2 * g + 2, :, 0:SEQ].rearrange("h p t -> p h t"), G("in"), [B_qT], [b_q])
            dma("sp", kT, kT_d[g, :, 0:SEQ], G("in"), [B_kT], [b_k])
            vi = 0
            for bi, dil in enumerate((1, 4, 16)):
                nblk_r = 32 // dil
                for r in range(dil):
                    vprev = None
                    for m in range(nblk_r):
                        s0 = r + dil * 128 * m
                        vt, b_vt = vb[vi % 4]
                        vi += 1
                        vsrc = bass.AP(v_d.tensor, v_d.offset + s0 * 512 + g * 128, [[dil * 512, 128], [1, 128]])
                        dma("sp", vt, vsrc, G("in"), [B_v], [b_vt])
                        qsl = qT[:, :, s0:s0 + dil * 127 + 1:dil]
                        psN, b_pN = next_ps()
                        psD, b_pD = next_ps()
                        kbl = [(s0, vt, b_vt, mcurb_t)]
                        if m > 0:
                            kbl.append((s0 - dil * 128, vprev[0], vprev[1], mprevb_t))
                        for ki, (ks0, vtile, b_vtile, mtile) in enumerate(kbl):
                            psS, b_pS = next_ps()
                            mm(psS[:, 0:256].rearrange("p (h q) -> p h q", q=128), kT[:, ks0:ks0 + dil * 127 + 1:dil], qsl,
                               True, True, [b_k, b_q], [b_pS])
                            et, b_et = eb[(vi + ki) % 4]
                            act(et, psS[:, 0:256], AF.Exp, [b_pS], [b_et], scale=SCALE)
                            tt("dve", et, et, mtile[:, :], ALU.mult, [b_et, B_const], [b_et])
                            mm(psN[:, 0:256], vtile, et, ki == 0, ki == len(kbl) - 1, [b_vtile, b_et], [b_pN])
                            mm(psD[:, 0:256], onesb_t[:, :], et, ki == 0, ki == len(kbl) - 1, [B_const, b_et], [b_pD])
                        vprev = (vt, b_vt)
                        aN = accN[:, :, s0:s0 + dil * 127 + 1:dil]
                        aD = accD[:, :, s0:s0 + dil * 127 + 1:dil]
                        pN = psN[:, 0:256].rearrange("p (h q) -> p h q", q=128)
                        pD = psD[:, 0:256].rearrange("p (h q) -> p h q", q=128)
                        if bi == 0:
                            cp("act", aN, pN, [b_pN], [b_aN])
                            cp("dve", aD, pD, [b_pD], [b_aD])
                        else:
                            tt("dve", aN, aN, pN, ALU.add, [b_aN, b_pN], [b_aN])
                            tt("pool", aD, aD, pD, ALU.add, [b_aD, b_pD], [b_aD]) if False else \
                                tt("dve", aD, aD, pD, ALU.add, [b_aD, b_pD], [b_aD])
            for tti in range(8):
                t0 = tti * 512
                K.add("dve", lambda e, t0=t0: e.reciprocal(rc, accD[:, :, t0:t0 + 512]), [b_aD], [b_rc])
                o_, b_o = ob[tti % 2]
                tt("dve", o_, accN[:, :, t0:t0 + 512], rc, ALU.mult, [b_aN, b_rc], [b_o])
                dma("sp", mixT_d[2 * g:2 * g + 2, :, t0:t0 + 512].rearrange("h p t -> p h t"), o_, G("st"),
                    [b_o], [B_mixA])
        K.barrier()

        areset()
        qs_t, b_qs = alloc([128, 4 * 128], BF16)
        qraw, b_qraw = alloc([128, 8, 64], BF16)
        kn, b_kn = alloc([128, 4, 64], BF16)
        vn, b_vn = alloc([64, 512], BF16)
        dma("sp", qraw, qT_d[:, :, SEQ:T].rearrange("h p t -> p h t"), G("in"), [B_qT], [b_qraw])
        dma("sp", kn, kT_d[:, :, SEQ:T].rearrange("h p t -> p h t"), G("in"), [B_kT], [b_kn])
        dma("sp", vn, v_d[SEQ:T, :], G("in"), [B_v], [b_vn])
        qs5 = qs_t.rearrange("p (g b r t) -> p g b r t", g=4, b=NSB, r=2)
        for g in range(4):
            for r in range(2):
                cp("dve", qs5[:, g, :, r, :], qraw[:, 2 * g + r, :].rearrange("p (b t) -> p b t", t=4),
                   [b_qraw], [b_qs])
        kc_t = [alloc([128, 7, 512], BF16) for _ in range(2)]
        vc_t = [alloc([128, 7, 512], BF16) for _ in range(2)]
        kTt = [alloc([128, 4, 7 * 128], BF16) for _ in range(2)]
        et_t = [alloc([128, 7 * 32], BF16) for _ in range(2)]
        smb, b_smb = alloc([128, 7 * 32], BF16)
        cp("dve", smb, C("smask"), [B_const], [b_smb])
        nmb, b_nmb = alloc([64, 128], BF16)
        cp("dve", nmb, C("nmask", 64), [B_const], [b_nmb])
        numC, b_numC = PS[0], B_PS[0]
        denC, b_denC = PS[1], B_PS[1]
        for b in range(NSB):
            kc, b_kc = kc_t[b % 2]
            vc, b_vc = vc_t[b % 2]
            for (src, dstt, b_dst) in ((ck, kc, b_kc), (cv, vc, b_vc)):
                dma("pool", dstt[:, 0:4, :], src[l, b, 1536:2048, :].rearrange("(j p) c -> p j c", p=128),
                    G("in"), (), [b_dst])
                for r in range(4):
                    base = src[l, b].offset + r * 512
                    sap = bass.AP(src.tensor, base, [[16 * 512, 32], [32 * 16 * 512, 3], [1, 512]])
                    dma("pool", dstt[32 * r:32 * r + 32, 4:7, :], sap, G("in"), (), [b_dst])
            kt_, b_kt = kTt[b % 2]
            for g in range(4):
                for half in range(2):
                    js = range(0, 4) if half == 0 else range(4, 7)
                    pb, b_pb = next_psb()
                    for jj, j in enumerate(js):
                        tr(pb[:, jj * 128:(jj + 1) * 128], kc[:, j, g * 128:(g + 1) * 128], identb, [b_kc, B_const], [b_pb])
                    n = len(js)
                    j0 = js[0]
                    cp("act" if half == 0 else "dve", kt_[:, g, j0 * 128:(j0 + n) * 128], pb[:, 0:n * 128], [b_pb], [b_kt])
            psS, b_pS = PS[2 + b % 2], B_PS[2 + b % 2]
            for j in range(7):
                for g in range(4):
                    mm(psS[:, j * 32 + g * 8:j * 32 + g * 8 + 8], kt_[:, g, j * 128:(j + 1) * 128],
                       qs5[:, g, b, :, :], True, True, [b_kt, b_qs], [b_pS])
            et, b_et = et_t[b % 2]
            act(et, psS[:, 0:224], AF.Exp, [b_pS], [b_et], scale=SCALE)
            tt("dve", et, et, smb, ALU.mult, [b_et, b_smb], [b_et])
            for g in range(4):
                c0 = g * 128 + b * 8
                for j in range(7):
                    mm(numC[:, c0:c0 + 8], vc[:, j, g * 128:(g + 1) * 128], et[:, j * 32 + g * 8:j * 32 + g * 8 + 8],
                       j == 0, j == 6, [b_vc, b_et], [b_numC])
                for j in range(7):
                    mm(denC[:, c0:c0 + 8], onesb_t[:, :], et[:, j * 32 + g * 8:j * 32 + g * 8 + 8],
                       j == 0, j == 6, [B_const, b_et], [b_denC])
        numS, b_numS = alloc([128, 512])
        denS, b_denS = alloc([128, 512])
        cp("act", numS, numC[:, :], [b_numC], [b_numS])
        cp("dve", denS, denC[:, :], [b_denC], [b_denS])
        en_t = [alloc([64, 128], BF16) for _ in range(2)]
        for g in range(4):
            psS, b_pS = PS[2 + g % 2], B_PS[2 + g % 2]
            mm(psS[0:64, 0:128], kn[:, g, :], qs_t[:, g * 128:(g + 1) * 128], True, True, [b_kn, b_qs], [b_pS])
            en, b_en = en_t[g % 2]
            act(en, psS[0:64, 0:128], AF.Exp, [b_pS], [b_en], scale=SCALE)
            tt("dve", en, en, nmb, ALU.mult, [b_en, b_nmb], [b_en])
            psA, b_pA = PS[4], B_PS[4]
            psB, b_pB = PS[5], B_PS[5]
            mm(psA[:, 0:128], vn[:, g * 128:(g + 1) * 128], en, True, True, [b_vn, b_en], [b_pA])
            mm(psB[:, 0:128], onesb_t[0:64, :], en, True, True, [B_const, b_en], [b_pB])
            tt("dve", numS[:, g * 128:(g + 1) * 128], numS[:, g * 128:(g + 1) * 128], psA[:, 0:128], ALU.add,
               [b_numS, b_pA], [b_numS])
            tt("dve", denS[:, g * 128:(g + 1) * 128], denS[:, g * 128:(g + 1) * 128], psB[:, 0:128], ALU.add,
               [b_denS, b_pB], [b_denS])
        K.add("dve", lambda e: e.reciprocal(denS, denS), [b_denS], [b_denS])
        tt("dve", numS, numS, denS, ALU.mult, [b_numS, b_denS], [b_numS])
        so, b_so = alloc([128, 8, 64], BF16)
        n5 = numS.rearrange("p (g b r t) -> p g b r t", g=4, b=NSB, r=2)
        for g in range(4):
            for r in range(2):
                cp("dve", so[:, 2 * g + r, :].rearrange("p (b t) -> p b t", t=4), n5[:, g, :, r, :], [b_numS], [b_so])
        dma("sp", mixT_d[0:8, :, SEQ:T].rearrange("h p t -> p h t"), so, G("st"), [b_so], [B_mixA])
        K.barrier()

    def phase_ssd(l):
        areset()
        alc, b_alc = alloc([128, 16])
        dsk, b_dsk = alloc([128, 16])
        Dbc, b_Dbc = alloc([128, 1024])
        gs, b_gs = alloc([128, 1024])
        dma("sp", alc, rowbc(a_log[l:l + 1, :], 16), g_misc, (), [b_alc])
        act(alc, alc, AF.Exp, [b_alc], [b_alc])
        ts("dve", alc, alc, -1.0, None, ALU.mult, None, [b_alc], [b_alc])
        dma("sp", dsk, rowbc(d_skip[l:l + 1, :], 16), g_misc, (), [b_dsk])
        cp("dve", Dbc.rearrange("p (h q) -> p h q", q=64), bcast_last(dsk, 64), [b_dsk], [b_Dbc])
        dma("sp", gs, rowbc(ssm_norm_g[l:l + 1, :], 1024), g_misc, (), [b_gs])
        hst, b_hst = alloc([128, 1024])
        hsb, b_hsb = alloc([128, 1024], BF16)
        mset("dve", hst, 0.0, [b_hst])
        xs2 = [alloc([128, 1024]) for _ in range(2)]
        zs2 = [alloc([128, 1024]) for _ in range(2)]
        dt2 = [alloc([128, 16]) for _ in range(2)]
        bt2 = [alloc([128, 512], BF16) for _ in range(2)]
        BT2 = [alloc([128, 4, 128], BF16) for _ in range(2)]
        CT2 = [alloc([128, 4, 128], BF16) for _ in range(2)]
        dtA, b_dtA = alloc([128, 16])
        acu, b_acu = alloc([128, 16])
        nacu, b_nacu = alloc([128, 16])
        tot, b_tot = alloc([128, 16])
        exA, b_exA = alloc([128, 16])
        toe, b_toe = alloc([128, 16])
        cde, b_cde = alloc([128, 16])
        acT, b_acT = alloc([16, 128])
        nacT, b_nacT = alloc([16, 128])
        cbT, b_cbT = alloc([128, 128])
        dec = [alloc([128, 4, 128]) for _ in range(2)]
        MT = [alloc([128, 4, 128], BF16) for _ in range(2)]
        xc, b_xc = alloc([128, 1024])
        xcb, b_xcb = alloc([128, 1024], BF16)
        xte, b_xte = alloc([128, 1024], BF16)
        y1, b_y1 = alloc([128, 1024])
        yb, b_yb = alloc([128, 1024], BF16)
        sq, b_sq = alloc([128, 1024])
        ss, b_ss = alloc([128, 4])
        yT, b_yT = alloc([128, 8, 128], BF16)
        h0T, b_h0T = alloc([128, NSB, 1024], BF16)
        h0n = [alloc([128, 8, 128]) for _ in range(2)]
        CTz, b_CTz = alloc([128, NSB * 64], BF16)
        Bz, b_Bz = alloc([64, NSB * 128], BF16)
        dtx, b_dtx = alloc([64, 1024])
        cdT, b_cdT = alloc([128, 8 * 16])
        nst = [alloc([128, 128]) for _ in range(2)]
        bmb, b_bmb = alloc([64, 16], BF16)
        cp("dve", bmb, C("bmask", 64), [B_const], [b_bmb])

        P_small, B_small = PS[0], B_PS[0]

        def chunk(cidx, nr, Ltri, Ones, negmb, sample):
            r0 = cidx * 128
            i2 = cidx % 2
            xs, b_xs = xs2[i2]
            zs, b_zs = zs2[i2]
            dtt, b_dtt = dt2[i2]
            btk, b_btk = bt2[i2]
            BTt, b_BTt = BT2[i2]
            CTt, b_CTt = CT2[i2]
            dma("sp", xs[0:nr, :], xs_d[r0:r0 + nr, :], G("in"), [B_xs], [b_xs])
            dma("sp", zs[0:nr, :], zs_d[r0:r0 + nr, :], G("in"), [B_zs], [b_zs])
            dma("sp", dtt[0:nr, :], dt_d[r0:r0 + nr, :], G("in"), [B_dt], [b_dtt])
            dma("sp", btk[0:nr, :], bt_d[r0:r0 + nr, :], G("in"), [B_bt], [b_btk])
            dma("sp", BTt[:, :, 0:nr], BT_d[:, :, r0:r0 + nr].rearrange("g p t -> p g t"), G("in"), [B_BT], [b_BTt])
            dma("sp", CTt[:, :, 0:nr], CT_d[:, :, r0:r0 + nr].rearrange("g p t -> p g t"), G("in"), [B_CT], [b_CTt])
            tt("dve", dtA[0:nr, :], dtt[0:nr, :], alc[0:nr, :], ALU.mult, [b_dtt, b_alc], [b_dtA])
            mm(P_small[0:nr, 0:16], Ltri[0:nr, 0:nr], dtA[0:nr, :], True, True, [B_const, b_dtA], [B_small])
            mm(P_small[0:nr, 16:32], Ones[0:nr, 0:nr], dtA[0:nr, :], True, True, [B_const, b_dtA], [B_small])
            mm(P_small[0:16, 32:32 + nr], dtA[0:nr, :], Ltri[0:nr, 0:nr], True, True, [B_const, b_dtA], [B_small])
            cp("act", acu[0:nr, :], P_small[0:nr, 0:16], [B_small], [b_acu])
            cp("act", tot[0:nr, :], P_small[0:nr, 16:32], [B_small], [b_tot])
            cp("act", acT[:, 0:nr], P_small[0:16, 32:32 + nr], [B_small], [b_acT])
            ts("dve", nacT[:, 0:nr], acT[:, 0:nr], -1.0, None, ALU.mult, None, [b_acT], [b_nacT])
            act(exA[0:nr, :], acu[0:nr, :], AF.Exp, [b_acu], [b_exA])
            tt("dve", toe[0:nr, :], tot[0:nr, :], acu[0:nr, :], ALU.subtract, [b_tot, b_acu], [b_toe])
            act(toe[0:nr, :], toe[0:nr, :], AF.Exp, [b_toe], [b_toe])
            act(cde[0:nr, :], tot[0:nr, :], AF.Exp, [b_tot], [b_cde])
            xs3 = xs[0:nr, :].rearrange("p (h q) -> p h q", q=64)
            tt("dve", xc[0:nr, :].rearrange("p (h q) -> p h q", q=64), xs3, bcast_last(dtt[0:nr, :], 64), ALU.mult,
               [b_xs, b_dtt], [b_xc])
            cp("act", xcb[0:nr, :], xc[0:nr, :], [b_xc], [b_xcb])
            tt("dve", xte[0:nr, :].rearrange("p (h q) -> p h q", q=64), xc[0:nr, :].rearrange("p (h q) -> p h q", q=64),
               bcast_last(toe[0:nr, :], 64), ALU.mult, [b_xc, b_toe], [b_xte])
            psY = [(PS[3], B_PS[3]), (PS[4], B_PS[4])]
            psO = [(PS[1], B_PS[1]), (PS[2], B_PS[2])]
            if not sample:
                cp("dve", hsb, hst, [b_hst], [b_hsb])
                for g in range(4):
                    po, b_po = psO[g // 2]
                    mm(po[0:nr, (g % 2) * 256:(g % 2) * 256 + 256], CTt[:, g, 0:nr], hsb[:, g * 256:(g + 1) * 256],
                       True, True, [b_CTt, b_hsb], [b_po])
            else:
                mset("dve", CTz, 0.0, [b_CTz])
                for g in range(4):
                    if g > 0:
                        mset("dve", CTz, 0.0, [b_CTz])
                    dstz = bass.AP(CTz.tensor, CTz.offset, [list(CTz.ap[0]), [68, NSB], [1, 4]])
                    cp("dve", dstz, CTt[:, g, 0:64].rearrange("p (b t) -> p b t", t=4), [b_CTt], [b_CTz])
                    po, b_po = psO[g // 2]
                    for b in range(NSB):
                        mm(po[0:64, (g % 2) * 256:(g % 2) * 256 + 256], CTz[:, b * 64:(b + 1) * 64],
                           h0T[:, b, g * 256:(g + 1) * 256], b == 0, b == NSB - 1, [b_CTz, b_h0T], [b_po])
            for g in range(4):
                mm(P_small[0:nr, 256:256 + nr], BTt[:, g, 0:nr], CTt[:, g, 0:nr], True, True, [b_BTt, b_CTt], [B_small])
                cp("act", cbT[0:nr, 0:nr], P_small[0:nr, 256:256 + nr], [B_small], [b_cbT])
                pd, b_pd = (PS[5], B_PS[5])
                selc = C("sel").rearrange("p (h m) -> p h m", m=128)
                for hh in range(4):
                    h = g * 4 + hh
                    o_ = pd[0:nr, hh * 128:hh * 128 + nr]
                    mm(o_, selc[0:16, h, 0:nr], acT[:, 0:nr], True, False, [B_const, b_acT], [b_pd])
                    mm(o_, nacT[:, 0:nr], selc[0:16, h, 0:nr], False, False, [B_const, b_nacT], [b_pd])
                    mm(o_, identb[0:nr, 0:nr], negmb[0:nr, 0:nr], False, True, [B_const], [b_pd])
                dc, b_dc = dec[g % 2]
                act(dc[0:nr, :, 0:nr], pd[0:nr, :].rearrange("p (a b) -> p a b", b=128)[:, :, 0:nr], AF.Exp, [b_pd], [b_dc])
                mt, b_mt = MT[g % 2]
                tt("dve", mt[0:nr, :, 0:nr], dc[0:nr, :, 0:nr], bcast_mid(cbT[0:nr, 0:nr], 4), ALU.mult,
                   [b_dc, b_cbT], [b_mt])
                py, b_py = psY[g // 2]
                for hh in range(4):
                    h = g * 4 + hh
                    cc = (h % 8) * 64
                    mm(py[0:nr, cc:cc + 64], mt[0:nr, hh, 0:nr], xcb[0:nr, h * 64:(h + 1) * 64], True, True,
                       [b_mt, b_xcb], [b_py])
            for half in range(2):
                po, b_po = psO[half]
                py, b_py = psY[half]
                cs_ = slice(half * 512, half * 512 + 512)
                tt("dve", y1[0:nr, cs_].rearrange("p (h q) -> p h q", q=64),
                   po[0:nr, :].rearrange("p (h q) -> p h q", q=64),
                   bcast_last(exA[0:nr, half * 8:half * 8 + 8], 64), ALU.mult, [b_po, b_exA], [b_y1])
                tt("dve", y1[0:nr, cs_], y1[0:nr, cs_], py[0:nr, :], ALU.add, [b_y1, b_py], [b_y1])
            tt("dve", sq[0:nr, :], xs[0:nr, :], Dbc[0:nr, :], ALU.mult, [b_xs, b_Dbc], [b_sq])
            tt("dve", y1[0:nr, :], y1[0:nr, :], sq[0:nr, :], ALU.add, [b_y1, b_sq], [b_y1])
            tt("dve", y1[0:nr, :], y1[0:nr, :], zs[0:nr, :], ALU.mult, [b_y1, b_zs], [b_y1])
            act(sq[0:nr, :], y1[0:nr, :], AF.Square, [b_y1], [b_sq])
            red("dve", ss[0:nr, 0:1], sq[0:nr, :], [b_sq], [b_ss])
            ts("dve", ss[0:nr, 0:1], ss[0:nr, 0:1], 1.0 / 1024, EPS, ALU.mult, ALU.add, [b_ss], [b_ss])
            rsqrt(ss[0:nr, 0:1], b_ss)
            stt("dve", yb[0:nr, :], y1[0:nr, :], ss[0:nr, 0:1], gs[0:nr, :], ALU.mult, ALU.mult,
                [b_y1, b_ss, b_gs], [b_yb])
            pb, b_pb = next_psb()
            for j in range(8):
                tr(pb[:, j * 128:j * 128 + nr], yb[0:nr, j * 128:(j + 1) * 128], identb[0:nr, 0:nr], [b_yb, B_const], [b_pb])
            cp("act", yT[:, :, 0:nr], pb[:, :].rearrange("p (a b) -> p a b", b=128)[:, :, 0:nr], [b_pb], [b_yT])
            dma("sp", mixT_d[8:16, :, r0:r0 + nr].rearrange("k p t -> p k t"), yT[:, :, 0:nr], G("st"), [b_yT], [B_mixS])
            if not sample:
                for g in range(4):
                    po, b_po = psO[g // 2]
                    mm(po[:, (g % 2) * 256:(g % 2) * 256 + 256], btk[0:nr, g * 128:(g + 1) * 128],
                       xte[0:nr, g * 256:(g + 1) * 256], True, True, [b_btk, b_xte], [b_po])
                tt("dve", hst.rearrange("p (h q) -> p h q", q=64), hst.rearrange("p (h q) -> p h q", q=64),
                   bcast_last(cde[:, :], 64), ALU.mult, [b_hst, b_cde], [b_hst])
                for half in range(2):
                    po, b_po = psO[half]
                    tt("dve", hst[:, half * 512:half * 512 + 512], hst[:, half * 512:half * 512 + 512], po[:, :],
                       ALU.add, [b_hst, b_po], [b_hst])
            return xte, b_xte, btk, b_btk, dtA, b_dtA

        for b in range(NSB):
            hn_, b_hn = h0n[b % 2]
            dma("sp", hn_, st_ssm[l, b].rearrange("(j p) n -> p j n", p=128), G("in"), (), [b_hn])
            for half in range(2):
                ps, b_ps = next_ps()
                for jj in range(4):
                    j = half * 4 + jj
                    tr(ps[:, jj * 128:(jj + 1) * 128], hn_[:, j, :], identf, [b_hn, B_const], [b_ps])
                cp("act" if half == 0 else "dve", h0T[:, b, half * 512:half * 512 + 512], ps[:, :], [b_ps], [b_h0T])
        xte_s, b_xte_s, btk_s, b_btk_s, dtA_s, b_dtA_s = chunk(32, 64, C("ltri_s"), C("ones_s"), negmsb_t, True)
        cp("dve", dtx.rearrange("p (h q) -> p h q", q=64), bcast_last(dtA_s[0:64, :], 64), [b_dtA_s], [b_dtx])
        pcd, b_pcd = next_ps()
        for j in range(8):
            mm(pcd[:, j * 16:(j + 1) * 16], dtx[:, j * 128:(j + 1) * 128], C("bmask", 64), True, True,
               [b_dtx, B_const], [b_pcd])
        act(cdT, pcd[:, 0:128], AF.Exp, [b_pcd], [b_cdT])
        for g in range(4):
            tt("dve", Bz.rearrange("p (b n) -> p b n", n=128), bcast_mid(btk_s[0:64, g * 128:(g + 1) * 128], NSB),
               bcast_last(bmb, 128), ALU.mult, [b_btk_s, b_bmb], [b_Bz])
            for b in range(NSB):
                hn_, b_hn = h0n[b % 2]
                if g == 0 or True:
                    dma("sp", hn_[:, 2 * g:2 * g + 2, :],
                        st_ssm[l, b, g * 256:(g + 1) * 256, :].rearrange("(j p) n -> p j n", p=128), G("in"), (), [b_hn])
                for jj in range(2):
                    j = 2 * g + jj
                    ps, b_ps = next_ps()
                    mm(ps[:, 0:128], xte_s[0:64, j * 128:(j + 1) * 128], Bz[:, b * 128:(b + 1) * 128], True, True,
                       [b_xte_s, b_Bz], [b_ps])
                    ns, b_ns = nst[(b * 2 + jj) % 2]
                    stt("dve", ns, hn_[:, j, :], cdT[:, j * 16 + b:j * 16 + b + 1], ps[:, 0:128], ALU.mult, ALU.add,
                        [b_hn, b_cdT, b_ps], [b_ns])
                    dma("sp", nssm_s[l, b, j * 128:(j + 1) * 128, :], ns, G("st"), [b_ns], ())
        for c in range(32):
            chunk(c, 128, C("ltri"), C("ones"), negmb_t, False)
        for half in range(2):
            ps, b_ps = next_ps()
            for jj in range(4):
                j = half * 4 + jj
                tr(ps[:, jj * 128:(jj + 1) * 128], hst[:, j * 128:(j + 1) * 128], identf, [b_hst, B_const], [b_ps])
            fo, b_fo = nst[half]
            fo4, b_fo4 = xs2[half]
            cp("act", fo4[:, 0:512], ps[:, :], [b_ps], [b_fo4])
            dma("sp", nssm_p[l, half * 512:half * 512 + 512, :].rearrange("(j p) n -> p j n", p=128),
                fo4[:, 0:512].rearrange("p (j n) -> p j n", n=128), G("st"), [b_fo4], ())
        K.barrier()

    def dense_tok(l, wsrc, nk, srcT, b_srcT, gamma_next_row, dstT_next, post):
        pass

    def phase_outproj(l):
        areset()
        w, b_w = alloc([128, 16, 2048], BF16)
        load_w(w, b_w, w_out[l], 0, 16, 0, 2048)
        gam, b_gam = load_gamma(norm_ffn_g[l:l + 1, :])
        tmps = [norm_tmp() for _ in range(2)]
        hb = [alloc([128, 2048]) for _ in range(2)]
        mx = [alloc([128, 16, 128], BF16) for _ in range(2)]
        for blk in range(NBLK):
            nr = nr_of(blk)
            r0 = blk * 128
            h, b_h = hb[blk % 2]
            m, b_m = mx[blk % 2]
            dma("sp", h[0:nr, :], h_d[r0:r0 + nr, :], G("in"), [B_h[blk]], [b_h])
            dma("sp", m[:, :, 0:nr], mixT_d[:, :, r0:r0 + nr].rearrange("k p t -> p k t"), G("in"),
                [B_mixA, B_mixS], [b_m])
            for cg in range(4):
                ps, b_ps = next_ps()
                for kc in range(16):
                    mm(ps[0:nr, :], m[:, kc, 0:nr], w[:, kc, cg * 512:(cg + 1) * 512], kc == 0, kc == 15,
                       [b_m, b_w], [b_ps])
                tt("dve", h[0:nr, cg * 512:(cg + 1) * 512], h[0:nr, cg * 512:(cg + 1) * 512], ps[0:nr, :], ALU.add,
                   [b_h, b_ps], [b_h])
            dma("sp", h_d[r0:r0 + nr, :], h[0:nr, :], G("st"), [b_h], [B_h[blk]])
            norm_block(h, b_h, gam, b_gam, hnT_d, B_hnT, blk, tmps[blk % 2])
        K.barrier()

    def phase_ffn(l):
        NG = 4
        GC = 11
        for fg in range(NG):
            areset()
            wg_, b_wg = alloc([128, 16, GC * 128], BF16)
            wu_, b_wu = alloc([128, 16, GC * 128], BF16)
            wd_, b_wd = alloc([128, GC, 2048], BF16)
            c0 = fg * GC * 128
            load_w(wg_, b_wg, w_g[l], 0, 16, c0, GC * 128)
            load_w(wu_, b_wu, w_u[l], 0, 16, c0, GC * 128)
            load_w(wd_, b_wd, w_d[l], fg * GC, GC, 0, 2048)
            last = False
            hx = [alloc([128, 16, 512], BF16) for _ in range(1)]
            at = [alloc([128, GC, 512], BF16) for _ in range(1)]
            sg = [alloc([128, 512]) for _ in range(2)]
            hb = [alloc([128, 2048]) for _ in range(2)]
            for tti in range(9):
                tw = 512 if tti < 8 else 64
                t0 = tti * 512
                hxt, b_hx = hx[0]
                dma("sp", hxt[:, :, 0:tw], hnT_d[:, :, t0:t0 + tw].rearrange("k p t -> p k t"), G("in"),
                    [B_hnT], [b_hx])
                a_, b_a = at[0]
                for j in range(GC):
                    pg, b_pg = next_ps()
                    pu, b_pu = next_ps()
                    for kc in range(16):
                        mm(pg[:, 0:tw], wg_[:, kc, j * 128:(j + 1) * 128], hxt[:, kc, 0:tw], kc == 0, kc == 15,
                           [b_wg, b_hx], [b_pg])
                    for kc in range(16):
                        mm(pu[:, 0:tw], wu_[:, kc, j * 128:(j + 1) * 128], hxt[:, kc, 0:tw], kc == 0, kc == 15,
                           [b_wu, b_hx], [b_pu])
                    s_, b_s = sg[j % 2]
                    act(s_[:, 0:tw], pg[:, 0:tw], AF.Silu, [b_pg], [b_s])
                    tt("dve", a_[:, j, 0:tw], s_[:, 0:tw], pu[:, 0:tw], ALU.mult, [b_s, b_pu], [b_a])
                nsb = 4 if tti < 8 else 1
                for sb in range(nsb):
                    blk = tti * 4 + sb if tti < 8 else 32
                    nr = nr_of(blk)
                    r0 = blk * 128
                    h, b_h = hb[blk % 2]
                    dma("sp", h[0:nr, :], h_d[r0:r0 + nr, :], G("in"), [B_h[blk]], [b_h])
                    for cg in range(4):
                        ps, b_ps = next_ps()
                        for j in range(GC):
                            mm(ps[0:nr, :], a_[:, j, sb * 128:sb * 128 + nr], wd_[:, j, cg * 512:(cg + 1) * 512],
                               j == 0, j == GC - 1, [b_a, b_wd], [b_ps])
                        tt("dve", h[0:nr, cg * 512:(cg + 1) * 512], h[0:nr, cg * 512:(cg + 1) * 512], ps[0:nr, :],
                           ALU.add, [b_h, b_ps], [b_h])
                    dma("sp", h_d[r0:r0 + nr, :], h[0:nr, :], G("st"), [b_h], [B_h[blk]])
                    if last:
                        norm_block(h, b_h, gam, b_gam, mixT_d, B_mixA, blk, tmps[0])
            K.barrier()

    def phase_ple(l):
        areset()
        w, b_w = alloc([128, 16, 2048], BF16)
        wp, b_wp = alloc([128, 2, 2048], BF16)
        load_w(w, b_w, w_pg[l], 0, 16, 0, 2048)
        load_w(wp, b_wp, w_pp[l], 0, 2, 0, 2048)
        lastl = l == DEPTH - 1
        gam, b_gam = load_gamma(final_g[0:1, :] if lastl else norm_mix_g[l + 1:l + 2, :])
        tmps = [norm_tmp() for _ in range(1)]
        hb = [alloc([128, 2048]) for _ in range(2)]
        mx = [alloc([128, 16, 128], BF16) for _ in range(2)]
        pin = [alloc([128, 256]) for _ in range(2)]
        pbf = [alloc([128, 256], BF16) for _ in range(2)]
        pT = [alloc([128, 2, 128], BF16) for _ in range(2)]
        sgm = [alloc([128, 512]) for _ in range(2)]
        yo = [alloc([128, 2048]) for _ in range(2)]
        for blk in range(NBLK):
            nr = nr_of(blk)
            r0 = blk * 128
            h, b_h = hb[blk % 2]
            m, b_m = mx[blk % 2]
            dma("sp", h[0:nr, :], h_d[r0:r0 + nr, :], G("in"), [B_h[blk]], [b_h])
            dma("sp", m[:, :, 0:nr], mixT_d[:, :, r0:r0 + nr].rearrange("k p t -> p k t"), G("in"), [B_mixA], [b_m])
            pi, b_pi = pin[blk % 2]
            psrc = p_p[l, r0:r0 + nr, :] if blk < 32 else p_s[l, :, :]
            dma("sp", pi[0:nr, :], psrc, G("in"), (), [b_pi])
            pb_, b_pb_ = pbf[blk % 2]
            cp("dve", pb_[0:nr, :], pi[0:nr, :], [b_pi], [b_pb_])
            pbk, b_pbk = next_psb()
            for j in range(2):
                tr(pbk[:, j * 128:j * 128 + nr], pb_[0:nr, j * 128:(j + 1) * 128], identb[0:nr, 0:nr],
                   [b_pb_, B_const], [b_pbk])
            pt_, b_pt = pT[blk % 2]
            cp("act", pt_[:, :, 0:nr], pbk[:, 0:256].rearrange("p (a b) -> p a b", b=128)[:, :, 0:nr], [b_pbk], [b_pt])
            for cg in range(4):
                pg, b_pg = next_ps()
                pp_, b_pp = next_ps()
                for kc in range(16):
                    mm(pg[0:nr, :], m[:, kc, 0:nr], w[:, kc, cg * 512:(cg + 1) * 512], kc == 0, kc == 15,
                       [b_m, b_w], [b_pg])
                for kc in range(2):
                    mm(pp_[0:nr, :], pt_[:, kc, 0:nr], wp[:, kc, cg * 512:(cg + 1) * 512], kc == 0, kc == 1,
                       [b_pt, b_wp], [b_pp])
                s_, b_s = sgm[cg % 2]
                act(s_[0:nr, :], pg[0:nr, :], AF.Sigmoid, [b_pg], [b_s])
                tt("dve", s_[0:nr, :], s_[0:nr, :], pp_[0:nr, :], ALU.mult, [b_s, b_pp], [b_s])
                tt("dve", h[0:nr, cg * 512:(cg + 1) * 512], h[0:nr, cg * 512:(cg + 1) * 512], s_[0:nr, :], ALU.add,
                   [b_h, b_s], [b_h])
            if not lastl:
                dma("sp", h_d[r0:r0 + nr, :], h[0:nr, :], G("st"), [b_h], [B_h[blk]])
                norm_block(h, b_h, gam, b_gam, hnT_d, B_hnT, blk, tmps[0])
            else:
                sq, b_sq = tmps[0]["sq"]
                ss, b_ss = tmps[0]["ss"]
                y_, b_y = yo[blk % 2]
                act(sq[0:nr, :], h[0:nr, :], AF.Square, [b_h], [b_sq])
                red("dve", ss[0:nr, 0:1], sq[0:nr, :], [b_sq], [b_ss])
                ts("dve", ss[0:nr, 0:1], ss[0:nr, 0:1], 1.0 / D, EPS, ALU.mult, ALU.add, [b_ss], [b_ss])
                rsqrt(ss[0:nr, 0:1], b_ss)
                stt("dve", y_[0:nr, :], h[0:nr, :], ss[0:nr, 0:1], gam[0:nr, :], ALU.mult, ALU.mult,
                    [b_h, b_ss, b_gam], [b_y])
                if blk < 32:
                    dma("sp", y_p[r0:r0 + 128, :], y_[:, :], G("st"), [b_y], ())
                else:
                    dma("sp", y_s[:, :], y_[0:64, :], G("st"), [b_y], ())
        K.barrier()

    import os as _os
    KSTOP = int(_os.environ.get("KSTOP", "99"))
    phases = [lambda l: phase_inproj(l), lambda l: phase_attention(l), lambda l: phase_ssd(l),
              lambda l: phase_outproj(l), lambda l: phase_ffn(l),
              lambda l: phase_norm_only(norm_ple_g[l:l + 1, :], mixT_d, B_mixA), lambda l: phase_ple(l)]
    if KSTOP >= 1:
        phase_norm_only(norm_mix_g[0:1, :])
    cnt = 1
    for l in range(DEPTH):
        K.epoch = l + 1
        for ph in phases:
            cnt += 1
            if KSTOP >= cnt:
                ph(l)
    K.final_wait()
    K.emit()
    es.close()
    return nc, carr


_CACHE = {}


def kernel(**inp):
    if "prog" not in _CACHE:
        _CACHE["prog"] = build_program()
    nc, carr = _CACHE["prog"]
    f = lambda a: np.ascontiguousarray(np.asarray(a, dtype=np.float32))
    x_prompt = f(inp["x_prompt"])
    x_sample = f(inp["x_sample"])
    ck = np.asarray(inp["cache_k"], dtype=np.float32).reshape(DEPTH, 128, L, 512)
    cv = np.asarray(inp["cache_v"], dtype=np.float32).reshape(DEPTH, 128, L, 512)
    ssm = np.asarray(inp["state_ssm"], dtype=np.float32).reshape(DEPTH, 128, 1024, 128)
    conv = np.asarray(inp["state_conv"], dtype=np.float32)
    p_prompt = f(inp["p_prompt"])
    p_sample = f(inp["p_sample"])
    shared = {}
    for k_ in ("norm_mix_g", "w_in", "conv_w", "conv_b", "dt_bias", "a_log", "d_skip", "ssm_norm_g", "w_out",
               "norm_ffn_g", "w_ffn_gate", "w_ffn_up", "w_ffn_down", "norm_ple_g", "w_ple_gate", "w_ple_proj"):
        shared[k_] = f(inp[k_])
    shared["final_norm_g"] = f(inp["final_norm_g"]).reshape(1, D)
    shared["consts"] = carr
    in_maps = []
    for c in range(NCORES):
        s = c % 2
        b0 = c * NSB
        m = dict(shared)
        m["x_p"] = x_prompt[s]
        m["x_s"] = np.ascontiguousarray(x_sample[b0:b0 + NSB].reshape(TS, D))
        m["ck"] = np.ascontiguousarray(ck[:, b0:b0 + NSB])
        m["cv"] = np.ascontiguousarray(cv[:, b0:b0 + NSB])
        m["st_ssm"] = np.ascontiguousarray(ssm[:, b0:b0 + NSB])
        m["st_conv"] = np.ascontiguousarray(conv[:, b0:b0 + NSB].reshape(DEPTH, NSB * 3, 2048))
        m["p_p"] = np.ascontiguousarray(p_prompt[:, s])
        m["p_s"] = np.ascontiguousarray(p_sample[:, b0:b0 + NSB].reshape(DEPTH, TS, PLE))
        in_maps.append(m)
    res = run_bass_kernel_spmd(nc, in_maps, core_ids=list(range(NCORES)))
    R = res.results
    y_prompt = np.stack([R[0]["y_p"], R[1]["y_p"]]).reshape(2, SEQ, D)
    y_sample = np.concatenate([R[c]["y_s"].reshape(NSB, 4, D) for c in range(NCORES)], axis=0)
    nk_p = np.stack([R[0]["nk_p"], R[1]["nk_p"]], axis=1).reshape(DEPTH, 2, 2048, 4, 128)
    nv_p = np.stack([R[0]["nv_p"], R[1]["nv_p"]], axis=1).reshape(DEPTH, 2, 2048, 4, 128)
    nssm_p = np.stack([R[0]["nssm_p"], R[1]["nssm_p"]], axis=1).reshape(DEPTH, 2, 16, 64, 128)
    nconv_p = np.stack([R[0]["nconv_p"], R[1]["nconv_p"]], axis=1).reshape(DEPTH, 2, 3, 2048)
    nk_s = np.concatenate([R[c]["nk_s"] for c in range(NCORES)], axis=1).reshape(DEPTH, 128, L, 4, 128)
    nv_s = np.concatenate([R[c]["nv_s"] for c in range(NCORES)], axis=1).reshape(DEPTH, 128, L, 4, 128)
    nssm_s = np.concatenate([R[c]["nssm_s"] for c in range(NCORES)], axis=1).reshape(DEPTH, 128, 16, 64, 128)
    nconv_s = np.concatenate([R[c]["nconv_s"].reshape(DEPTH, NSB, 3, 2048) for c in range(NCORES)], axis=1)
    outs = (y_prompt, y_sample, nk_p, nv_p, nssm_p, nconv_p, nk_s, nv_s, nssm_s, nconv_s)
    return tuple(np.ascontiguousarray(o, dtype=np.float32) for o in outs)
```

```python
import math
from contextlib import ExitStack

import numpy as np
import concourse.bass as bass
import concourse.mybir as mybir
from concourse.bass_utils import run_bass_kernel_spmd

F32 = mybir.dt.float32
BF16 = mybir.dt.bfloat16
I32 = mybir.dt.int32
AF = mybir.ActivationFunctionType
ALU = mybir.AluOpType
AX = mybir.AxisListType

NCORES = 8
D = 2048
SEQ = 4096
NSB = 16
TS = 64
T = SEQ + TS
NBLK = 33
DEPTH = 2
L = 2048
DFF = 5632
PLE = 256
INP = 5136
EPS = 1e-6
SCALE = 128 ** -0.5
NEG = -30000.0


def nr_of(b):
    return 128 if b < 32 else 64


class Buf:
    __slots__ = ("w", "r", "name", "dram", "dg")

    def __init__(self, name="", dram=False):
        self.w = None
        self.r = {}
        self.name = name
        self.dram = dram
        self.dg = None


class DG:
    def __init__(self, sem, bulk=False):
        self.sem = sem
        self.cnt = 0
        self.bulk = bulk


class Op:
    __slots__ = ("eng", "fn", "deps", "dg", "has_dep", "sig", "epoch")


class Sched:
    ENG = ("pe", "act", "dve", "pool", "sp")

    def __init__(self, nc, es):
        self.nc = nc
        self.es = es
        self.ops = {e: [] for e in self.ENG}
        self.epoch = 0
        self.dgs = []
        self.nsem = 0
        self.dgpool = []
        self.pool_idx = 0

    def phase_reset(self):
        self.pool_idx = 0

    def buf_dg(self, buf):
        if buf.dg is None:
            if self.pool_idx >= len(self.dgpool):
                self.dgpool.append(self.dgroup("p%d" % len(self.dgpool)))
            buf.dg = self.dgpool[self.pool_idx]
            self.pool_idx += 1
        return buf.dg

    def newsem(self, name):
        self.nsem += 1
        return self.es.enter_context(self.nc.semaphore(name))

    def dgroup(self, name, bulk=False):
        g = DG(self.newsem("dg_" + name), bulk)
        self.dgs.append(g)
        return g

    def add(self, eng, fn, reads=(), writes=(), dg=None):
        op = Op()
        op.eng = eng
        op.fn = fn
        op.dg = dg
        op.has_dep = False
        op.sig = None
        op.epoch = self.epoch
        deps = []
        for b in reads:
            if b.w is not None:
                deps.append(b.w)
        for b in writes:
            if b.w is not None:
                deps.append(b.w)
            deps.extend(b.r.values())
        if dg is not None:
            dg.cnt += 16
            me = (dg, dg.cnt)
            key = dg
        else:
            me = op
            key = eng
        clean = []
        for d in deps:
            if isinstance(d, Op):
                if eng == "pe" and d.eng == "pe":
                    continue
                if d is op:
                    continue
                d.has_dep = True
            clean.append(d)
        op.deps = clean
        for b in reads:
            b.r[key] = me
        for b in writes:
            b.w = me
            b.r = {}
        self.ops[eng].append(op)
        return op

    def barrier(self):
        lasts = []
        for e in self.ENG:
            for o in reversed(self.ops[e]):
                if o.dg is None and o.fn is not None:
                    lasts.append(o)
                    break
        dgl = [(g, g.cnt) for g in self.dgs if g.cnt > 0 and not g.bulk]
        for e in self.ENG:
            op = Op()
            op.eng = e
            op.fn = None
            op.dg = None
            op.has_dep = False
            op.sig = None
            op.epoch = self.epoch
            op.deps = []
            for o in lasts:
                if o.eng != e:
                    o.has_dep = True
                    op.deps.append(o)
            op.deps.extend(dgl)
            self.ops[e].append(op)

    def final_wait(self):
        dgl = [(g, g.cnt) for g in self.dgs if g.cnt > 0]
        op = Op()
        op.eng = "sp"
        op.fn = None
        op.dg = None
        op.has_dep = False
        op.sig = None
        op.epoch = self.epoch
        op.deps = list(dgl)
        for e in self.ENG:
            for o in reversed(self.ops[e]):
                if o.dg is None and o.fn is not None:
                    if e != "sp":
                        o.has_dep = True
                        op.deps.append(o)
                    break
        self.ops["sp"].append(op)

    def emit(self):
        nc = self.nc
        sems = {}
        for e in self.ENG:
            cnt = {}
            for o in self.ops[e]:
                if o.dg is None and o.has_dep and o.fn is not None:
                    k = (e, o.epoch)
                    cnt[k] = cnt.get(k, 0) + 1
                    o.sig = cnt[k]
                    if k not in sems:
                        sems[k] = self.newsem("e_%s_%d" % (e, o.epoch))
        block = self.es.enter_context(nc.Block())

        def run(eng_name):
            def body(e):
                waited = {}
                for o in self.ops[eng_name]:
                    for d in o.deps:
                        if isinstance(d, Op):
                            key = (d.eng, d.epoch)
                            sem = sems[key]
                            v = d.sig
                        else:
                            key = d[0]
                            sem = d[0].sem
                            v = d[1]
                        if waited.get(key, 0) < v:
                            e.wait_ge(sem, v)
                            waited[key] = v
                    if o.fn is None:
                        continue
                    ins = o.fn(e)
                    if o.dg is not None:
                        ins.then_inc(o.dg.sem, 16)
                    elif o.has_dep:
                        ins.then_inc(sems[(eng_name, o.epoch)], 1)
            return body

        block.tensor(run("pe"))
        block.scalar(run("act"))
        block.vector(run("dve"))
        block.gpsimd(run("pool"))
        block.sync(run("sp"))


def make_consts():
    c = {}
    ident = np.eye(128, dtype=np.float32)
    c["ident"] = ident
    k = np.arange(128)[:, None]
    q = np.arange(128)[None, :]
    mcur = (q >= k).astype(np.float32)
    mprev = (k >= q).astype(np.float32)
    c["mcur"] = np.concatenate([mcur, mcur], axis=1)
    c["mprev"] = np.concatenate([mprev, mprev], axis=1)
    ltri = (k <= q).astype(np.float32)
    c["ltri"] = ltri
    c["negm"] = np.where(k <= q, 0.0, NEG).astype(np.float32)
    c["ones"] = np.ones((128, 128), np.float32)
    bs = np.arange(64)[:, None] // 4
    bt = np.arange(64)[None, :] // 4
    same = bs == bt
    s_ = np.arange(64)[:, None]
    t_ = np.arange(64)[None, :]
    lbd = np.zeros((128, 128), np.float32)
    lbd[:64, :64] = (same & (s_ <= t_)).astype(np.float32)
    c["ltri_s"] = lbd
    nbd = np.full((128, 128), NEG, np.float32)
    nbd[:64, :64] = np.where(same & (s_ <= t_), 0.0, NEG)
    c["negm_s"] = nbd
    obd = np.zeros((128, 128), np.float32)
    obd[:64, :64] = same.astype(np.float32)
    c["ones_s"] = obd
    sel = np.zeros((128, 16, 128), np.float32)
    for h in range(16):
        sel[h, h, :] = 1.0
    c["sel"] = sel.reshape(128, 2048)
    bm = np.zeros((128, 16), np.float32)
    bm[np.arange(64), np.arange(64) // 4] = 1.0
    c["bmask"] = bm
    msk = np.zeros((128, 7, 4), np.float32)
    for j in range(7):
        for p in range(128):
            if j < 4:
                row = 1536 + 128 * j + p
            else:
                r = p // 32
                m = 32 * (j - 4) + (p % 32)
                row = 16 * m + r
            for t in range(4):
                diff = 2048 + t - row
                mult = 0
                if 0 <= diff <= 128:
                    mult += 1
                if diff % 4 == 0 and 0 <= diff <= 512:
                    mult += 1
                if diff % 16 == 0 and 0 <= diff <= 2048:
                    mult += 1
                msk[p, j, t] = mult
    mexp = np.broadcast_to(msk[:, :, None, None, :], (128, 7, 4, 2, 4)).reshape(128, 7 * 32)
    c["smask"] = np.ascontiguousarray(mexp)
    mn = np.zeros((128, 128), np.float32)
    for kk in range(64):
        b1, t1 = kk // 4, kk % 4
        for b in range(16):
            for r in range(2):
                for t in range(4):
                    if b == b1 and t1 <= t:
                        mn[kk, b * 8 + r * 4 + t] = 3.0 if t1 == t else 1.0
    c["nmask"] = mn
    pos = np.zeros((128, NBLK), np.float32)
    for b in range(32):
        pos[:, b] = 128 * b + np.arange(128)
    pos[:64, 32] = 2048 + (np.arange(64) % 4)
    c["pos"] = pos
    c["jidx"] = np.broadcast_to(np.arange(64, dtype=np.float32)[None, :], (128, 64)).copy()
    return c


CONST_ORDER = ["ident", "mcur", "mprev", "ltri", "negm", "ones", "ltri_s", "negm_s", "ones_s",
               "sel", "bmask", "smask", "nmask", "pos", "jidx"]


def pack_consts():
    c = make_consts()
    offs = {}
    cols = 0
    for n in CONST_ORDER:
        offs[n] = (cols, c[n].shape[1])
        cols += c[n].shape[1]
    arr = np.zeros((128, cols), np.float32)
    for n in CONST_ORDER:
        o, w = offs[n]
        arr[:, o:o + w] = c[n]
    return arr, offs


def build_program():
    carr, coffs = pack_consts()
    NCC = carr.shape[1]
    nc = bass.Bass("TRN2", target_bir_lowering=False)
    es = ExitStack()
    K = Sched(nc, es)

    def din(name, shape, dt=F32):
        return nc.dram_tensor(name, list(shape), dt, kind="ExternalInput").ap()

    def dout(name, shape, dt=F32):
        return nc.dram_tensor(name, list(shape), dt, kind="ExternalOutput").ap()

    def dscr(name, shape, dt):
        return nc.dram_tensor(name, list(shape), dt).ap()

    x_p = din("x_p", [SEQ, D])
    x_s = din("x_s", [TS, D])
    ck = din("ck", [DEPTH, NSB, L, 512])
    cv = din("cv", [DEPTH, NSB, L, 512])
    st_ssm = din("st_ssm", [DEPTH, NSB, 1024, 128])
    st_conv = din("st_conv", [DEPTH, NSB * 3, 2048])
    p_p = din("p_p", [DEPTH, SEQ, PLE])
    p_s = din("p_s", [DEPTH, TS, PLE])
    norm_mix_g = din("norm_mix_g", [DEPTH, D])
    w_in = din("w_in", [DEPTH, D, INP])
    conv_w = din("conv_w", [DEPTH, 4, 2048])
    conv_b = din("conv_b", [DEPTH, 2048])
    dt_bias = din("dt_bias", [DEPTH, 16])
    a_log = din("a_log", [DEPTH, 16])
    d_skip = din("d_skip", [DEPTH, 16])
    ssm_norm_g = din("ssm_norm_g", [DEPTH, 1024])
    w_out = din("w_out", [DEPTH, D, D])
    norm_ffn_g = din("norm_ffn_g", [DEPTH, D])
    w_g = din("w_ffn_gate", [DEPTH, D, DFF])
    w_u = din("w_ffn_up", [DEPTH, D, DFF])
    w_d = din("w_ffn_down", [DEPTH, DFF, D])
    norm_ple_g = din("norm_ple_g", [DEPTH, D])
    w_pg = din("w_ple_gate", [DEPTH, D, D])
    w_pp = din("w_ple_proj", [DEPTH, PLE, D])
    final_g = din("final_norm_g", [1, D])
    cst = din("consts", [128, NCC])
    y_p = dout("y_p", [SEQ, D])
    y_s = dout("y_s", [TS, D])
    nk_p = dout("nk_p", [DEPTH, 2048, 512])
    nv_p = dout("nv_p", [DEPTH, 2048, 512])
    nssm_p = dout("nssm_p", [DEPTH, 1024, 128])
    nconv_p = dout("nconv_p", [DEPTH, 3, 2048])
    nk_s = dout("nk_s", [DEPTH, NSB, L, 512])
    nv_s = dout("nv_s", [DEPTH, NSB, L, 512])
    nssm_s = dout("nssm_s", [DEPTH, NSB, 1024, 128])
    nconv_s = dout("nconv_s", [DEPTH, NSB * 3, 2048])
    h_d = dscr("h_d", [T, D], F32)
    hnT_d = dscr("hnT_d", [16, 128, T], BF16)
    qT_d = dscr("qT_d", [8, 128, T], BF16)
    kT_d = dscr("kT_d", [4, 128, T], BF16)
    v_d = dscr("v_d", [T, 512], BF16)
    zs_d = dscr("zs_d", [T, 1024], F32)
    dt_d = dscr("dt_d", [T, 16], F32)
    xs_d = dscr("xs_d", [T, 1024], F32)
    bt_d = dscr("bt_d", [T, 512], BF16)
    BT_d = dscr("BT_d", [4, 128, T], BF16)
    CT_d = dscr("CT_d", [4, 128, T], BF16)
    mixT_d = dscr("mixT_d", [16, 128, T], BF16)
    cs_d = dscr("cs_d", [T, 128], F32)

    B_h = [Buf("h%d" % b, True) for b in range(NBLK)]
    B_hnT = Buf("hnT", True)
    B_qT, B_kT, B_v, B_zs, B_dt, B_xs, B_bt, B_BT, B_CT = (Buf("d", True) for _ in range(9))
    B_mixA, B_mixS, B_cs = Buf("d", True), Buf("d", True), Buf("d", True)

    ARENA = 48600
    cs_t = es.enter_context(nc.sbuf_tensor("cst", [128, NCC], F32))
    arena = es.enter_context(nc.sbuf_tensor("arena", [128, ARENA], F32))
    identb_t = es.enter_context(nc.sbuf_tensor("identb", [128, 128], BF16))
    onesb_t = es.enter_context(nc.sbuf_tensor("onesb", [128, 128], BF16))
    negmb_t = es.enter_context(nc.sbuf_tensor("negmb", [128, 128], BF16))
    negmsb_t = es.enter_context(nc.sbuf_tensor("negmsb", [128, 128], BF16))
    mcurb_t = es.enter_context(nc.sbuf_tensor("mcurb", [128, 256], BF16))
    mprevb_t = es.enter_context(nc.sbuf_tensor("mprevb", [128, 256], BF16))
    B_const = Buf("const")

    def C(name, rows=128):
        o, w = coffs[name]
        return cs_t[0:rows, o:o + w]

    PS = [es.enter_context(nc.psum_tensor("ps%d" % i, [128, 512], F32)) for i in range(6)]
    PSB = [es.enter_context(nc.psum_tensor("psb%d" % i, [128, 1024], BF16)) for i in range(2)]
    B_PS = [Buf("ps%d" % i) for i in range(6)]
    B_PSB = [Buf("psb%d" % i) for i in range(2)]
    ps_rr = [0]
    psb_rr = [0]

    def next_ps():
        i = ps_rr[0] % 6
        ps_rr[0] += 1
        return PS[i], B_PS[i]

    def next_psb():
        i = psb_rr[0] % 2
        psb_rr[0] += 1
        return PSB[i], B_PSB[i]

    apos = [0]

    phase0 = [True]

    def areset():
        apos[0] = 0
        if not phase0[0]:
            K.phase_reset()

    def alloc(shape, dt=F32):
        n = 1
        for s in shape[1:]:
            n *= s
        words = n if dt == F32 or dt == I32 else (n + 1) // 2
        o = apos[0]
        apos[0] += words
        assert apos[0] <= ARENA, ("arena overflow", apos[0])
        v = arena[0:shape[0], o:o + words]
        if dt != F32:
            v = v.bitcast(dt)
            v = v[:, 0:n]
        if len(shape) == 3:
            v = v.rearrange("p (a b) -> p a b", b=shape[2])
        elif len(shape) == 4:
            v = v.rearrange("p (a b c) -> p a b c", b=shape[2], c=shape[3])
        return v, Buf()

    def mm(out, lhsT, rhs, start, stop, R, W):
        return K.add("pe", lambda e: e.matmul(out, lhsT, rhs, start=start, stop=stop), R, W)

    def tr(out, in_, ident, R, W):
        return K.add("pe", lambda e: e.transpose(out, in_, ident), R, W)

    def act(out, in_, func, R, W, bias=None, scale=None):
        def f(e):
            kw = {}
            if bias is not None:
                kw["bias"] = bias
            if scale is not None:
                kw["scale"] = scale
            return e.activation(out, in_, func, **kw)
        return K.add("act", f, R, W)

    def cp(eng, out, in_, R, W):
        if eng == "act":
            return act(out, in_, AF.Copy, R, W)
        return K.add(eng, lambda e: e.tensor_copy(out, in_), R, W)

    def tt(eng, out, a, b, op, R, W):
        return K.add(eng, lambda e: e.tensor_tensor(out, a, b, op), R, W)

    def ts(eng, out, a, s1, s2, op0, op1, R, W):
        if s2 is None:
            return K.add(eng, lambda e: e.tensor_scalar(out, a, s1, None, op0), R, W)
        return K.add(eng, lambda e: e.tensor_scalar(out, a, s1, s2, op0, op1), R, W)

    def stt(eng, out, a, s, b, op0, op1, R, W):
        return K.add(eng, lambda e: e.scalar_tensor_tensor(out, a, s, b, op0, op1), R, W)

    def rsqrt(col, b_col):
        act(col, col, AF.Sqrt, [b_col], [b_col])
        K.add("dve", lambda e: e.reciprocal(col, col), [b_col], [b_col])

    def red(eng, out, in_, R, W):
        return K.add(eng, lambda e: e.reduce_sum(out, in_, axis=AX.X), R, W)

    def mset(eng, out, val, W):
        return K.add(eng, lambda e: e.memset(out, val), (), W)

    def dma(q, out, in_, dg, R, W, slow=False):
        if dg in g_st:
            q = "pool"
        sb = None
        for b_ in W:
            if not b_.dram:
                sb = b_
                break
        if sb is None:
            for b_ in R:
                if not b_.dram:
                    sb = b_
                    break
        if sb is not None:
            dg = K.buf_dg(sb)
        if slow:
            return K.add(q, lambda e: e.dma_start(out=out, in_=in_, allow_slow_non_contiguous=True), R, W, dg=dg)
        return K.add(q, lambda e: e.dma_start(out=out, in_=in_), R, W, dg=dg)

    def bc_mid(ap2d, n_mid, n_in):
        raise NotImplementedError

    def bcast_last(ap, n):
        a = [list(x) for x in ap.ap]
        return bass.AP(ap.tensor, ap.offset, a + [[0, n]])

    def bcast_mid(ap, n):
        a = [list(x) for x in ap.ap]
        return bass.AP(ap.tensor, ap.offset, [a[0], [0, n]] + a[1:])

    def rowbc(dram_row_ap, ncols):
        return bass.AP(dram_row_ap.tensor, dram_row_ap.offset, [[0, 128], [1, ncols]])

    g_in = [K.dgroup("in%d" % i) for i in range(4)]
    g_w = [K.dgroup("w%d" % i) for i in range(4)]
    g_st = [K.dgroup("st%d" % i) for i in range(4)]
    g_bulk = K.dgroup("bulk", bulk=True)
    g_misc = K.dgroup("misc")
    rr = {"in": 0, "w": 0, "st": 0}

    def G(kind):
        lst = {"in": g_in, "w": g_w, "st": g_st}[kind]
        i = rr[kind] % len(lst)
        rr[kind] += 1
        return lst[i]

    dma("sp", cs_t[:, :], cst, g_misc, (), [B_const])
    cp("dve", identb_t[:, :], C("ident"), [B_const], [B_const])
    cp("dve", onesb_t[:, :], C("ones"), [B_const], [B_const])
    cp("dve", negmb_t[:, :], C("negm"), [B_const], [B_const])
    cp("dve", negmsb_t[:, :], C("negm_s"), [B_const], [B_const])
    cp("dve", mcurb_t[:, :], C("mcur"), [B_const], [B_const])
    cp("dve", mprevb_t[:, :], C("mprev"), [B_const], [B_const])
    identb = identb_t[:, :]
    identf = C("ident")

    for l in range(DEPTH):
        for b in range(NSB):
            for (src, dst) in ((ck, nk_s), (cv, nv_s)):
                for part in range(4):
                    r0 = 4 + part * 511
                    dma("sp", dst[l, b, r0 - 4:r0 - 4 + 511, :], src[l, b, r0:r0 + 511, :], g_bulk, (), ())
    for i in range(8):
        dma("sp", h_d[i * 512:(i + 1) * 512, :], x_p[i * 512:(i + 1) * 512, :], g_misc, (), B_h[4 * i:4 * i + 4])
    dma("sp", h_d[SEQ:T, :], x_s[:, :], g_misc, (), [B_h[32]])

    areset()
    invf, b_invf = alloc([128, 64])
    act(invf, C("jidx"), AF.Exp, [B_const], [b_invf], scale=-math.log(10000.0) / 64.0)
    TWO_PI = 2.0 * math.pi
    for b in range(NBLK):
        nr = nr_of(b)
        ang, b_ang = alloc([128, 128])
        o, _ = coffs["pos"]
        posc = cs_t[:, o + b:o + b + 1]
        kf, b_kf = alloc([128, 128])
        ki, b_ki = alloc([128, 128], I32)
        ts("dve", ang[:, 64:128], invf, posc, None, ALU.mult, None, [b_invf, B_const], [b_ang])
        ts("dve", ang[:, 0:64], invf, posc, 0.5 * math.pi, ALU.mult, ALU.add, [b_invf, B_const], [b_ang])
        ts("dve", kf, ang, 1.0 / TWO_PI, None, ALU.mult, None, [b_ang], [b_kf])
        cp("dve", ki, kf, [b_kf], [b_ki])
        cp("dve", kf, ki, [b_ki], [b_kf])
        stt("dve", ang, kf, -TWO_PI, ang, ALU.mult, ALU.add, [b_kf, b_ang], [b_ang])
        ts("dve", ang, ang, math.pi, -math.pi, ALU.min, ALU.max, [b_ang], [b_ang])
        act(ang, ang, AF.Sin, [b_ang], [b_ang])
        r0 = b * 128
        dma("sp", cs_d[r0:r0 + nr, :], ang[0:nr, :], g_misc, [b_ang], [B_cs])
    K.barrier()
    phase0[0] = False

    def norm_block(hblk, b_hblk, gam, b_gam, dstT, b_dst, blk, tmp):
        nr = nr_of(blk)
        sq, b_sq = tmp["sq"]
        ss, b_ss = tmp["ss"]
        hn, b_hn = tmp["hn"]
        hT, b_hT = tmp["hT"]
        act(sq[0:nr, :], hblk[0:nr, :], AF.Square, [b_hblk], [b_sq])
        red("dve", ss[0:nr, 0:1], sq[0:nr, :], [b_sq], [b_ss])
        ts("dve", ss[0:nr, 0:1], ss[0:nr, 0:1], 1.0 / D, EPS, ALU.mult, ALU.add, [b_ss], [b_ss])
        rsqrt(ss[0:nr, 0:1], b_ss)
        stt("dve", hn[0:nr, :], hblk[0:nr, :], ss[0:nr, 0:1], gam[0:nr, :], ALU.mult, ALU.mult,
            [b_hblk, b_ss, b_gam], [b_hn])
        for half in range(2):
            pb, b_pb = next_psb()
            for j in range(8):
                kc = half * 8 + j
                tr(pb[:, j * 128:j * 128 + nr], hn[0:nr, kc * 128:(kc + 1) * 128], identb[0:nr, 0:nr],
                   [b_hn, B_const], [b_pb])
            cp("act" if half == 0 else "dve", hT[:, half * 8:(half + 1) * 8, 0:nr],
               pb[:, :].rearrange("p (a b) -> p a b", b=128)[:, :, 0:nr], [b_pb], [b_hT])
        r0 = blk * 128
        dma("sp", dstT[:, :, r0:r0 + nr].rearrange("k p t -> p k t"), hT[:, :, 0:nr], G("st"), [b_hT], [b_dst])

    def norm_tmp():
        return {"sq": alloc([128, 2048]), "ss": alloc([128, 4]), "hn": alloc([128, 2048], BF16),
                "hT": alloc([128, 16, 128], BF16)}

    def load_gamma(row_ap):
        g, b_g = alloc([128, 2048])
        dma("sp", g, rowbc(row_ap, 2048), g_misc, (), [b_g])
        return g, b_g

    def phase_norm_only(gamma_row, dstT=None, b_dst=None):
        if dstT is None:
            dstT, b_dst = hnT_d, B_hnT
        areset()
        gam, b_gam = load_gamma(gamma_row)
        tmps = [norm_tmp() for _ in range(2)]
        hb = [alloc([128, 2048]) for _ in range(2)]
        for blk in range(NBLK):
            nr = nr_of(blk)
            h, b_h = hb[blk % 2]
            dma("sp", h[0:nr, :], h_d[blk * 128:blk * 128 + nr, :], G("in"), [B_h[blk]], [b_h])
            norm_block(h, b_h, gam, b_gam, dstT, b_dst, blk, tmps[blk % 2])
        K.barrier()

    def load_w(dst, b_dst, wsrc, k0, nk, c0, ncols):
        for kk in range(0, nk, 4):
            n = min(4, nk - kk)
            src = wsrc[(k0 + kk) * 128:(k0 + kk + n) * 128, c0:c0 + ncols].rearrange("(k p) c -> p k c", p=128)
            dma("pool", dst[:, kk:kk + n, 0:ncols], src, G("w"), (), [b_dst])

    def phase_inproj(l):
        areset()
        wb = [alloc([128, 16, 512], BF16) for _ in range(2)]
        hx = [alloc([128, 16, 512], BF16) for _ in range(2)]
        ev = [alloc([128, 512]) for _ in range(2)]
        evb = [alloc([128, 512], BF16) for _ in range(2)]
        cst_t = [alloc([128, 128]) for _ in range(2)]
        rot = [alloc([128, 512]) for _ in range(2)]
        rtmp = [alloc([128, 4, 64]) for _ in range(4)]
        hT4 = [alloc([128, 4, 128], BF16) for _ in range(2)]
        dtb, b_dtb = alloc([128, 16])
        dtw = [alloc([128, 16]) for _ in range(2)]
        pad = [alloc([128, 3 + 512]) for _ in range(2)]
        carry = alloc([128, 16, 3])
        acc = [alloc([128, 512]) for _ in range(2)]
        cw, b_cw = alloc([128, 16, 4])
        cb, b_cb = alloc([128, 16])
        sct, b_sct = alloc([128, 16, 48])
        scin, b_scin = alloc([48, 2048])
        pads = [alloc([128, 16, 7]) for _ in range(2)]
        accs = [alloc([128, 16, 4]) for _ in range(2)]
        xsf = [alloc([128, 512]) for _ in range(2)]
        xsb = [alloc([128, 512], BF16) for _ in range(2)]
        xtk = [alloc([128, 4, 128]) for _ in range(2)]
        btk = [alloc([128, 4, 128], BF16) for _ in range(2)]
        ncs, b_ncs = alloc([48, 2048])
        ncp, b_ncp = alloc([128, 16, 3])

        dma("sp", dtb, rowbc(dt_bias[l:l + 1, :], 16), g_misc, (), [b_dtb])
        cwin, b_cwin = alloc([8, 2048])
        dma("sp", cwin[0:4, :], conv_w[l], g_misc, (), [b_cwin])
        dma("sp", cwin[4:5, :], conv_b[l:l + 1, :], g_misc, (), [b_cwin])
        for kc in range(16):
            ps, b_ps = next_ps()
            tr(ps[:, 0:5], cwin[0:5, kc * 128:(kc + 1) * 128], identf[0:5, 0:5], [b_cwin, B_const], [b_ps])
            cp("act", cw[:, kc, :], ps[:, 0:4], [b_ps], [b_cw])
            cp("act", cb[:, kc:kc + 1], ps[:, 4:5], [b_ps], [b_cb])
        dma("sp", scin, st_conv[l], g_misc, (), [b_scin])
        for kc in range(16):
            ps, b_ps = next_ps()
            tr(ps[:, 0:48], scin[:, kc * 128:(kc + 1) * 128], identf[0:48, 0:48], [b_scin, B_const], [b_ps])
            cp("act", sct[:, kc, :], ps[:, 0:48], [b_ps], [b_sct])
        mset("dve", carry[0], 0.0, [carry[1]])

        groups = [("q", 0, 512), ("q", 512, 512), ("k", 1024, 512), ("v", 1536, 512),
                  ("z", 2048, 512), ("z", 2560, 512),
                  ("x", 3072, 512), ("x", 3584, 512), ("x", 4096, 512), ("x", 4608, 512),
                  ("dt", 5120, 16)]
        wi = w_in[l]
        for gi, (kind, c0, ncw) in enumerate(groups):
            w, b_w = wb[gi % 2]
            load_w(w, b_w, wi, 0, 16, c0, ncw)
            for tti in range(9):
                tw = 512 if tti < 8 else 64
                t0 = tti * 512
                hxt, b_hx = hx[tti % 2]
                dma("sp", hxt[:, :, 0:tw], hnT_d[:, :, t0:t0 + tw].rearrange("k p t -> p k t"), G("in"),
                    [B_hnT], [b_hx])
                if kind == "x":
                    for sc in range(4):
                        kcx = (c0 - 3072) // 128 + sc
                        ps, b_ps = next_ps()
                        for kc in range(16):
                            mm(ps[:, 0:tw], w[:, kc, sc * 128:(sc + 1) * 128], hxt[:, kc, 0:tw],
                               kc == 0, kc == 15, [b_w, b_hx], [b_ps])
                        xf, b_xf = xsf[sc % 2]
                        if tti < 8:
                            pd, b_pd = pad[sc % 2]
                            ac, b_ac = acc[sc % 2]
                            cp("dve", pd[:, 0:3], carry[0][:, kcx, :], [carry[1]], [b_pd])
                            cp("act", pd[:, 3:3 + 512], ps[:, :], [b_ps], [b_pd])
                            cp("dve", carry[0][:, kcx, :], pd[:, 512:515], [b_pd], [carry[1]])
                            if tti == 7:
                                cp("dve", ncp[:, kcx, :], pd[:, 512:515], [b_pd], [b_ncp])
                            ts("dve", ac, pd[:, 0:512], cw[:, kcx, 0:1], None, ALU.mult, None, [b_pd, b_cw], [b_ac])
                            for wv in range(1, 4):
                                stt("dve", ac, pd[:, wv:wv + 512], cw[:, kcx, wv:wv + 1], ac, ALU.mult, ALU.add,
                                    [b_pd, b_cw, b_ac], [b_ac])
                            act(xf, ac, AF.Silu, [b_ac, b_cb], [b_xf], bias=cb[:, kcx:kcx + 1])
                        else:
                            pd, b_pd = pads[sc % 2]
                            ac, b_ac = accs[sc % 2]
                            cp("dve", pd[:, :, 0:3], sct[:, kcx, :].rearrange("p (b t) -> p b t", t=3),
                               [b_sct], [b_pd])
                            cp("act", pd[:, :, 3:7], ps[:, 0:64].rearrange("p (b t) -> p b t", t=4), [b_ps], [b_pd])
                            ts("dve", ac, pd[:, :, 0:4], cw[:, kcx, 0:1], None, ALU.mult, None, [b_pd, b_cw], [b_ac])
                            for wv in range(1, 4):
                                stt("dve", ac, pd[:, :, wv:wv + 4], cw[:, kcx, wv:wv + 1], ac, ALU.mult, ALU.add,
                                    [b_pd, b_cw, b_ac], [b_ac])
                            act(xf[:, 0:64], ac.rearrange("p b t -> p (b t)"), AF.Silu, [b_ac, b_cb], [b_xf],
                                bias=cb[:, kcx:kcx + 1])
                            nct, b_nct = xtk[sc % 2]
                            cp("dve", nct[:, 0, 0:48].rearrange("p (b t) -> p b t", t=3), pd[:, :, 4:7], [b_pd], [b_nct])
                            ps2, b_ps2 = next_ps()
                            tr(ps2[0:48, 0:128], nct[:, 0, 0:48], identf, [b_nct, B_const], [b_ps2])
                            cp("act", ncs[:, kcx * 128:(kcx + 1) * 128], ps2[0:48, 0:128], [b_ps2], [b_ncs])
                        if kcx < 8:
                            xt, b_xt = xtk[sc % 2]
                            nsb = 4 if tti < 8 else 1
                            ps3, b_ps3 = next_ps()
                            for sb in range(nsb):
                                nrr = 128 if tti < 8 else 64
                                tr(ps3[0:nrr, sb * 128:(sb + 1) * 128], xf[:, sb * 128:sb * 128 + nrr], identf,
                                   [b_xf, B_const], [b_ps3])
                            nrr = 128 if tti < 8 else 64
                            cp("act", xt[0:nrr, 0:nsb, :], ps3[0:nrr, 0:nsb * 128].rearrange("p (a b) -> p a b", b=128),
                               [b_ps3], [b_xt])
                            if tti < 8:
                                dma("sp", xs_d[t0:t0 + 512, kcx * 128:(kcx + 1) * 128].rearrange("(a p) c -> p a c", p=128),
                                    xt[:, :, :], G("st"), [b_xt], [B_xs])
                            else:
                                dma("sp", xs_d[t0:t0 + 64, kcx * 128:(kcx + 1) * 128], xt[0:64, 0, :], G("st"),
                                    [b_xt], [B_xs])
                        else:
                            xb_, b_xb = xsb[sc % 2]
                            cp("dve", xb_[:, 0:tw], xf[:, 0:tw], [b_xf], [b_xb])
                            isB = kcx < 12
                            gq = kcx - 8 if isB else kcx - 12
                            dst = BT_d if isB else CT_d
                            dma("sp", dst[gq, :, t0:t0 + tw], xb_[:, 0:tw], G("st"), [b_xb], [B_BT if isB else B_CT])
                            if isB:
                                bk, b_bk = btk[sc % 2]
                                nsb = 4 if tti < 8 else 1
                                nrr = 128 if tti < 8 else 64
                                pb, b_pb = next_psb()
                                for sb in range(nsb):
                                    tr(pb[0:nrr, sb * 128:(sb + 1) * 128], xb_[:, sb * 128:sb * 128 + nrr], identb,
                                       [b_xb, B_const], [b_pb])
                                cp("act", bk[0:nrr, 0:nsb, :], pb[0:nrr, 0:nsb * 128].rearrange("p (a b) -> p a b", b=128),
                                   [b_pb], [b_bk])
                                if tti < 8:
                                    dma("sp", bt_d[t0:t0 + 512, gq * 128:(gq + 1) * 128].rearrange("(a p) c -> p a c", p=128),
                                        bk[:, :, :], G("st"), [b_bk], [B_bt])
                                else:
                                    dma("sp", bt_d[t0:t0 + 64, gq * 128:(gq + 1) * 128], bk[0:64, 0, :], G("st"),
                                        [b_bk], [B_bt])
                    continue
                nsb = 4 if tti < 8 else 1
                for sb in range(nsb):
                    blk = tti * 4 + sb if tti < 8 else 32
                    nr = nr_of(blk)
                    r0 = blk * 128
                    ps, b_ps = next_ps()
                    for kc in range(16):
                        mm(ps[0:nr, 0:ncw], hxt[:, kc, sb * 128:sb * 128 + nr], w[:, kc, 0:ncw],
                           kc == 0, kc == 15, [b_w, b_hx], [b_ps])
                    if kind in ("q", "k"):
                        cs_, b_cs_ = cst_t[blk % 2]
                        dma("sp", cs_[0:nr, :], cs_d[r0:r0 + nr, :], G("in"), [B_cs], [b_cs_])
                        ro, b_ro = rot[blk % 2]
                        pv = ps[0:nr, :].rearrange("p (h d) -> p h d", d=128)
                        rv = ro[0:nr, :].rearrange("p (h d) -> p h d", d=128)
                        cosb = bcast_mid(cs_[0:nr, 0:64], 4)
                        sinb = bcast_mid(cs_[0:nr, 64:128], 4)
                        t1, b_t1 = rtmp[0]
                        t2, b_t2 = rtmp[1]
                        t3, b_t3 = rtmp[2]
                        t4, b_t4 = rtmp[3]
                        tt("dve", t1[0:nr], pv[:, :, 0:64], cosb, ALU.mult, [b_ps, b_cs_], [b_t1])
                        tt("dve", t2[0:nr], pv[:, :, 64:128], sinb, ALU.mult, [b_ps, b_cs_], [b_t2])
                        tt("dve", t3[0:nr], pv[:, :, 64:128], cosb, ALU.mult, [b_ps, b_cs_], [b_t3])
                        tt("dve", t4[0:nr], pv[:, :, 0:64], sinb, ALU.mult, [b_ps, b_cs_], [b_t4])
                        tt("dve", rv[:, :, 0:64], t1[0:nr], t2[0:nr], ALU.subtract, [b_t1, b_t2], [b_ro])
                        tt("dve", rv[:, :, 64:128], t3[0:nr], t4[0:nr], ALU.add, [b_t3, b_t4], [b_ro])
                        if kind == "k":
                            if 16 <= blk < 32:
                                dma("sp", nk_p[l, r0 - 2048:r0 - 2048 + 128, :], ro[:, :], G("st"), [b_ro], ())
                            if blk == 32:
                                for bb in range(NSB):
                                    dma("sp", nk_s[l, bb, 2044:2048, :], ro[bb * 4:bb * 4 + 4, :], G("st"), [b_ro], ())
                        rb_, b_rb = evb[blk % 2]
                        cp("act", rb_[0:nr, :], ro[0:nr, :], [b_ro], [b_rb])
                        pb, b_pb = next_psb()
                        for hh in range(4):
                            tr(pb[:, hh * 128:hh * 128 + nr], rb_[0:nr, hh * 128:(hh + 1) * 128], identb[0:nr, 0:nr],
                               [b_rb, B_const], [b_pb])
                        h4, b_h4 = hT4[blk % 2]
                        cp("act", h4[:, :, 0:nr], pb[:, 0:512].rearrange("p (a b) -> p a b", b=128)[:, :, 0:nr],
                           [b_pb], [b_h4])
                        if kind == "q":
                            h0 = c0 // 128
                            dma("sp", qT_d[h0:h0 + 4, :, r0:r0 + nr].rearrange("h p t -> p h t"), h4[:, :, 0:nr],
                                G("st"), [b_h4], [B_qT])
                        else:
                            dma("sp", kT_d[:, :, r0:r0 + nr].rearrange("h p t -> p h t"), h4[:, :, 0:nr],
                                G("st"), [b_h4], [B_kT])
                    elif kind == "v":
                        e_, b_e = ev[blk % 2]
                        cp("act", e_[0:nr, :], ps[0:nr, :], [b_ps], [b_e])
                        if 16 <= blk < 32:
                            dma("sp", nv_p[l, r0 - 2048:r0 - 2048 + 128, :], e_[:, :], G("st"), [b_e], ())
                        if blk == 32:
                            for bb in range(NSB):
                                dma("sp", nv_s[l, bb, 2044:2048, :], e_[bb * 4:bb * 4 + 4, :], G("st"), [b_e], ())
                        eb_, b_eb = evb[blk % 2]
                        cp("dve", eb_[0:nr, :], e_[0:nr, :], [b_e], [b_eb])
                        dma("sp", v_d[r0:r0 + nr, :], eb_[0:nr, :], G("st"), [b_eb], [B_v])
                    elif kind == "z":
                        e_, b_e = ev[blk % 2]
                        act(e_[0:nr, :], ps[0:nr, :], AF.Silu, [b_ps], [b_e])
                        zc = c0 - 2048
                        dma("sp", zs_d[r0:r0 + nr, zc:zc + 512], e_[0:nr, :], G("st"), [b_e], [B_zs])
                    else:
                        dw, b_dw = dtw[blk % 2]
                        tt("dve", dw[0:nr, :], ps[0:nr, 0:16], dtb[0:nr, :], ALU.add, [b_ps, b_dtb], [b_dw])
                        act(dw[0:nr, :], dw[0:nr, :], AF.Exp, [b_dw], [b_dw])
                        act(dw[0:nr, :], dw[0:nr, :], AF.Ln, [b_dw], [b_dw], bias=1.0)
                        dma("sp", dt_d[r0:r0 + nr, :], dw[0:nr, :], G("st"), [b_dw], [B_dt])
        dma("sp", nconv_s[l], ncs[:, :], G("st"), [b_ncs], ())
        ncpo, b_ncpo = alloc([4, 2048])
        for kc in range(16):
            ps, b_ps = next_ps()
            tr(ps[0:3, 0:128], ncp[:, kc, :], identf, [b_ncp, B_const], [b_ps])
            cp("act", ncpo[0:3, kc * 128:(kc + 1) * 128], ps[0:3, 0:128], [b_ps], [b_ncpo])
        dma("sp", nconv_p[l], ncpo[0:3, :], G("st"), [b_ncpo], ())
        K.barrier()

    def phase_attention(l):
        areset()
        qT, b_q = alloc([128, 2, SEQ], BF16)
        kT, b_k = alloc([128, SEQ], BF16)
        accN, b_aN = alloc([128, 2, SEQ])
        accD, b_aD = alloc([128, 2, SEQ])
        rc, b_rc = alloc([128, 2, 512])
        ob = [alloc([128, 2, 512], BF16) for _ in range(2)]
        vb = [alloc([128, 128], BF16) for _ in range(8)]
        eb = [alloc([128, 256], BF16) for _ in range(6)]
        for g in range(4):
            dma("sp", qT, qT_d[2 * g:2 * g + 2, :, 0:SEQ].rearrange("h p t -> p h t"), G("in"), [B_qT], [b_q])
            dma("sp", kT, kT_d[g, :, 0:SEQ], G("in"), [B_kT], [b_k])
            vi = 0
            ei = 0
            si = 0
            ndi = 0
            pending = None
            for bi, dil in enumerate((1, 4, 16)):
                nblk_r = 32 // dil
                for r in range(dil):
                    vprev = None
                    for m in range(nblk_r):
                        s0 = r + dil * 128 * m
                        vt, b_vt = vb[vi % 8]
                        vi += 1
                        vsrc = bass.AP(v_d.tensor, v_d.offset + s0 * 512 + g * 128, [[dil * 512, 128], [1, 128]])
                        dma("sp", vt, vsrc, G("in"), [B_v], [b_vt])
                        qsl = qT[:, :, s0:s0 + dil * 127 + 1:dil]
                        kbl = [(s0, vt, b_vt, mcurb_t)]
                        if m > 0:
                            kbl.append((s0 - dil * 128, vprev[0], vprev[1], mprevb_t))
                        ets = []
                        for ki, (ks0, vtile, b_vtile, mtile) in enumerate(kbl):
                            psS, b_pS = PS[si % 4], B_PS[si % 4]
                            si += 1
                            mm(psS[:, 0:256].rearrange("p (h q) -> p h q", q=128), kT[:, ks0:ks0 + dil * 127 + 1:dil], qsl,
                               True, True, [b_k, b_q], [b_pS])
                            et, b_et = eb[ei % 6]
                            ei += 1
                            act(et, psS[:, 0:256], AF.Exp, [b_pS], [b_et], scale=SCALE)
                            tt("dve", et, et, mtile[:, :], ALU.mult, [b_et, B_const], [b_et])
                            ets.append((et, b_et, vtile, b_vtile))
                        vprev = (vt, b_vt)
                        psND, b_pND = PS[4 + ndi % 2], B_PS[4 + ndi % 2]
                        ndi += 1

                        def stage2(ets=ets, psND=psND, b_pND=b_pND, s0=s0, dil=dil, bi=bi):
                            n = len(ets)
                            for ki, (et, b_et, vtile, b_vtile) in enumerate(ets):
                                mm(psND[:, 0:256], vtile, et, ki == 0, ki == n - 1, [b_vtile, b_et], [b_pND])
                            for ki, (et, b_et, vtile, b_vtile) in enumerate(ets):
                                mm(psND[:, 256:512], onesb_t[:, :], et, ki == 0, ki == n - 1, [B_const, b_et], [b_pND])
                            aN = accN[:, :, s0:s0 + dil * 127 + 1:dil]
                            aD = accD[:, :, s0:s0 + dil * 127 + 1:dil]
                            pN = psND[:, 0:256].rearrange("p (h q) -> p h q", q=128)
                            pD = psND[:, 256:512].rearrange("p (h q) -> p h q", q=128)
                            if bi == 0:
                                cp("dve", aN, pN, [b_pND], [b_aN])
                                cp("dve", aD, pD, [b_pND], [b_aD])
                            else:
                                tt("dve", aN, aN, pN, ALU.add, [b_aN, b_pND], [b_aN])
                                tt("dve", aD, aD, pD, ALU.add, [b_aD, b_pND], [b_aD])

                        if pending is not None:
                            pending()
                        pending = stage2
            if pending is not None:
                pending()
            for tti in range(8):
                t0 = tti * 512
                K.add("dve", lambda e, t0=t0: e.reciprocal(rc, accD[:, :, t0:t0 + 512]), [b_aD], [b_rc])
                o_, b_o = ob[tti % 2]
                tt("dve", o_, accN[:, :, t0:t0 + 512], rc, ALU.mult, [b_aN, b_rc], [b_o])
                dma("sp", mixT_d[2 * g:2 * g + 2, :, t0:t0 + 512].rearrange("h p t -> p h t"), o_, G("st"),
                    [b_o], [B_mixA])
        K.barrier()

        areset()
        qs_t, b_qs = alloc([128, 4 * 128], BF16)
        qraw, b_qraw = alloc([128, 8, 64], BF16)
        kn, b_kn = alloc([128, 4, 64], BF16)
        vn, b_vn = alloc([64, 512], BF16)
        dma("sp", qraw, qT_d[:, :, SEQ:T].rearrange("h p t -> p h t"), G("in"), [B_qT], [b_qraw])
        dma("sp", kn, kT_d[:, :, SEQ:T].rearrange("h p t -> p h t"), G("in"), [B_kT], [b_kn])
        dma("sp", vn, v_d[SEQ:T, :], G("in"), [B_v], [b_vn])
        qs5 = qs_t.rearrange("p (g b r t) -> p g b r t", g=4, b=NSB, r=2)
        for g in range(4):
            for r in range(2):
                cp("dve", qs5[:, g, :, r, :], qraw[:, 2 * g + r, :].rearrange("p (b t) -> p b t", t=4),
                   [b_qraw], [b_qs])
        kc_t = [alloc([128, 7, 512], BF16) for _ in range(2)]
        vc_t = [alloc([128, 7, 512], BF16) for _ in range(2)]
        kTt = [alloc([128, 4, 7 * 128], BF16) for _ in range(2)]
        et_t = [alloc([128, 7 * 32], BF16) for _ in range(2)]
        smb, b_smb = alloc([128, 7 * 32], BF16)
        cp("dve", smb, C("smask"), [B_const], [b_smb])
        nmb, b_nmb = alloc([64, 128], BF16)
        cp("dve", nmb, C("nmask", 64), [B_const], [b_nmb])
        numC, b_numC = PS[0], B_PS[0]
        denC, b_denC = PS[1], B_PS[1]
        for b in range(NSB):
            kc, b_kc = kc_t[b % 2]
            vc, b_vc = vc_t[b % 2]
            for (src, dstt, b_dst) in ((ck, kc, b_kc), (cv, vc, b_vc)):
                dma("pool", dstt[:, 0:4, :], src[l, b, 1536:2048, :].rearrange("(j p) c -> p j c", p=128),
                    G("in"), (), [b_dst])
                for r in range(4):
                    base = src[l, b].offset + r * 512
                    sap = bass.AP(src.tensor, base, [[16 * 512, 32], [32 * 16 * 512, 3], [1, 512]])
                    dma("pool", dstt[32 * r:32 * r + 32, 4:7, :], sap, G("in"), (), [b_dst])
            kt_, b_kt = kTt[b % 2]
            for g in range(4):
                for half in range(2):
                    js = range(0, 4) if half == 0 else range(4, 7)
                    pb, b_pb = next_psb()
                    for jj, j in enumerate(js):
                        tr(pb[:, jj * 128:(jj + 1) * 128], kc[:, j, g * 128:(g + 1) * 128], identb, [b_kc, B_const], [b_pb])
                    n = len(js)
                    j0 = js[0]
                    cp("act" if half == 0 else "dve", kt_[:, g, j0 * 128:(j0 + n) * 128], pb[:, 0:n * 128], [b_pb], [b_kt])
            psS, b_pS = PS[2 + b % 2], B_PS[2 + b % 2]
            for j in range(7):
                for g in range(4):
                    mm(psS[:, j * 32 + g * 8:j * 32 + g * 8 + 8], kt_[:, g, j * 128:(j + 1) * 128],
                       qs5[:, g, b, :, :], True, True, [b_kt, b_qs], [b_pS])
            et, b_et = et_t[b % 2]
            act(et, psS[:, 0:224], AF.Exp, [b_pS], [b_et], scale=SCALE)
            tt("dve", et, et, smb, ALU.mult, [b_et, b_smb], [b_et])
            for g in range(4):
                c0 = g * 128 + b * 8
                for j in range(7):
                    mm(numC[:, c0:c0 + 8], vc[:, j, g * 128:(g + 1) * 128], et[:, j * 32 + g * 8:j * 32 + g * 8 + 8],
                       j == 0, j == 6, [b_vc, b_et], [b_numC])
                for j in range(7):
                    mm(denC[:, c0:c0 + 8], onesb_t[:, :], et[:, j * 32 + g * 8:j * 32 + g * 8 + 8],
                       j == 0, j == 6, [B_const, b_et], [b_denC])
        numS, b_numS = alloc([128, 512])
        denS, b_denS = alloc([128, 512])
        cp("act", numS, numC[:, :], [b_numC], [b_numS])
        cp("dve", denS, denC[:, :], [b_denC], [b_denS])
        en_t = [alloc([64, 128], BF16) for _ in range(2)]
        for g in range(4):
            psS, b_pS = PS[2 + g % 2], B_PS[2 + g % 2]
            mm(psS[0:64, 0:128], kn[:, g, :], qs_t[:, g * 128:(g + 1) * 128], True, True, [b_kn, b_qs], [b_pS])
            en, b_en = en_t[g % 2]
            act(en, psS[0:64, 0:128], AF.Exp, [b_pS], [b_en], scale=SCALE)
            tt("dve", en, en, nmb, ALU.mult, [b_en, b_nmb], [b_en])
            psA, b_pA = PS[4], B_PS[4]
            psB, b_pB = PS[5], B_PS[5]
            mm(psA[:, 0:128], vn[:, g * 128:(g + 1) * 128], en, True, True, [b_vn, b_en], [b_pA])
            mm(psB[:, 0:128], onesb_t[0:64, :], en, True, True, [B_const, b_en], [b_pB])
            tt("dve", numS[:, g * 128:(g + 1) * 128], numS[:, g * 128:(g + 1) * 128], psA[:, 0:128], ALU.add,
               [b_numS, b_pA], [b_numS])
            tt("dve", denS[:, g * 128:(g + 1) * 128], denS[:, g * 128:(g + 1) * 128], psB[:, 0:128], ALU.add,
               [b_denS, b_pB], [b_denS])
        K.add("dve", lambda e: e.reciprocal(denS, denS), [b_denS], [b_denS])
        tt("dve", numS, numS, denS, ALU.mult, [b_numS, b_denS], [b_numS])
        so, b_so = alloc([128, 8, 64], BF16)
        n5 = numS.rearrange("p (g b r t) -> p g b r t", g=4, b=NSB, r=2)
        for g in range(4):
            for r in range(2):
                cp("dve", so[:, 2 * g + r, :].rearrange("p (b t) -> p b t", t=4), n5[:, g, :, r, :], [b_numS], [b_so])
        dma("sp", mixT_d[0:8, :, SEQ:T].rearrange("h p t -> p h t"), so, G("st"), [b_so], [B_mixA])
        K.barrier()

    def phase_ssd(l):
        areset()
        alc, b_alc = alloc([128, 16])
        dsk, b_dsk = alloc([128, 16])
        Dbc, b_Dbc = alloc([128, 1024])
        gs, b_gs = alloc([128, 1024])
        dma("sp", alc, rowbc(a_log[l:l + 1, :], 16), g_misc, (), [b_alc])
        act(alc, alc, AF.Exp, [b_alc], [b_alc])
        ts("dve", alc, alc, -1.0, None, ALU.mult, None, [b_alc], [b_alc])
        dma("sp", dsk, rowbc(d_skip[l:l + 1, :], 16), g_misc, (), [b_dsk])
        cp("dve", Dbc.rearrange("p (h q) -> p h q", q=64), bcast_last(dsk, 64), [b_dsk], [b_Dbc])
        dma("sp", gs, rowbc(ssm_norm_g[l:l + 1, :], 1024), g_misc, (), [b_gs])
        hst, b_hst = alloc([128, 1024])
        hsb, b_hsb = alloc([128, 1024], BF16)
        mset("dve", hst, 0.0, [b_hst])
        xs2 = [alloc([128, 1024]) for _ in range(2)]
        zs2 = [alloc([128, 1024]) for _ in range(2)]
        dt2 = [alloc([128, 16]) for _ in range(2)]
        bt2 = [alloc([128, 512], BF16) for _ in range(2)]
        BT2 = [alloc([128, 4, 128], BF16) for _ in range(2)]
        CT2 = [alloc([128, 4, 128], BF16) for _ in range(2)]
        dtA, b_dtA = alloc([128, 16])
        acu, b_acu = alloc([128, 16])
        nacu, b_nacu = alloc([128, 16])
        tot, b_tot = alloc([128, 16])
        exA, b_exA = alloc([128, 16])
        toe, b_toe = alloc([128, 16])
        cde, b_cde = alloc([128, 16])
        acT, b_acT = alloc([16, 128])
        nacT, b_nacT = alloc([16, 128])
        cbT, b_cbT = alloc([128, 128])
        dec = [alloc([128, 4, 128]) for _ in range(2)]
        MT = [alloc([128, 4, 128], BF16) for _ in range(2)]
        xc, b_xc = alloc([128, 1024])
        xcb, b_xcb = alloc([128, 1024], BF16)
        xte, b_xte = alloc([128, 1024], BF16)
        y1, b_y1 = alloc([128, 1024])
        yb, b_yb = alloc([128, 1024], BF16)
        sq, b_sq = alloc([128, 1024])
        ss, b_ss = alloc([128, 4])
        yT, b_yT = alloc([128, 8, 128], BF16)
        h0T, b_h0T = alloc([128, NSB, 1024], BF16)
        h0n = [alloc([128, 8, 128]) for _ in range(2)]
        CTz, b_CTz = alloc([128, NSB * 64], BF16)
        Bz, b_Bz = alloc([64, NSB * 128], BF16)
        dtx, b_dtx = alloc([64, 1024])
        cdT, b_cdT = alloc([128, 8 * 16])
        nst = [alloc([128, 128]) for _ in range(2)]
        bmb, b_bmb = alloc([64, 16], BF16)
        cp("dve", bmb, C("bmask", 64), [B_const], [b_bmb])

        P_small, B_small = PS[0], B_PS[0]

        def chunk(cidx, nr, Ltri, Ones, negmb, sample):
            r0 = cidx * 128
            i2 = cidx % 2
            xs, b_xs = xs2[i2]
            zs, b_zs = zs2[i2]
            dtt, b_dtt = dt2[i2]
            btk, b_btk = bt2[i2]
            BTt, b_BTt = BT2[i2]
            CTt, b_CTt = CT2[i2]
            dma("sp", xs[0:nr, :], xs_d[r0:r0 + nr, :], G("in"), [B_xs], [b_xs])
            dma("sp", zs[0:nr, :], zs_d[r0:r0 + nr, :], G("in"), [B_zs], [b_zs])
            dma("sp", dtt[0:nr, :], dt_d[r0:r0 + nr, :], G("in"), [B_dt], [b_dtt])
            dma("sp", btk[0:nr, :], bt_d[r0:r0 + nr, :], G("in"), [B_bt], [b_btk])
            dma("sp", BTt[:, :, 0:nr], BT_d[:, :, r0:r0 + nr].rearrange("g p t -> p g t"), G("in"), [B_BT], [b_BTt])
            dma("sp", CTt[:, :, 0:nr], CT_d[:, :, r0:r0 + nr].rearrange("g p t -> p g t"), G("in"), [B_CT], [b_CTt])
            tt("dve", dtA[0:nr, :], dtt[0:nr, :], alc[0:nr, :], ALU.mult, [b_dtt, b_alc], [b_dtA])
            mm(P_small[0:nr, 0:16], Ltri[0:nr, 0:nr], dtA[0:nr, :], True, True, [B_const, b_dtA], [B_small])
            mm(P_small[0:nr, 16:32], Ones[0:nr, 0:nr], dtA[0:nr, :], True, True, [B_const, b_dtA], [B_small])
            mm(P_small[0:16, 32:32 + nr], dtA[0:nr, :], Ltri[0:nr, 0:nr], True, True, [B_const, b_dtA], [B_small])
            cp("act", acu[0:nr, :], P_small[0:nr, 0:16], [B_small], [b_acu])
            cp("act", tot[0:nr, :], P_small[0:nr, 16:32], [B_small], [b_tot])
            cp("act", acT[:, 0:nr], P_small[0:16, 32:32 + nr], [B_small], [b_acT])
            ts("dve", nacT[:, 0:nr], acT[:, 0:nr], -1.0, None, ALU.mult, None, [b_acT], [b_nacT])
            act(exA[0:nr, :], acu[0:nr, :], AF.Exp, [b_acu], [b_exA])
            tt("dve", toe[0:nr, :], tot[0:nr, :], acu[0:nr, :], ALU.subtract, [b_tot, b_acu], [b_toe])
            act(toe[0:nr, :], toe[0:nr, :], AF.Exp, [b_toe], [b_toe])
            act(cde[0:nr, :], tot[0:nr, :], AF.Exp, [b_tot], [b_cde])
            xs3 = xs[0:nr, :].rearrange("p (h q) -> p h q", q=64)
            tt("dve", xc[0:nr, :].rearrange("p (h q) -> p h q", q=64), xs3, bcast_last(dtt[0:nr, :], 64), ALU.mult,
               [b_xs, b_dtt], [b_xc])
            cp("act", xcb[0:nr, :], xc[0:nr, :], [b_xc], [b_xcb])
            tt("dve", xte[0:nr, :].rearrange("p (h q) -> p h q", q=64), xc[0:nr, :].rearrange("p (h q) -> p h q", q=64),
               bcast_last(toe[0:nr, :], 64), ALU.mult, [b_xc, b_toe], [b_xte])
            psY = [(PS[3], B_PS[3]), (PS[4], B_PS[4])]
            psO = [(PS[1], B_PS[1]), (PS[2], B_PS[2])]
            if not sample:
                cp("dve", hsb, hst, [b_hst], [b_hsb])
                for g in range(4):
                    po, b_po = psO[g // 2]
                    mm(po[0:nr, (g % 2) * 256:(g % 2) * 256 + 256], CTt[:, g, 0:nr], hsb[:, g * 256:(g + 1) * 256],
                       True, True, [b_CTt, b_hsb], [b_po])
            else:
                mset("dve", CTz, 0.0, [b_CTz])
                for g in range(4):
                    if g > 0:
                        mset("dve", CTz, 0.0, [b_CTz])
                    dstz = bass.AP(CTz.tensor, CTz.offset, [list(CTz.ap[0]), [68, NSB], [1, 4]])
                    cp("dve", dstz, CTt[:, g, 0:64].rearrange("p (b t) -> p b t", t=4), [b_CTt], [b_CTz])
                    po, b_po = psO[g // 2]
                    for b in range(NSB):
                        mm(po[0:64, (g % 2) * 256:(g % 2) * 256 + 256], CTz[:, b * 64:(b + 1) * 64],
                           h0T[:, b, g * 256:(g + 1) * 256], b == 0, b == NSB - 1, [b_CTz, b_h0T], [b_po])
            for g in range(4):
                mm(P_small[0:nr, 256:256 + nr], BTt[:, g, 0:nr], CTt[:, g, 0:nr], True, True, [b_BTt, b_CTt], [B_small])
                cp("act", cbT[0:nr, 0:nr], P_small[0:nr, 256:256 + nr], [B_small], [b_cbT])
                pd, b_pd = (PS[5], B_PS[5])
                selc = C("sel").rearrange("p (h m) -> p h m", m=128)
                for hh in range(4):
                    h = g * 4 + hh
                    o_ = pd[0:nr, hh * 128:hh * 128 + nr]
                    mm(o_, selc[0:16, h, 0:nr], acT[:, 0:nr], True, False, [B_const, b_acT], [b_pd])
                    mm(o_, nacT[:, 0:nr], selc[0:16, h, 0:nr], False, False, [B_const, b_nacT], [b_pd])
                    mm(o_, identb[0:nr, 0:nr], negmb[0:nr, 0:nr], False, True, [B_const], [b_pd])
                dc, b_dc = dec[g % 2]
                act(dc[0:nr, :, 0:nr], pd[0:nr, :].rearrange("p (a b) -> p a b", b=128)[:, :, 0:nr], AF.Exp, [b_pd], [b_dc])
                mt, b_mt = MT[g % 2]
                tt("dve", mt[0:nr, :, 0:nr], dc[0:nr, :, 0:nr], bcast_mid(cbT[0:nr, 0:nr], 4), ALU.mult,
                   [b_dc, b_cbT], [b_mt])
                py, b_py = psY[g // 2]
                for hh in range(4):
                    h = g * 4 + hh
                    cc = (h % 8) * 64
                    mm(py[0:nr, cc:cc + 64], mt[0:nr, hh, 0:nr], xcb[0:nr, h * 64:(h + 1) * 64], True, True,
                       [b_mt, b_xcb], [b_py])
            for half in range(2):
                po, b_po = psO[half]
                py, b_py = psY[half]
                cs_ = slice(half * 512, half * 512 + 512)
                tt("dve", y1[0:nr, cs_].rearrange("p (h q) -> p h q", q=64),
                   po[0:nr, :].rearrange("p (h q) -> p h q", q=64),
                   bcast_last(exA[0:nr, half * 8:half * 8 + 8], 64), ALU.mult, [b_po, b_exA], [b_y1])
                tt("dve", y1[0:nr, cs_], y1[0:nr, cs_], py[0:nr, :], ALU.add, [b_y1, b_py], [b_y1])
            tt("dve", sq[0:nr, :], xs[0:nr, :], Dbc[0:nr, :], ALU.mult, [b_xs, b_Dbc], [b_sq])
            tt("dve", y1[0:nr, :], y1[0:nr, :], sq[0:nr, :], ALU.add, [b_y1, b_sq], [b_y1])
            tt("dve", y1[0:nr, :], y1[0:nr, :], zs[0:nr, :], ALU.mult, [b_y1, b_zs], [b_y1])
            act(sq[0:nr, :], y1[0:nr, :], AF.Square, [b_y1], [b_sq])
            red("dve", ss[0:nr, 0:1], sq[0:nr, :], [b_sq], [b_ss])
            ts("dve", ss[0:nr, 0:1], ss[0:nr, 0:1], 1.0 / 1024, EPS, ALU.mult, ALU.add, [b_ss], [b_ss])
            rsqrt(ss[0:nr, 0:1], b_ss)
            stt("dve", yb[0:nr, :], y1[0:nr, :], ss[0:nr, 0:1], gs[0:nr, :], ALU.mult, ALU.mult,
                [b_y1, b_ss, b_gs], [b_yb])
            pb, b_pb = next_psb()
            for j in range(8):
                tr(pb[:, j * 128:j * 128 + nr], yb[0:nr, j * 128:(j + 1) * 128], identb[0:nr, 0:nr], [b_yb, B_const], [b_pb])
            cp("act", yT[:, :, 0:nr], pb[:, :].rearrange("p (a b) -> p a b", b=128)[:, :, 0:nr], [b_pb], [b_yT])
            dma("sp", mixT_d[8:16, :, r0:r0 + nr].rearrange("k p t -> p k t"), yT[:, :, 0:nr], G("st"), [b_yT], [B_mixS])
            if not sample:
                for g in range(4):
                    po, b_po = psO[g // 2]
                    mm(po[:, (g % 2) * 256:(g % 2) * 256 + 256], btk[0:nr, g * 128:(g + 1) * 128],
                       xte[0:nr, g * 256:(g + 1) * 256], True, True, [b_btk, b_xte], [b_po])
                tt("dve", hst.rearrange("p (h q) -> p h q", q=64), hst.rearrange("p (h q) -> p h q", q=64),
                   bcast_last(cde[:, :], 64), ALU.mult, [b_hst, b_cde], [b_hst])
                for half in range(2):
                    po, b_po = psO[half]
                    tt("dve", hst[:, half * 512:half * 512 + 512], hst[:, half * 512:half * 512 + 512], po[:, :],
                       ALU.add, [b_hst, b_po], [b_hst])
            return xte, b_xte, btk, b_btk, dtA, b_dtA

        for b in range(NSB):
            hn_, b_hn = h0n[b % 2]
            dma("sp", hn_, st_ssm[l, b].rearrange("(j p) n -> p j n", p=128), G("in"), (), [b_hn])
            for half in range(2):
                ps, b_ps = next_ps()
                for jj in range(4):
                    j = half * 4 + jj
                    tr(ps[:, jj * 128:(jj + 1) * 128], hn_[:, j, :], identf, [b_hn, B_const], [b_ps])
                cp("act" if half == 0 else "dve", h0T[:, b, half * 512:half * 512 + 512], ps[:, :], [b_ps], [b_h0T])
        xte_s, b_xte_s, btk_s, b_btk_s, dtA_s, b_dtA_s = chunk(32, 64, C("ltri_s"), C("ones_s"), negmsb_t, True)
        cp("dve", dtx.rearrange("p (h q) -> p h q", q=64), bcast_last(dtA_s[0:64, :], 64), [b_dtA_s], [b_dtx])
        pcd, b_pcd = next_ps()
        for j in range(8):
            mm(pcd[:, j * 16:(j + 1) * 16], dtx[:, j * 128:(j + 1) * 128], C("bmask", 64), True, True,
               [b_dtx, B_const], [b_pcd])
        act(cdT, pcd[:, 0:128], AF.Exp, [b_pcd], [b_cdT])
        for g in range(4):
            tt("dve", Bz.rearrange("p (b n) -> p b n", n=128), bcast_mid(btk_s[0:64, g * 128:(g + 1) * 128], NSB),
               bcast_last(bmb, 128), ALU.mult, [b_btk_s, b_bmb], [b_Bz])
            for b in range(NSB):
                hn_, b_hn = h0n[b % 2]
                if g == 0 or True:
                    dma("sp", hn_[:, 2 * g:2 * g + 2, :],
                        st_ssm[l, b, g * 256:(g + 1) * 256, :].rearrange("(j p) n -> p j n", p=128), G("in"), (), [b_hn])
                for jj in range(2):
                    j = 2 * g + jj
                    ps, b_ps = next_ps()
                    mm(ps[:, 0:128], xte_s[0:64, j * 128:(j + 1) * 128], Bz[:, b * 128:(b + 1) * 128], True, True,
                       [b_xte_s, b_Bz], [b_ps])
                    ns, b_ns = nst[(b * 2 + jj) % 2]
                    stt("dve", ns, hn_[:, j, :], cdT[:, j * 16 + b:j * 16 + b + 1], ps[:, 0:128], ALU.mult, ALU.add,
                        [b_hn, b_cdT, b_ps], [b_ns])
                    dma("sp", nssm_s[l, b, j * 128:(j + 1) * 128, :], ns, G("st"), [b_ns], ())
        for c in range(32):
            chunk(c, 128, C("ltri"), C("ones"), negmb_t, False)
        for half in range(2):
            ps, b_ps = next_ps()
            for jj in range(4):
                j = half * 4 + jj
                tr(ps[:, jj * 128:(jj + 1) * 128], hst[:, j * 128:(j + 1) * 128], identf, [b_hst, B_const], [b_ps])
            fo, b_fo = nst[half]
            fo4, b_fo4 = xs2[half]
            cp("act", fo4[:, 0:512], ps[:, :], [b_ps], [b_fo4])
            dma("sp", nssm_p[l, half * 512:half * 512 + 512, :].rearrange("(j p) n -> p j n", p=128),
                fo4[:, 0:512].rearrange("p (j n) -> p j n", n=128), G("st"), [b_fo4], ())
        K.barrier()

    def dense_tok(l, wsrc, nk, srcT, b_srcT, gamma_next_row, dstT_next, post):
        pass

    def phase_outproj(l):
        areset()
        w, b_w = alloc([128, 16, 2048], BF16)
        load_w(w, b_w, w_out[l], 0, 16, 0, 2048)
        gam, b_gam = load_gamma(norm_ffn_g[l:l + 1, :])
        tmps = [norm_tmp() for _ in range(2)]
        hb = [alloc([128, 2048]) for _ in range(2)]
        mx = [alloc([128, 16, 128], BF16) for _ in range(2)]
        for blk in range(NBLK):
            nr = nr_of(blk)
            r0 = blk * 128
            h, b_h = hb[blk % 2]
            m, b_m = mx[blk % 2]
            dma("sp", h[0:nr, :], h_d[r0:r0 + nr, :], G("in"), [B_h[blk]], [b_h])
            dma("sp", m[:, :, 0:nr], mixT_d[:, :, r0:r0 + nr].rearrange("k p t -> p k t"), G("in"),
                [B_mixA, B_mixS], [b_m])
            for cg in range(4):
                ps, b_ps = next_ps()
                for kc in range(16):
                    mm(ps[0:nr, :], m[:, kc, 0:nr], w[:, kc, cg * 512:(cg + 1) * 512], kc == 0, kc == 15,
                       [b_m, b_w], [b_ps])
                tt("dve", h[0:nr, cg * 512:(cg + 1) * 512], h[0:nr, cg * 512:(cg + 1) * 512], ps[0:nr, :], ALU.add,
                   [b_h, b_ps], [b_h])
            dma("sp", h_d[r0:r0 + nr, :], h[0:nr, :], G("st"), [b_h], [B_h[blk]])
            norm_block(h, b_h, gam, b_gam, hnT_d, B_hnT, blk, tmps[blk % 2])
        K.barrier()

    def phase_ffn(l):
        NG = 4
        GC = 11
        for fg in range(NG):
            areset()
            wg_, b_wg = alloc([128, 16, GC * 128], BF16)
            wu_, b_wu = alloc([128, 16, GC * 128], BF16)
            wd_, b_wd = alloc([128, GC, 2048], BF16)
            c0 = fg * GC * 128
            load_w(wg_, b_wg, w_g[l], 0, 16, c0, GC * 128)
            load_w(wu_, b_wu, w_u[l], 0, 16, c0, GC * 128)
            load_w(wd_, b_wd, w_d[l], fg * GC, GC, 0, 2048)
            last = False
            hx = [alloc([128, 16, 512], BF16) for _ in range(1)]
            at = [alloc([128, GC, 512], BF16) for _ in range(1)]
            sg = [alloc([128, 512]) for _ in range(2)]
            hb = [alloc([128, 2048]) for _ in range(2)]
            for tti in range(9):
                tw = 512 if tti < 8 else 64
                t0 = tti * 512
                hxt, b_hx = hx[0]
                dma("sp", hxt[:, :, 0:tw], hnT_d[:, :, t0:t0 + tw].rearrange("k p t -> p k t"), G("in"),
                    [B_hnT], [b_hx])
                a_, b_a = at[0]
                for j in range(GC):
                    pg, b_pg = next_ps()
                    pu, b_pu = next_ps()
                    for kc in range(16):
                        mm(pg[:, 0:tw], wg_[:, kc, j * 128:(j + 1) * 128], hxt[:, kc, 0:tw], kc == 0, kc == 15,
                           [b_wg, b_hx], [b_pg])
                    for kc in range(16):
                        mm(pu[:, 0:tw], wu_[:, kc, j * 128:(j + 1) * 128], hxt[:, kc, 0:tw], kc == 0, kc == 15,
                           [b_wu, b_hx], [b_pu])
                    s_, b_s = sg[j % 2]
                    act(s_[:, 0:tw], pg[:, 0:tw], AF.Silu, [b_pg], [b_s])
                    tt("dve", a_[:, j, 0:tw], s_[:, 0:tw], pu[:, 0:tw], ALU.mult, [b_s, b_pu], [b_a])
                nsb = 4 if tti < 8 else 1
                for sb in range(nsb):
                    blk = tti * 4 + sb if tti < 8 else 32
                    nr = nr_of(blk)
                    r0 = blk * 128
                    h, b_h = hb[blk % 2]
                    dma("sp", h[0:nr, :], h_d[r0:r0 + nr, :], G("in"), [B_h[blk]], [b_h])
                    for cg in range(4):
                        ps, b_ps = next_ps()
                        for j in range(GC):
                            mm(ps[0:nr, :], a_[:, j, sb * 128:sb * 128 + nr], wd_[:, j, cg * 512:(cg + 1) * 512],
                               j == 0, j == GC - 1, [b_a, b_wd], [b_ps])
                        tt("dve", h[0:nr, cg * 512:(cg + 1) * 512], h[0:nr, cg * 512:(cg + 1) * 512], ps[0:nr, :],
                           ALU.add, [b_h, b_ps], [b_h])
                    dma("sp", h_d[r0:r0 + nr, :], h[0:nr, :], G("st"), [b_h], [B_h[blk]])
                    if last:
                        norm_block(h, b_h, gam, b_gam, mixT_d, B_mixA, blk, tmps[0])
            K.barrier()

    def phase_ple(l):
        areset()
        w, b_w = alloc([128, 16, 2048], BF16)
        wp, b_wp = alloc([128, 2, 2048], BF16)
        load_w(w, b_w, w_pg[l], 0, 16, 0, 2048)
        load_w(wp, b_wp, w_pp[l], 0, 2, 0, 2048)
        lastl = l == DEPTH - 1
        gam, b_gam = load_gamma(final_g[0:1, :] if lastl else norm_mix_g[l + 1:l + 2, :])
        tmps = [norm_tmp() for _ in range(1)]
        hb = [alloc([128, 2048]) for _ in range(2)]
        mx = [alloc([128, 16, 128], BF16) for _ in range(2)]
        pin = [alloc([128, 256]) for _ in range(2)]
        pbf = [alloc([128, 256], BF16) for _ in range(2)]
        pT = [alloc([128, 2, 128], BF16) for _ in range(2)]
        sgm = [alloc([128, 512]) for _ in range(2)]
        yo = [alloc([128, 2048]) for _ in range(2)]
        for blk in range(NBLK):
            nr = nr_of(blk)
            r0 = blk * 128
            h, b_h = hb[blk % 2]
            m, b_m = mx[blk % 2]
            dma("sp", h[0:nr, :], h_d[r0:r0 + nr, :], G("in"), [B_h[blk]], [b_h])
            dma("sp", m[:, :, 0:nr], mixT_d[:, :, r0:r0 + nr].rearrange("k p t -> p k t"), G("in"), [B_mixA], [b_m])
            pi, b_pi = pin[blk % 2]
            psrc = p_p[l, r0:r0 + nr, :] if blk < 32 else p_s[l, :, :]
            dma("sp", pi[0:nr, :], psrc, G("in"), (), [b_pi])
            pb_, b_pb_ = pbf[blk % 2]
            cp("dve", pb_[0:nr, :], pi[0:nr, :], [b_pi], [b_pb_])
            pbk, b_pbk = next_psb()
            for j in range(2):
                tr(pbk[:, j * 128:j * 128 + nr], pb_[0:nr, j * 128:(j + 1) * 128], identb[0:nr, 0:nr],
                   [b_pb_, B_const], [b_pbk])
            pt_, b_pt = pT[blk % 2]
            cp("act", pt_[:, :, 0:nr], pbk[:, 0:256].rearrange("p (a b) -> p a b", b=128)[:, :, 0:nr], [b_pbk], [b_pt])
            for cg in range(4):
                pg, b_pg = next_ps()
                pp_, b_pp = next_ps()
                for kc in range(16):
                    mm(pg[0:nr, :], m[:, kc, 0:nr], w[:, kc, cg * 512:(cg + 1) * 512], kc == 0, kc == 15,
                       [b_m, b_w], [b_pg])
                for kc in range(2):
                    mm(pp_[0:nr, :], pt_[:, kc, 0:nr], wp[:, kc, cg * 512:(cg + 1) * 512], kc == 0, kc == 1,
                       [b_pt, b_wp], [b_pp])
                s_, b_s = sgm[cg % 2]
                act(s_[0:nr, :], pg[0:nr, :], AF.Sigmoid, [b_pg], [b_s])
                tt("dve", s_[0:nr, :], s_[0:nr, :], pp_[0:nr, :], ALU.mult, [b_s, b_pp], [b_s])
                tt("dve", h[0:nr, cg * 512:(cg + 1) * 512], h[0:nr, cg * 512:(cg + 1) * 512], s_[0:nr, :], ALU.add,
                   [b_h, b_s], [b_h])
            if not lastl:
                dma("sp", h_d[r0:r0 + nr, :], h[0:nr, :], G("st"), [b_h], [B_h[blk]])
                norm_block(h, b_h, gam, b_gam, hnT_d, B_hnT, blk, tmps[0])
            else:
                sq, b_sq = tmps[0]["sq"]
                ss, b_ss = tmps[0]["ss"]
                y_, b_y = yo[blk % 2]
                act(sq[0:nr, :], h[0:nr, :], AF.Square, [b_h], [b_sq])
                red("dve", ss[0:nr, 0:1], sq[0:nr, :], [b_sq], [b_ss])
                ts("dve", ss[0:nr, 0:1], ss[0:nr, 0:1], 1.0 / D, EPS, ALU.mult, ALU.add, [b_ss], [b_ss])
                rsqrt(ss[0:nr, 0:1], b_ss)
                stt("dve", y_[0:nr, :], h[0:nr, :], ss[0:nr, 0:1], gam[0:nr, :], ALU.mult, ALU.mult,
                    [b_h, b_ss, b_gam], [b_y])
                if blk < 32:
                    dma("sp", y_p[r0:r0 + 128, :], y_[:, :], G("st"), [b_y], ())
                else:
                    dma("sp", y_s[:, :], y_[0:64, :], G("st"), [b_y], ())
        K.barrier()

    import os as _os
    KSTOP = int(_os.environ.get("KSTOP", "99"))
    phases = [lambda l: phase_inproj(l), lambda l: phase_attention(l), lambda l: phase_ssd(l),
              lambda l: phase_outproj(l), lambda l: phase_ffn(l),
              lambda l: phase_norm_only(norm_ple_g[l:l + 1, :], mixT_d, B_mixA), lambda l: phase_ple(l)]
    if KSTOP >= 1:
        phase_norm_only(norm_mix_g[0:1, :])
    cnt = 1
    for l in range(DEPTH):
        K.epoch = l + 1
        for ph in phases:
            cnt += 1
            if KSTOP >= cnt:
                ph(l)
    K.final_wait()
    K.emit()
    es.close()
    return nc, carr


_CACHE = {}


def kernel(**inp):
    if "prog" not in _CACHE:
        _CACHE["prog"] = build_program()
    nc, carr = _CACHE["prog"]
    f = lambda a: np.ascontiguousarray(np.asarray(a, dtype=np.float32))
    x_prompt = f(inp["x_prompt"])
    x_sample = f(inp["x_sample"])
    ck = np.asarray(inp["cache_k"], dtype=np.float32).reshape(DEPTH, 128, L, 512)
    cv = np.asarray(inp["cache_v"], dtype=np.float32).reshape(DEPTH, 128, L, 512)
    ssm = np.asarray(inp["state_ssm"], dtype=np.float32).reshape(DEPTH, 128, 1024, 128)
    conv = np.asarray(inp["state_conv"], dtype=np.float32)
    p_prompt = f(inp["p_prompt"])
    p_sample = f(inp["p_sample"])
    shared = {}
    for k_ in ("norm_mix_g", "w_in", "conv_w", "conv_b", "dt_bias", "a_log", "d_skip", "ssm_norm_g", "w_out",
               "norm_ffn_g", "w_ffn_gate", "w_ffn_up", "w_ffn_down", "norm_ple_g", "w_ple_gate", "w_ple_proj"):
        shared[k_] = f(inp[k_])
    shared["final_norm_g"] = f(inp["final_norm_g"]).reshape(1, D)
    shared["consts"] = carr
    in_maps = []
    for c in range(NCORES):
        s = c % 2
        b0 = c * NSB
        m = dict(shared)
        m["x_p"] = x_prompt[s]
        m["x_s"] = np.ascontiguousarray(x_sample[b0:b0 + NSB].reshape(TS, D))
        m["ck"] = np.ascontiguousarray(ck[:, b0:b0 + NSB])
        m["cv"] = np.ascontiguousarray(cv[:, b0:b0 + NSB])
        m["st_ssm"] = np.ascontiguousarray(ssm[:, b0:b0 + NSB])
        m["st_conv"] = np.ascontiguousarray(conv[:, b0:b0 + NSB].reshape(DEPTH, NSB * 3, 2048))
        m["p_p"] = np.ascontiguousarray(p_prompt[:, s])
        m["p_s"] = np.ascontiguousarray(p_sample[:, b0:b0 + NSB].reshape(DEPTH, TS, PLE))
        in_maps.append(m)
    res = run_bass_kernel_spmd(nc, in_maps, core_ids=list(range(NCORES)))
    R = res.results
    y_prompt = np.stack([R[0]["y_p"], R[1]["y_p"]]).reshape(2, SEQ, D)
    y_sample = np.concatenate([R[c]["y_s"].reshape(NSB, 4, D) for c in range(NCORES)], axis=0)
    nk_p = np.stack([R[0]["nk_p"], R[1]["nk_p"]], axis=1).reshape(DEPTH, 2, 2048, 4, 128)
    nv_p = np.stack([R[0]["nv_p"], R[1]["nv_p"]], axis=1).reshape(DEPTH, 2, 2048, 4, 128)
    nssm_p = np.stack([R[0]["nssm_p"], R[1]["nssm_p"]], axis=1).reshape(DEPTH, 2, 16, 64, 128)
    nconv_p = np.stack([R[0]["nconv_p"], R[1]["nconv_p"]], axis=1).reshape(DEPTH, 2, 3, 2048)
    nk_s = np.concatenate([R[c]["nk_s"] for c in range(NCORES)], axis=1).reshape(DEPTH, 128, L, 4, 128)
    nv_s = np.concatenate([R[c]["nv_s"] for c in range(NCORES)], axis=1).reshape(DEPTH, 128, L, 4, 128)
    nssm_s = np.concatenate([R[c]["nssm_s"] for c in range(NCORES)], axis=1).reshape(DEPTH, 128, 16, 64, 128)
    nconv_s = np.concatenate([R[c]["nconv_s"].reshape(DEPTH, NSB, 3, 2048) for c in range(NCORES)], axis=1)
    outs = (y_prompt, y_sample, nk_p, nv_p, nssm_p, nconv_p, nk_s, nv_s, nssm_s, nconv_s)
    return tuple(np.ascontiguousarray(o, dtype=np.float32) for o in outs)
```
